# Optimizing a Trainium2 kernel written in Bass

```python
import jax, jax.numpy as jnp
from jax import lax
import numpy as np

D_MODEL = 1024
BATCH = 4
SEQ = 8192
DEPTH = 2
DEC_BATCH = 16
DEC_SEQ = 16
PAST_LEN = 1024

CHUNK = 64
D_FF = 2816
CONV_W = 4
LRU_W = D_MODEL
LRU_BLOCKS = 8
LRU_BD = LRU_W // LRU_BLOCKS
LRU_C = 8.0
M_HEADS = 4
M_W = D_MODEL
M_HD = M_W // M_HEADS
R_HD = 64
R_W = D_MODEL
R_HEADS = R_W // R_HD
R_LORA_W = 64
R_LORA_A = 64
R_LORA_G = 128
N_BRANCH = 3
BR_W = D_MODEL
CONV_CH = LRU_W + 2 * M_W
R_IN = 3 * R_W + R_LORA_W + R_LORA_A + R_LORA_G
IN_SPLITS = [CONV_CH, CONV_CH + M_W, CONV_CH + 2 * M_W, CONV_CH + 2 * M_W + 2 * M_HEADS,
             CONV_CH + 2 * M_W + 2 * M_HEADS + R_IN]
IN_W = CONV_CH + 2 * M_W + 2 * M_HEADS + R_IN + N_BRANCH * D_MODEL
R_SPLITS = [R_W, 2 * R_W, 3 * R_W, 3 * R_W + R_LORA_W, 3 * R_W + R_LORA_W + R_LORA_A]
RMS_EPS = 1e-6
MH_EPS = 1e-6
RWKV_GN_EPS = 64e-5
N_STATE = 7

kernel_name = "hybrid_lru_mlstm_rwkv7_stream_step"


def rms_norm(x, g):
    xf = x.astype(jnp.float32)
    y = xf * lax.rsqrt(jnp.mean(xf * xf, axis=-1, keepdims=True) + RMS_EPS)
    return (y * g.astype(jnp.float32)).astype(x.dtype)


def head_layer_norm(h, eps):
    d = h - jnp.mean(h, axis=-1, keepdims=True)
    return d * lax.rsqrt(jnp.mean(d * d, axis=-1, keepdims=True) + eps)


def swiglu(x, w_gate, w_up, w_down):
    return (jax.nn.silu(x @ w_gate) * (x @ w_up)) @ w_down


def causal_conv(x, prev, w, b):
    t = x.shape[1]
    xp = jnp.concatenate([prev.astype(x.dtype), x], axis=1)
    y = b + xp[:, 0:t] * w[0]
    for j in range(1, CONV_W):
        y = y + xp[:, j:j + t] * w[j]
    return y, xp[:, t:]


def rg_lru(x, h0, wa, ba, wx, bx, lam):
    bsz, t, _ = x.shape
    xb = x.reshape(bsz, t, LRU_BLOCKS, LRU_BD)
    r = jax.nn.sigmoid(jnp.einsum("btni,nij->btnj", xb, wa).reshape(bsz, t, LRU_W) + ba)
    i = jax.nn.sigmoid(jnp.einsum("btni,nij->btnj", xb, wx).reshape(bsz, t, LRU_W) + bx)
    log_a = -LRU_C * r * jax.nn.softplus(-lam)
    a = jnp.exp(log_a)
    b = jnp.sqrt(-jnp.expm1(2.0 * log_a)) * (i * x)
    b = b.at[:, 0].add(a[:, 0] * h0)

    def combine(l, rr):
        return (l[0] * rr[0], rr[0] * l[1] + rr[1])

    _, h = lax.associative_scan(combine, (a, b), axis=1)
    return h, h[:, -1]


def mlstm_block(carry, inp):
    c_prev, n_prev, m_prev = carry
    q, k, v, ig, lf = inp
    lb = q.shape[1]
    bcum = jnp.cumsum(lf, axis=1)
    dmat = bcum[:, :, None, :] - bcum[:, None, :, :] + ig[:, None, :, :]
    causal = jnp.tril(jnp.ones((lb, lb), bool))
    dmat = jnp.where(causal[None, :, :, None], dmat, -jnp.inf)
    inter = bcum + m_prev[:, None, :]
    m_t = jnp.maximum(inter, jnp.max(dmat, axis=2))
    w_intra = jnp.exp(dmat - m_t[:, :, None, :])
    w_inter = jnp.exp(inter - m_t)
    s = jnp.einsum("bthd,bshd->btsh", q, k) * w_intra
    num = jnp.einsum("btsh,bshv->bthv", s, v) + w_inter[..., None] * jnp.einsum("bhvd,bthd->bthv", c_prev, q)
    den = jnp.sum(s, axis=2) + w_inter * jnp.einsum("bhd,bthd->bth", n_prev, q)
    h = num / jnp.maximum(jnp.abs(den), jnp.exp(-m_t))[..., None]
    m_new = m_t[:, -1]
    g_state = jnp.exp(bcum[:, -1] + m_prev - m_new)
    g_src = jnp.exp(bcum[:, -1:, :] - bcum + ig - m_new[:, None, :])
    c_new = g_state[..., None, None] * c_prev + jnp.einsum("bsh,bshv,bshd->bhvd", g_src, v, k)
    n_new = g_state[..., None] * n_prev + jnp.einsum("bsh,bshd->bhd", g_src, k)
    return (c_new, n_new, m_new), h


def mlstm(q, k, v, ig, lf, c0, n0, m0):
    bsz, t, nh, hd = q.shape
    lb = min(CHUNK, t)
    nc = t // lb

    def to_blocks(a):
        return jnp.moveaxis(a.reshape((bsz, nc, lb) + a.shape[2:]), 1, 0)

    (c, n, m), h = lax.scan(mlstm_block, (c0, n0, m0),
                            (to_blocks(q), to_blocks(k), to_blocks(v), to_blocks(ig), to_blocks(lf)))
    h = jnp.moveaxis(h, 0, 1).reshape(bsz, t, nh, hd)
    return h, c, n, m


def rwkv7_scan(r, w, k, v, a_vec, b_vec, s0):
    def step(s, inp):
        r_t, w_t, k_t, v_t, a_t, b_t = inp
        sa = jnp.einsum("bhij,bhj->bhi", s, a_t)
        s = s * w_t[:, :, None, :] + sa[..., None] * b_t[:, :, None, :] + v_t[..., None] * k_t[:, :, None, :]
        return s, jnp.einsum("bhij,bhj->bhi", s, r_t)

    seq = tuple(jnp.moveaxis(a, 1, 0) for a in (r, w, k, v, a_vec, b_vec))
    s, y = lax.scan(step, s0, seq)
    return jnp.moveaxis(y, 0, 1), s


def rwkv7(xs, s0, lp):
    bsz, t, _ = xs.shape
    r, k, v, wd, ad, gd = jnp.split(xs, R_SPLITS, axis=-1)
    w_log = -jax.nn.softplus(-(lp["rwkv_w0"] + jnp.tanh(wd) @ lp["rwkv_w2"])) - 0.5
    decay = jnp.exp(-jnp.exp(w_log))
    a = jax.nn.sigmoid(lp["rwkv_a0"] + ad @ lp["rwkv_a2"])
    g = jax.nn.sigmoid(gd) @ lp["rwkv_g2"]

    def heads(y):
        return y.reshape(bsz, t, R_HEADS, R_HD)

    kk = heads(k * lp["rwkv_k_k"])
    kk = kk * lax.rsqrt(jnp.maximum(jnp.sum(kk * kk, axis=-1, keepdims=True), 1e-24))
    k = k * (1.0 + (a - 1.0) * lp["rwkv_k_a"])
    r_h, k_h, v_h, a_h = heads(r), heads(k), heads(v), heads(a)
    y, s = rwkv7_scan(r_h, heads(decay), k_h, v_h, -kk, kk * a_h, s0)
    y = (head_layer_norm(y, RWKV_GN_EPS) * lp["rwkv_ln_w"].reshape(R_HEADS, R_HD)
         + lp["rwkv_ln_b"].reshape(R_HEADS, R_HD))
    y = y + jnp.sum(r_h * k_h * lp["rwkv_r_k"], axis=-1, keepdims=True) * v_h
    return y.reshape(bsz, t, R_W) * g, s


def token_mix(u, st, lp):
    conv_prev, lru_h, m_c, m_n, m_m, shift_prev, rwkv_s = st
    f32 = jnp.float32
    bsz, t, _ = u.shape
    z = u @ lp["w_in"]
    z_conv, z_mv, z_mo, z_if, z_rw, z_gate = jnp.split(z, IN_SPLITS, axis=-1)

    c, conv_new = causal_conv(z_conv, conv_prev, lp["conv_w"], lp["conv_b"])
    c = c.astype(f32)

    h_lru, lru_new = rg_lru(c[..., :LRU_W], lru_h.astype(f32), lp["lru_wa"], lp["lru_ba"],
                            lp["lru_wx"], lp["lru_bx"], lp["lru_lambda"])

    qk = jax.nn.silu(c[..., LRU_W:])
    q = qk[..., :M_W].reshape(bsz, t, M_HEADS, M_HD)
    k = qk[..., M_W:].reshape(bsz, t, M_HEADS, M_HD) * (M_HD ** -0.5)
    v = z_mv.astype(f32).reshape(bsz, t, M_HEADS, M_HD)
    gif = z_if.astype(f32) + lp["mlstm_if_bias"]
    ig = gif[..., :M_HEADS]
    lf = jax.nn.log_sigmoid(gif[..., M_HEADS:])
    h_m, c_new, n_new, m_new = mlstm(q, k, v, ig, lf, m_c.astype(f32), m_n.astype(f32), m_m.astype(f32))
    h_m = head_layer_norm(h_m, MH_EPS).reshape(bsz, t, M_W) * lp["mlstm_norm"]
    o_m = jax.nn.sigmoid(z_mo.astype(f32)) * h_m

    zr = z_rw.astype(f32)
    zp = jnp.concatenate([shift_prev.astype(f32), zr], axis=1)
    xs = zr + (zp[:, :-1] - zr) * lp["rwkv_mu"]
    shift_new = zr[:, -1:]
    o_r, s_new = rwkv7(xs, rwkv_s.astype(f32), lp)

    outs = jnp.stack([h_lru, o_m, o_r], axis=2)
    proj = jnp.einsum("btgc,gcd->btgd", outs, lp["w_branch"])
    gates = jax.nn.sigmoid(z_gate.astype(f32).reshape(bsz, t, N_BRANCH, D_MODEL))
    y = jnp.sum(gates * proj, axis=2) @ lp["w_out"]
    return y.astype(u.dtype), (conv_new, lru_new, c_new, n_new, m_new, shift_new, s_new)


def layer(x, st, lp):
    x = x + 0.5 * swiglu(rms_norm(x, lp["ffn1_norm"]), lp["ffn1_w_gate"], lp["ffn1_w_up"], lp["ffn1_w_down"])
    mix, new_st = token_mix(rms_norm(x, lp["mix_norm"]), st, lp)
    x = x + mix
    x = x + 0.5 * swiglu(rms_norm(x, lp["ffn2_norm"]), lp["ffn2_w_gate"], lp["ffn2_w_up"], lp["ffn2_w_down"])
    return x, new_st


def run_trunk(x, states, params, final_norm):
    new_states = []
    for l in range(DEPTH):
        lp = {name: w[l] for name, w in params.items()}
        x, st = layer(x, states[l], lp)
        new_states.append(st)
    stacked = tuple(jnp.stack([st[i] for st in new_states]) for i in range(N_STATE))
    return rms_norm(x, final_norm), stacked


def zero_states(bsz):
    f32 = jnp.float32
    return (jnp.zeros((bsz, CONV_W - 1, CONV_CH), f32), jnp.zeros((bsz, LRU_W), f32),
            jnp.zeros((bsz, M_HEADS, M_HD, M_HD), f32), jnp.zeros((bsz, M_HEADS, M_HD), f32),
            jnp.zeros((bsz, M_HEADS), f32), jnp.zeros((bsz, 1, R_IN), f32),
            jnp.zeros((bsz, R_HEADS, R_HD, R_HD), f32))


def setup_inputs(seed: int = 0) -> dict:
    key = jax.random.key(seed)
    keys = iter(jax.random.split(key, 64))

    def nrm(shape, scale):
        return scale * jax.random.normal(next(keys), shape, jnp.float32)

    def unif(shape, lo, hi):
        return jax.random.uniform(next(keys), shape, jnp.float32, lo, hi)

    L = DEPTH
    lam_a = unif((L, LRU_W), 0.9, 0.999) ** (1.0 / LRU_C)
    if_bias = jnp.concatenate([nrm((L, M_HEADS), 0.1),
                               jnp.broadcast_to(jnp.linspace(3.0, 6.0, M_HEADS), (L, M_HEADS))
                               + nrm((L, M_HEADS), 0.01)], axis=-1)
    return {
        "x_prompt": nrm((BATCH, SEQ, D_MODEL), 1.0),
        "x_sample": nrm((DEC_BATCH, DEC_SEQ, D_MODEL), 1.0),
        "state_conv": nrm((L, DEC_BATCH, CONV_W - 1, CONV_CH), 1.0),
        "state_lru_h": nrm((L, DEC_BATCH, LRU_W), 0.5),
        "state_mlstm_C": nrm((L, DEC_BATCH, M_HEADS, M_HD, M_HD), 0.05),
        "state_mlstm_n": nrm((L, DEC_BATCH, M_HEADS, M_HD), 0.05),
        "state_mlstm_m": nrm((L, DEC_BATCH, M_HEADS), 0.5),
        "state_rwkv_shift": nrm((L, DEC_BATCH, 1, R_IN), 1.0),
        "state_rwkv_S": nrm((L, DEC_BATCH, R_HEADS, R_HD, R_HD), 0.1),
        "ffn1_norm": 1.0 + nrm((L, D_MODEL), 0.02),
        "ffn1_w_gate": nrm((L, D_MODEL, D_FF), D_MODEL ** -0.5),
        "ffn1_w_up": nrm((L, D_MODEL, D_FF), D_MODEL ** -0.5),
        "ffn1_w_down": nrm((L, D_FF, D_MODEL), D_FF ** -0.5),
        "mix_norm": 1.0 + nrm((L, D_MODEL), 0.02),
        "w_in": nrm((L, D_MODEL, IN_W), D_MODEL ** -0.5),
        "conv_w": nrm((L, CONV_W, CONV_CH), 0.5),
        "conv_b": nrm((L, CONV_CH), 0.01),
        "lru_wa": nrm((L, LRU_BLOCKS, LRU_BD, LRU_BD), LRU_BD ** -0.5),
        "lru_ba": nrm((L, LRU_W), 0.01),
        "lru_wx": nrm((L, LRU_BLOCKS, LRU_BD, LRU_BD), LRU_BD ** -0.5),
        "lru_bx": nrm((L, LRU_W), 0.01),
        "lru_lambda": jnp.log(lam_a) - jnp.log1p(-lam_a),
        "mlstm_if_bias": if_bias,
        "mlstm_norm": 1.0 + nrm((L, M_W), 0.02),
        "rwkv_mu": unif((L, R_IN), 0.0, 1.0),
        "rwkv_w0": unif((L, R_W), -6.0, 1.0),
        "rwkv_w2": nrm((L, R_LORA_W, R_W), 0.1),
        "rwkv_a0": nrm((L, R_W), 0.1),
        "rwkv_a2": nrm((L, R_LORA_A, R_W), 0.5 * R_LORA_A ** -0.5),
        "rwkv_g2": nrm((L, R_LORA_G, R_W), R_LORA_G ** -0.5),
        "rwkv_k_k": 0.85 + nrm((L, R_W), 0.02),
        "rwkv_k_a": 1.0 + nrm((L, R_W), 0.02),
        "rwkv_r_k": nrm((L, R_HEADS, R_HD), 0.1),
        "rwkv_ln_w": 1.0 + nrm((L, R_W), 0.02),
        "rwkv_ln_b": nrm((L, R_W), 0.01),
        "w_branch": nrm((L, N_BRANCH, BR_W, D_MODEL), BR_W ** -0.5),
        "w_out": nrm((L, D_MODEL, D_MODEL), D_MODEL ** -0.5),
        "ffn2_norm": 1.0 + nrm((L, D_MODEL), 0.02),
        "ffn2_w_gate": nrm((L, D_MODEL, D_FF), D_MODEL ** -0.5),
        "ffn2_w_up": nrm((L, D_MODEL, D_FF), D_MODEL ** -0.5),
        "ffn2_w_down": nrm((L, D_FF, D_MODEL), D_FF ** -0.5),
        "final_norm": 1.0 + nrm((D_MODEL,), 0.02),
    }


def reference(x_prompt, x_sample, state_conv, state_lru_h, state_mlstm_C, state_mlstm_n,
              state_mlstm_m, state_rwkv_shift, state_rwkv_S,
              ffn1_norm, ffn1_w_gate, ffn1_w_up, ffn1_w_down, mix_norm, w_in, conv_w, conv_b,
              lru_wa, lru_ba, lru_wx, lru_bx, lru_lambda, mlstm_if_bias, mlstm_norm,
              rwkv_mu, rwkv_w0, rwkv_w2, rwkv_a0, rwkv_a2, rwkv_g2, rwkv_k_k, rwkv_k_a, rwkv_r_k,
              rwkv_ln_w, rwkv_ln_b, w_branch, w_out,
              ffn2_norm, ffn2_w_gate, ffn2_w_up, ffn2_w_down, final_norm):
    params = dict(ffn1_norm=ffn1_norm, ffn1_w_gate=ffn1_w_gate, ffn1_w_up=ffn1_w_up, ffn1_w_down=ffn1_w_down,
                  mix_norm=mix_norm, w_in=w_in, conv_w=conv_w, conv_b=conv_b,
                  lru_wa=lru_wa, lru_ba=lru_ba, lru_wx=lru_wx, lru_bx=lru_bx, lru_lambda=lru_lambda,
                  mlstm_if_bias=mlstm_if_bias, mlstm_norm=mlstm_norm,
                  rwkv_mu=rwkv_mu, rwkv_w0=rwkv_w0, rwkv_w2=rwkv_w2, rwkv_a0=rwkv_a0, rwkv_a2=rwkv_a2,
                  rwkv_g2=rwkv_g2, rwkv_k_k=rwkv_k_k, rwkv_k_a=rwkv_k_a, rwkv_r_k=rwkv_r_k,
                  rwkv_ln_w=rwkv_ln_w, rwkv_ln_b=rwkv_ln_b, w_branch=w_branch, w_out=w_out,
                  ffn2_norm=ffn2_norm, ffn2_w_gate=ffn2_w_gate, ffn2_w_up=ffn2_w_up, ffn2_w_down=ffn2_w_down)
    states_p = [zero_states(x_prompt.shape[0]) for _ in range(DEPTH)]
    y_prompt, (p_conv, p_lru_h, p_mlstm_C, p_mlstm_n, p_mlstm_m, p_rwkv_shift, p_rwkv_S) = run_trunk(
        x_prompt, states_p, params, final_norm)
    states_s = [(state_conv[l], state_lru_h[l], state_mlstm_C[l], state_mlstm_n[l], state_mlstm_m[l],
                 state_rwkv_shift[l], state_rwkv_S[l]) for l in range(DEPTH)]
    y_sample, (s_conv, s_lru_h, s_mlstm_C, s_mlstm_n, s_mlstm_m, s_rwkv_shift, s_rwkv_S) = run_trunk(
        x_sample, states_s, params, final_norm)
    return (y_prompt, y_sample,
            p_conv, p_lru_h, p_mlstm_C, p_mlstm_n, p_mlstm_m, p_rwkv_shift, p_rwkv_S,
            s_conv, s_lru_h, s_mlstm_C, s_mlstm_n, s_mlstm_m, s_rwkv_shift, s_rwkv_S)
```

```python
import types
import numpy as np
from contextlib import ExitStack
import concourse.bass as bass
import concourse.mybir as mybir
from concourse.bass_utils import run_bass_kernel_spmd

F32 = mybir.dt.float32
BF16 = mybir.dt.bfloat16
AF = mybir.ActivationFunctionType
ALU = mybir.AluOpType
AX = mybir.AxisListType

D = 1024
DFF = 2816
NFF = 22
DEPTH = 2
SEQ = 8192
IN_W = 11528
C_V = 3072
C_O = 4096
C_IF = 5120
C_RW = 5128
C_G = 8456
R_IN = 3328
RMS_EPS = 1e-6
MH_EPS = 1e-6
GN_EPS = 64e-5


def _freeze(fn):
    if getattr(fn, "__closure__", None) is None:
        return fn
    cells = []
    for c in fn.__closure__:
        try:
            cells.append(types.CellType(c.cell_contents))
        except ValueError:
            cells.append(c)
    return types.FunctionType(fn.__code__, fn.__globals__, fn.__name__, fn.__defaults__, tuple(cells))


class Prog:
    ENGS = ("pe", "act", "dve", "pool", "sp")

    def __init__(self, nc, ndma=8):
        self.nc = nc
        self.stream = {e: [] for e in self.ENGS}
        self.cnt = {e: 0 for e in self.ENGS}
        self.lastw = {}
        self.readers = {}
        self.seen = {e: {} for e in self.ENGS}
        self.ndma = ndma
        self.dma_uses = {}
        self.dma_rr = {e: 0 for e in self.ENGS}
        self.semkeys = set(["pe", "act", "dve", "pool"])
        self.pending = {}

    def _deps(self, eng, r, w):
        deps = {}
        def add(tok):
            k, v = tok
            if k == "pe" and eng == "pe":
                return
            if deps.get(k, 0) < v:
                deps[k] = v
        for x in r:
            if x in self.lastw:
                add(self.lastw[x])
        for x in w:
            if x in self.lastw:
                add(self.lastw[x])
            for t in self.readers.get(x, ()):
                add(t)
        out = []
        seen = self.seen[eng]
        for k, v in deps.items():
            if seen.get(k, 0) >= v:
                continue
            seen[k] = v
            out.append((k, v))
        return out

    def _mark(self, tok, r, w):
        for x in r:
            self.readers.setdefault(x, []).append(tok)
        for x in w:
            self.lastw[x] = tok
            self.readers[x] = []

    def barrier(self):
        toks = [(e, self.cnt[e]) for e in ("pe", "act", "dve", "pool") if self.cnt[e]]
        toks += [(k, 16 * u) for k, u in self.dma_uses.items()]
        for e in self.ENGS:
            self.pending[e] = list(toks)

    def _flush(self, eng, waits):
        pend = self.pending.get(eng)
        if pend:
            seen = self.seen[eng]
            for k, v in pend:
                if k == eng and eng == "pe":
                    continue
                if seen.get(k, 0) < v:
                    seen[k] = v
                    waits.append((k, v))
            self.pending[eng] = []
        return waits

    def op(self, eng, fn, r=(), w=()):
        fn = _freeze(fn)
        waits = self._flush(eng, self._deps(eng, r, w))
        self.cnt[eng] += 1
        tok = (eng, self.cnt[eng])
        self.stream[eng].append((waits, fn, (eng, 1)))
        self._mark(tok, r, w)
        return tok

    def dma(self, q, fn, r=(), w=()):
        fn = _freeze(fn)
        slot = self.dma_rr[q] % self.ndma
        self.dma_rr[q] += 1
        key = ("dma", q, slot)
        self.semkeys.add(key)
        uses = self.dma_uses.get(key, 0)
        waits = self._flush(q, self._deps(q, r, w))
        if uses > 0 and self.seen[q].get(key, 0) < 16 * uses:
            self.seen[q][key] = 16 * uses
            waits.append((key, 16 * uses))
        self.dma_uses[key] = uses + 1
        tok = (key, 16 * (uses + 1))
        self.stream[q].append((waits, fn, (key, 16)))
        self._mark(tok, r, w)
        return tok

    def emit(self):
        nc = self.nc
        with ExitStack() as es:
            sems = {}
            for i, k in enumerate(sorted(self.semkeys, key=str)):
                sems[k] = es.enter_context(nc.semaphore("s%d" % i))
            fin = []
            for k, u in self.dma_uses.items():
                fin.append((k, 16 * u))
            for e in ("pe", "act", "dve", "pool"):
                if self.cnt[e]:
                    fin.append((e, self.cnt[e]))
            block = es.enter_context(nc.Block())

            def run(e, name):
                for waits, fn, inc in self.stream[name]:
                    for k, v in waits:
                        e.wait_ge(sems[k], v)
                    ins = fn(e)
                    ins.then_inc(sems[inc[0]], inc[1])

            @block.tensor
            def _(e):
                run(e, "pe")

            @block.scalar
            def _(e):
                run(e, "act")

            @block.vector
            def _(e):
                run(e, "dve")

            @block.gpsimd
            def _(e):
                run(e, "pool")

            @block.sync
            def _(e):
                run(e, "sp")
                for k, v in fin:
                    e.wait_ge(sems[k], v)


VEC_COLS = {}


def _vec_layout():
    off = 0
    lay = {}
    def add(name, n):
        nonlocal off
        lay[name] = (off, n)
        off += n
    for l in range(DEPTH):
        for nm in ("ffn1_norm", "mix_norm", "ffn2_norm"):
            add((nm, l), 8)
        add(("conv_w", l), 4 * 24)
        add(("conv_b", l), 24)
        for nm in ("lru_ba", "lru_bx", "lru_lambda"):
            add((nm, l), 8)
        add(("rwkv_mu", l), 26)
        for nm in ("rwkv_w0", "rwkv_a0", "rwkv_k_k", "rwkv_k_a"):
            add((nm, l), 8)
        add(("if_bias", l), 2)
        for nm in ("rwkv_r_k", "rwkv_ln_w", "rwkv_ln_b", "mlstm_norm"):
            add((nm, l), 8)
    add(("final_norm", 0), 8)
    return lay, off


VEC_LAY, VEC_N = _vec_layout()
CST_BD = 640 + 512
CST_PM = CST_BD + 384
CST_N = CST_PM + 2


class Builder:
    def __init__(self, n_prompt_tiles=32, n_sample=2, NP=256, NS=16, do_mix=True):
        self.NPT = n_prompt_tiles
        self.NSMP = n_sample
        self.NP = NP
        self.NS = NS
        self.TP = n_prompt_tiles * NP
        self.do_mix = do_mix
        self.nc = bass.Bass("TRN2", target_bir_lowering=False)
        self.P = Prog(self.nc)
        self.es = ExitStack()
        self._bank = 0
        self._slab = 0

    def din(self, name, shape, dt=F32):
        return self.nc.dram_tensor(name, list(shape), dt, kind="ExternalInput").ap()

    def dout(self, name, shape, dt=F32):
        return self.nc.dram_tensor(name, list(shape), dt, kind="ExternalOutput").ap()

    def sb(self, name, shape, dt=F32):
        return self.es.enter_context(self.nc.sbuf_tensor("sb_" + name, list(shape), dt))

    def bank(self):
        b = self._bank
        self._bank = (self._bank + 1) % 8
        return b

    def declare(self):
        TP, NSMP, NS = self.TP, self.NSMP, self.NS
        d = {}
        d["xp"] = self.din("xp", [TP, D])
        d["xs"] = self.din("xs", [NSMP * NS, D])
        d["vecs"] = self.din("vecs", [128, VEC_N])
        d["ident"] = self.din("ident", [128, 128])
        for nm in ("ffn1_w_gate", "ffn1_w_up", "ffn2_w_gate", "ffn2_w_up"):
            d[nm] = self.din(nm, [DEPTH, D, DFF])
        for nm in ("ffn1_w_down", "ffn2_w_down"):
            d[nm] = self.din(nm, [DEPTH, DFF, D])
        d["w_in"] = self.din("w_in", [DEPTH, D, IN_W])
        d["w_branch"] = self.din("w_branch", [DEPTH, 3, D, D])
        d["w_out"] = self.din("w_out", [DEPTH, D, D])
        d["cst"] = self.din("cst", [128, CST_N])
        d["lru_wa"] = self.din("lru_wa", [DEPTH, 1024, 128])
        d["lru_wx"] = self.din("lru_wx", [DEPTH, 1024, 128])
        self.d32 = {}
        self.wrows = {}
        for nm in ("ffn1_w_gate", "ffn1_w_up", "ffn1_w_down", "w_in", "lru_wa", "lru_wx", "w_branch", "w_out",
                   "ffn2_w_gate", "ffn2_w_up", "ffn2_w_down"):
            a32 = d[nm]
            self.d32[nm] = a32
            d[nm] = self.nc.dram_tensor(nm + "_bf", list(a32.shape), BF16, kind="Internal").ap()
        d["w2a2"] = self.din("w2a2", [DEPTH, 128, 1024])
        d["g2"] = self.din("g2", [DEPTH, 128, 1024])
        d["s_conv"] = self.din("s_conv", [DEPTH, NSMP, 128, 72])
        d["s_lru"] = self.din("s_lru", [DEPTH, NSMP, 128, 8])
        d["s_C"] = self.din("s_C", [DEPTH, NSMP, 128, 4 * 2 * 257])
        d["s_m"] = self.din("s_m", [DEPTH, NSMP, 4, 1])
        d["s_mbc"] = self.din("s_mbc", [DEPTH, NSMP, 128, 4])
        d["s_shift"] = self.din("s_shift", [DEPTH, NSMP, 128, 26])
        d["s_S"] = self.din("s_S", [DEPTH, NSMP, 128, 512])
        self.d = d
        o = {}
        o["yp"] = self.dout("yp", [TP, D])
        o["ys"] = self.dout("ys", [NSMP * NS, D])
        NQ = NSMP + 1
        o["o_conv"] = self.dout("o_conv", [DEPTH, NQ, 128, 72])
        o["o_lru"] = self.dout("o_lru", [DEPTH, NQ, 128, 8])
        o["o_C"] = self.dout("o_C", [DEPTH, NQ, 128, 4 * 2 * 257])
        o["o_m"] = self.dout("o_m", [DEPTH, NQ, 4, 1])
        o["o_shift"] = self.dout("o_shift", [DEPTH, NQ, 128, 26])
        o["o_S"] = self.dout("o_S", [DEPTH, NQ, 128, 512])
        self.o = o

    def alloc(self):
        NP = self.NP
        s = {}
        s["vecs"] = self.sb("vecs", [128, VEC_N])
        s["ident"] = self.sb("ident", [128, 128])
        s["ones_bf"] = self.sb("ones_bf", [128, 128], BF16)
        s["xT"] = self.sb("xT", [128, 8, NP])
        s["uT"] = self.sb("uT", [128, 8, NP], BF16)
        s["sq"] = self.sb("sq", [128, 8, NP], BF16)
        s["rstd"] = self.sb("rstd", [128, NP])
        s["hT"] = self.sb("hT", [128, NFF, NP], BF16)
        _x = self.sb("xio0", [128, D])
        s["xio"] = [_x, _x]
        self.NSLAB = 3
        s["slab"] = [self.sb("slab%d" % i, [128, 4096], BF16) for i in range(self.NSLAB)]
        W = NP + 4
        s["cst"] = self.sb("cst", [128, CST_N])
        s["cstb"] = self.sb("cstb", [128, 384], BF16)
        s["der"] = self.sb("der", [128, DEPTH * 32])
        s["ones_f"] = self.sb("ones_f", [128, NP])
        s["zA"] = self.sb("zA", [128, 8, W])
        s["cB"] = self.sb("cB", [128, 8, W])
        self.NT = 12
        s["T"] = [self.sb("T%d" % i, [128, NP]) for i in range(self.NT)]
        s["br"] = self.sb("br", [128, 8, NP], BF16)
        s["macc"] = self.sb("macc", [128, 8, NP])
        nbl = max(1, NP // 128)
        self.nbl = nbl
        AM = 4 * (NP * 8 // 2) // 4 * 0 + (NP * 8 // 2) * 3 + (nbl * 4 * 258 // 2) + nbl * 1024 + 64
        arena = self.sb("arenaM", [128, max(AM, 6144 + 64)])
        self.arena = arena
        o = 0
        def carve(n_f32, dt, pat=None, **kw):
            nonlocal o
            v = arena[:, o:o + n_f32]
            o += n_f32
            if dt is BF16:
                v = v.bitcast(BF16)
            if pat:
                v = v.rearrange(pat, **kw)
            return v
        s["xb"] = carve(NP * 4, BF16, "p (a b) -> p a b", a=8)
        s["kTb"] = carve(NP * 4, BF16, "p (a b) -> p a b", a=8)
        s["ktm"] = carve(nbl * 512, BF16, "p (a b) -> p a b", a=nbl)
        s["vs"] = carve(nbl * 4 * 129, BF16, "p (a h b) -> p a h b", a=nbl, h=4)
        s["og"] = carve(nbl * 1024, F32, "p (a b) -> p a b", a=nbl)
        o = 0
        for nm in ("P0", "Q0", "P1", "Q1", "Wb", "TTb"):
            s[nm] = carve(256, BF16)
        for nm in ("TT", "Yf", "Ysq", "Yt"):
            s[nm] = carve(512, F32)
        o = 4096
        nchr = max(1, NP // 64)
        for nm in ("vtmX", "ktTX"):
            s[nm] = carve(nchr * 256, BF16, "p (a b) -> p a b", a=nchr)
        s["gtok"] = self.sb("gtok", [128, nbl, 8])
        s["gLbc"] = self.sb("gLbc", [128, 4, nbl])
        s["sm2"] = [self.sb("sm%d" % i, [128, 128], BF16) for i in range(2)]
        s["hs2"] = [self.sb("hs%d" % i, [128, 256]) for i in range(2)]
        s["sml2"] = [self.sb("sml%d" % i, [128, 32]) for i in range(2)]
        s["tmpC2"] = [self.sb("tmpC%d" % i, [128, 257]) for i in range(2)]
        s["sml"] = s["sml2"][0]
        s["chist"] = [self.sb("chist%d" % l, [128, 24, 3]) for l in range(DEPTH)]
        s["hstate"] = [self.sb("hstate%d" % l, [128, 8]) for l in range(DEPTH)]
        s["mstate"] = [self.sb("mstate%d" % l, [128, 1]) for l in range(DEPTH)]
        s["Ct"] = [self.sb("Ct%d" % l, [128, 4, 2, 257]) for l in range(DEPTH)]
        s["Ctb"] = [self.sb("Ctb%d" % l, [128, 4, 2, 258], BF16) for l in range(DEPTH)]
        s["shist"] = [self.sb("shist%d" % l, [128, 26]) for l in range(DEPTH)]
        s["Z"] = [self.sb("Z%d" % l, [128, 8, 64]) for l in range(DEPTH)]
        s["Zb"] = [self.sb("Zb%d" % l, [128, 2, 8, 64], BF16) for l in range(DEPTH)]
        s["w2a2"] = [self.sb("w2a2_%d" % l, [128, 1024], BF16) for l in range(DEPTH)]
        s["g2"] = [self.sb("g2_%d" % l, [128, 1024], BF16) for l in range(DEPTH)]
        s["zL"] = s["cB"][:, 6:8, :]
        s["lorab"] = self.sb("lorab", [128, NP], BF16)
        s["sgb"] = self.sb("sgb", [128, NP], BF16)
        nchr_ = max(1, NP // 64)
        for nm in ("atX", "btX", "ktX", "rtX"):
            s[nm] = self.sb(nm, [128, 4, nchr_, 2, 64], BF16)
        s["xvX"] = self.sb("xvX", [128, 4, 2, 64])
        s["btTX"] = self.sb("btTX", [128, nchr_, 512], BF16)
        s["tb"] = [self.sb("tb%d" % i, [128, NP], BF16) for i in range(2)]
        s["xvg"] = s["sq"][:, :, :].rearrange("p a b -> p (a b)").bitcast(F32).rearrange("p (a b) -> p a b", a=4)
        hTf = s["hT"][:, :, :].rearrange("p a b -> p (a b)").bitcast(F32)
        s["bon"] = hTf[:, 0:4 * NP].rearrange("p (a b) -> p a b", a=4)
        s["gfm"] = hTf[:, 4 * NP:8 * NP].rearrange("p (a b) -> p a b", a=4)
        s["stC"] = hTf[:, 0:2056]
        s["egL"] = self.sb("egL", [128, 4, nchr])
        s["Gs"] = self.sb("Gs", [128, nchr])
        for nm in ("AakT", "ArbT", "ArkT", "Ub"):
            s[nm] = self.sb(nm, [128, 512], BF16)
        s["ot"] = self.sb("ot", [128, 4, 64])
        s["tmpZ"] = self.sb("tmpZ", [128, 4, 64])
        s["st"] = self.sb("st", [128, 16])
        self.s = s
        self.ps = [self.es.enter_context(self.nc.psum_tensor("ps%d" % i, [128, 512], F32)) for i in range(8)]

    def vec(self, name, l, c0=0, n=1):
        off, cnt = VEC_LAY[(name, l)]
        return self.s["vecs"][:, off + c0: off + c0 + n]

    def load_slab(self, src_ap, kch, cols, wkeys=()):
        i = self._slab % self.NSLAB
        self._slab += 1
        t = self.s["slab"][i]
        view = t[:, 0:kch * cols].rearrange("p (k c) -> p k c", k=kch)
        key = ("slab", i)
        src = src_ap.rearrange("(k p) c -> p k c", p=128)
        wkeys = self.wkeys.get(src_ap.tensor.name, ())
        self.P.dma("sp", lambda e, view=view, src=src: e.dma_start(out=view, in_=src), r=tuple(wkeys), w=(key,))
        return view, key

    def setup(self):
        P, s, d = self.P, self.s, self.d
        P.dma("sp", lambda e: e.dma_start(out=s["vecs"][:], in_=d["vecs"][:, :]), w=("vecs",))
        P.dma("sp", lambda e: e.dma_start(out=s["ident"][:], in_=d["ident"][:, :]), w=("ident",))
        P.op("dve", lambda e: e.memset(s["ones_bf"][:], 1.0), w=("ones",))

    def load_x(self, src, N, xi):
        P, s = self.P, self.s
        nblk = (N + 127) // 128
        for b in range(nblk):
            nb = min(128, N - b * 128)
            xio = s["xio"][xi % 2]
            xk = ("xio", 0)
            xi += 1
            P.dma("sp", lambda e, xio=xio, b=b, nb=nb: e.dma_start(out=xio[0:nb, :], in_=src[b * 128:b * 128 + nb, :]), w=(xk,))
            for g in range(2):
                bk = self.bank()
                ps = self.ps[bk]
                for j in range(4):
                    kc = g * 4 + j
                    P.op("pe", lambda e, ps=ps, xio=xio, kc=kc, j=j, nb=nb: e.transpose(
                        out=ps[:, j * 128:j * 128 + nb], in_=xio[0:nb, kc * 128:(kc + 1) * 128],
                        identity=s["ident"][0:nb, 0:nb]), r=(xk, "ident"), w=(("ps", bk),))
                src_v = ps[:, :].rearrange("p (j t) -> p j t", j=4)[:, :, 0:nb]
                dst_v = s["xT"][:, g * 4:(g + 1) * 4, b * 128:b * 128 + nb]
                P.op("dve", lambda e, src_v=src_v, dst_v=dst_v: e.tensor_copy(out=dst_v, in_=src_v),
                     r=(("ps", bk),), w=("xT",))
        return xi

    def rmsnorm(self, gname, l, N, out_key="uT"):
        P, s = self.P, self.s
        xT, sq, rstd, uT = s["xT"], s["sq"], s["rstd"], s["uT"]
        P.op("act", lambda e: e.activation(out=sq[:, :, 0:N], in_=xT[:, :, 0:N], func=AF.Square),
             r=("xT",), w=("sq",))
        bk = self.bank()
        ps = self.ps[bk]
        for kc in range(8):
            P.op("pe", lambda e, kc=kc: e.matmul(ps[:, 0:N], lhsT=s["ones_bf"][:, :], rhs=sq[:, kc, 0:N],
                                                  start=(kc == 0), stop=(kc == 7)),
                 r=("sq", "ones"), w=(("ps", bk),))
        P.op("act", lambda e: e.activation(out=rstd[:, 0:N], in_=ps[:, 0:N], func=AF.Sqrt,
                                           scale=1.0 / D, bias=self.eps_col(RMS_EPS)),
             r=(("ps", bk), "consts"), w=("rstd",))
        P.op("dve", lambda e: e.reciprocal(out=rstd[:, 0:N], in_=rstd[:, 0:N]), r=("rstd",), w=("rstd",))
        for kc in range(8):
            g = self.vec(gname, l, kc, 1)
            P.op("dve", lambda e, kc=kc, g=g: e.scalar_tensor_tensor(
                out=uT[:, kc, 0:N], in0=xT[:, kc, 0:N], scalar=g, in1=rstd[:, 0:N],
                op0=ALU.mult, op1=ALU.mult), r=("xT", "rstd", "vecs"), w=(out_key,))

    def eps_col(self, val):
        return self.s["eps"][val]

    def ffn(self, pfx, l, N):
        P, s, d = self.P, self.s, self.d
        self.rmsnorm(pfx + "_norm", l, N)
        uT, hT, xT = s["uT"], s["hT"], s["xT"]
        wg, wu, wd = d[pfx + "_w_gate"], d[pfx + "_w_up"], d[pfx + "_w_down"]
        MG = 4
        m = 0
        while m < NFF:
            nm = min(MG, NFF - m)
            cols = nm * 128
            sg_v, sg_k = self.load_slab(wg[l, :, m * 128:m * 128 + cols], 8, cols)
            su_v, su_k = self.load_slab(wu[l, :, m * 128:m * 128 + cols], 8, cols)
            for j in range(nm):
                bg, bu = self.bank(), self.bank()
                pg, pu = self.ps[bg], self.ps[bu]
                for kc in range(8):
                    P.op("pe", lambda e, kc=kc, j=j, pg=pg, sg_v=sg_v: e.matmul(
                        pg[:, 0:N], lhsT=sg_v[:, kc, j * 128:(j + 1) * 128], rhs=uT[:, kc, 0:N],
                        start=(kc == 0), stop=(kc == 7)), r=(sg_k, "uT"), w=(("ps", bg),))
                for kc in range(8):
                    P.op("pe", lambda e, kc=kc, j=j, pu=pu, su_v=su_v: e.matmul(
                        pu[:, 0:N], lhsT=su_v[:, kc, j * 128:(j + 1) * 128], rhs=uT[:, kc, 0:N],
                        start=(kc == 0), stop=(kc == 7)), r=(su_k, "uT"), w=(("ps", bu),))
                sgi = (m + j) % 2
                sgt = s["T"][10 + sgi]
                P.op("act", lambda e, pg=pg, sgt=sgt: e.activation(out=sgt[:, 0:N], in_=pg[:, 0:N], func=AF.Silu),
                     r=(("ps", bg),), w=(("sg", sgi),))
                P.op("dve", lambda e, pu=pu, sgt=sgt, mj=m + j: e.tensor_tensor(
                    out=hT[:, mj, 0:N], in0=pu[:, 0:N], in1=sgt[:, 0:N], op=ALU.mult),
                    r=(("ps", bu), ("sg", sgi)), w=("hT",))
            m += nm
        H = NFF // 2
        for mp in range(4):
            c0 = mp * 256
            sa_v, sa_k = self.load_slab(wd[l, 0:H * 128, c0:c0 + 256], H, 256)
            sb_v, sb_k = self.load_slab(wd[l, H * 128:NFF * 128, c0:c0 + 256], H, 256)
            for jj in range(2):
                mo = mp * 2 + jj
                bk = self.bank()
                ps = self.ps[bk]
                for kc in range(NFF):
                    sv, sk, kk = (sa_v, sa_k, kc) if kc < H else (sb_v, sb_k, kc - H)
                    P.op("pe", lambda e, kc=kc, kk=kk, ps=ps, sv=sv, jj=jj: e.matmul(
                        ps[:, 0:N], lhsT=sv[:, kk, jj * 128:(jj + 1) * 128], rhs=hT[:, kc, 0:N],
                        start=(kc == 0), stop=(kc == NFF - 1)), r=(sk, "hT"), w=(("ps", bk),))
                P.op("dve", lambda e, mo=mo, ps=ps: e.scalar_tensor_tensor(
                    out=xT[:, mo, 0:N], in0=ps[:, 0:N], scalar=0.5, in1=xT[:, mo, 0:N],
                    op0=ALU.mult, op1=ALU.add), r=(("ps", bk), "xT"), w=("xT",))

    def store_y(self, dst, N, xi):
        P, s = self.P, self.s
        xT, sq, rstd = s["xT"], s["sq"], s["rstd"]
        P.op("act", lambda e: e.activation(out=sq[:, :, 0:N], in_=xT[:, :, 0:N], func=AF.Square),
             r=("xT",), w=("sq",))
        bk = self.bank()
        ps = self.ps[bk]
        for kc in range(8):
            P.op("pe", lambda e, kc=kc: e.matmul(ps[:, 0:N], lhsT=s["ones_bf"][:, :], rhs=sq[:, kc, 0:N],
                                                  start=(kc == 0), stop=(kc == 7)),
                 r=("sq", "ones"), w=(("ps", bk),))
        P.op("act", lambda e: e.activation(out=rstd[:, 0:N], in_=ps[:, 0:N], func=AF.Sqrt,
                                           scale=1.0 / D, bias=self.eps_col(RMS_EPS)),
             r=(("ps", bk), "consts"), w=("rstd",))
        P.op("dve", lambda e: e.reciprocal(out=rstd[:, 0:N], in_=rstd[:, 0:N]), r=("rstd",), w=("rstd",))
        for kc in range(8):
            g = self.vec("final_norm", 0, kc, 1)
            P.op("dve", lambda e, kc=kc, g=g: e.scalar_tensor_tensor(
                out=xT[:, kc, 0:N], in0=xT[:, kc, 0:N], scalar=g, in1=rstd[:, 0:N],
                op0=ALU.mult, op1=ALU.mult), r=("xT", "rstd", "vecs"), w=("xT",))
        nblk = (N + 127) // 128
        for b in range(nblk):
            nb = min(128, N - b * 128)
            xio = s["xio"][xi % 2]
            xk = ("xio", 0)
            xi += 1
            for g in range(2):
                bk = self.bank()
                ps = self.ps[bk]
                for j in range(4):
                    kc = g * 4 + j
                    P.op("pe", lambda e, ps=ps, kc=kc, j=j, nb=nb, b=b: e.transpose(
                        out=ps[0:nb, j * 128:(j + 1) * 128], in_=xT[:, kc, b * 128:b * 128 + nb],
                        identity=s["ident"][:, :]), r=("xT", "ident"), w=(("ps", bk),))
                P.op("act", lambda e, ps=ps, xio=xio, g=g, nb=nb: e.activation(
                    out=xio[0:nb, g * 512:(g + 1) * 512], in_=ps[0:nb, :], func=AF.Copy),
                    r=(("ps", bk),), w=(xk,))
            P.dma("sp", lambda e, xio=xio, b=b, nb=nb: e.dma_start(out=dst[b * 128:b * 128 + nb, :], in_=xio[0:nb, :]),
                  r=(xk,), w=())
        return xi

    def T(self, i):
        return self.s["T"][i], ("T", i)

    def der(self, name, l, c0=0, n=1):
        off = l * 32 + {"cl": 0, "cl2": 8, "omka": 16, "nbf": 24}[name]
        return self.s["der"][:, off + c0: off + c0 + n]

    def cst(self, name):
        c = self.s["cst"]
        o = {"ident": 0, "mU1": 128, "mU0": 256, "mL0": 384, "blk": 512}[name]
        return c[:, o:o + 128]

    def cstn(self, o):
        return self.s["cst"][:, o:o + 128]

    def sel(self, h):
        return self.s["cst"][0:4, 640 + h * 128: 640 + (h + 1) * 128]

    def proj_fm(self, l, c0, nch, evac, wsrc=None, rhs=None, rkey="uT", kch=8):
        P = self.P
        N = self.N
        wsrc = wsrc if wsrc is not None else self.d["w_in"][l]
        rhs = rhs if rhs is not None else self.s["uT"]
        m = 0
        while m < nch:
            nm = min(4, nch - m)
            cols = nm * 128
            sv, sk = self.load_slab(wsrc[:, c0 + m * 128: c0 + m * 128 + cols], kch, cols)
            for j in range(nm):
                bk = self.bank()
                ps = self.ps[bk]
                for kc in range(kch):
                    P.op("pe", lambda e, kc=kc, j=j, ps=ps, sv=sv: e.matmul(
                        ps[:, 0:N], lhsT=sv[:, kc, j * 128:(j + 1) * 128], rhs=rhs[:, kc, 0:N],
                        start=(kc == 0), stop=(kc == kch - 1)), r=(sk, rkey), w=(("ps", bk),))
                evac(m + j, ps, bk)
            m += nm

    def proj_tm(self, l, c0, cols, evac):
        P = self.P
        N = self.N
        uT = self.s["uT"]
        sv, sk = self.load_slab(self.d["w_in"][l][:, c0:c0 + cols], 8, cols)
        for b in range(self.nblk):
            nb = min(128, N - b * 128)
            bk = self.bank()
            ps = self.ps[bk]
            for kc in range(8):
                P.op("pe", lambda e, kc=kc, ps=ps, sv=sv, b=b, nb=nb: e.matmul(
                    ps[0:nb, 0:cols], lhsT=uT[:, kc, b * 128:b * 128 + nb], rhs=sv[:, kc, 0:cols],
                    start=(kc == 0), stop=(kc == 7)), r=(sk, "uT"), w=(("ps", bk),))
            evac(b, nb, ps, bk)

    def conv_group(self, l, g):
        P, s, N = self.P, self.s, self.N
        zA, cB, ch = s["zA"], s["cB"], s["chist"][l]
        P.op("pool", lambda e: e.tensor_copy(out=zA[:, :, 0:3], in_=ch[:, g * 8:(g + 1) * 8, :]),
             r=(("chist", l),), w=("zA",))
        def ev(m, ps, bk):
            P.op("act", lambda e: e.activation(out=zA[:, m, 3:3 + N], in_=ps[:, 0:N], func=AF.Copy),
                 r=(("ps", bk),), w=("zA",))
        self.proj_fm(l, g * 1024, 8, ev)
        P.op("pool", lambda e: e.tensor_copy(out=ch[:, g * 8:(g + 1) * 8, :], in_=zA[:, :, N:N + 3]),
             r=("zA",), w=(("chist", l),))
        for m in range(8):
            c = g * 8 + m
            w = [self.vec("conv_w", l, j * 24 + c, 1) for j in range(4)]
            b = self.vec("conv_b", l, c, 1)
            P.op("dve", lambda e, m=m, w=w, b=b: e.tensor_scalar(
                out=cB[:, m, 0:N], in0=zA[:, m, 0:N], scalar1=w[0], scalar2=b, op0=ALU.mult, op1=ALU.add),
                r=("zA", "vecs"), w=("cB",))
            for j in range(1, 4):
                P.op("dve", lambda e, m=m, w=w, j=j: e.scalar_tensor_tensor(
                    out=cB[:, m, 0:N], in0=zA[:, m, j:j + N], scalar=w[j], in1=cB[:, m, 0:N],
                    op0=ALU.mult, op1=ALU.add), r=("zA", "vecs", "cB"), w=("cB",))

    def merge_branch(self, l, b):
        P, s, N = self.P, self.s, self.N
        br, macc = s["br"], s["macc"]
        wb = self.d["w_branch"][l, b]
        m = 0
        while m < 8:
            gv, gk = self.load_slab(self.d["w_in"][l][:, C_G + b * 1024 + m * 128: C_G + b * 1024 + m * 128 + 512], 8, 512)
            pv, pk = self.load_slab(wb[:, m * 128:m * 128 + 512], 8, 512)
            for j in range(4):
                bg, bp = self.bank(), self.bank()
                pg, pp = self.ps[bg], self.ps[bp]
                for kc in range(8):
                    P.op("pe", lambda e, kc=kc, j=j, pg=pg, gv=gv: e.matmul(
                        pg[:, 0:N], lhsT=gv[:, kc, j * 128:(j + 1) * 128], rhs=s["uT"][:, kc, 0:N],
                        start=(kc == 0), stop=(kc == 7)), r=(gk, "uT"), w=(("ps", bg),))
                for kc in range(8):
                    P.op("pe", lambda e, kc=kc, j=j, pp=pp, pv=pv: e.matmul(
                        pp[:, 0:N], lhsT=pv[:, kc, j * 128:(j + 1) * 128], rhs=br[:, kc, 0:N],
                        start=(kc == 0), stop=(kc == 7)), r=(pk, "br"), w=(("ps", bp),))
                ti = (m + j) % 2
                t, tk = self.T(ti)
                P.op("act", lambda e, t=t, pg=pg: e.activation(out=t[:, 0:N], in_=pg[:, 0:N], func=AF.Sigmoid),
                     r=(("ps", bg),), w=(tk,))
                mj = m + j
                if False:
                    P.op("dve", lambda e, t=t, pp=pp, mj=mj: e.tensor_tensor(
                        out=macc[:, mj, 0:N], in0=pp[:, 0:N], in1=t[:, 0:N], op=ALU.mult),
                        r=(("ps", bp), tk), w=("macc",))
                else:
                    P.op("dve", lambda e, t=t, pp=pp: e.tensor_tensor(
                        out=t[:, 0:N], in0=pp[:, 0:N], in1=t[:, 0:N], op=ALU.mult),
                        r=(("ps", bp), tk), w=(tk,))
                    P.op("pool", lambda e, t=t, mj=mj: e.tensor_tensor(
                        out=macc[:, mj, 0:N], in0=macc[:, mj, 0:N], in1=t[:, 0:N], op=ALU.add),
                        r=(tk, "macc"), w=("macc",))
            m += 4

    def merge_out(self, l):
        P, s, N = self.P, self.s, self.N
        br, macc, xT = s["br"], s["macc"], s["xT"]
        P.op("act", lambda e: e.activation(out=br[:, :, 0:N], in_=macc[:, :, 0:N], func=AF.Copy),
             r=("macc",), w=("br",))
        def ev(m, ps, bk):
            P.op("dve", lambda e: e.tensor_tensor(out=xT[:, m, 0:N], in0=ps[:, 0:N], in1=xT[:, m, 0:N], op=ALU.add),
                 r=(("ps", bk), "xT"), w=("xT",))
        self.proj_fm(l, 0, 8, ev, wsrc=self.d["w_out"][l], rhs=br, rkey="br")

    def lru(self, l):
        P, s, N = self.P, self.s, self.N
        cB, xb, br = s["cB"], s["xb"], s["br"]
        self.conv_group(l, 0)
        P.op("act", lambda e: e.activation(out=xb[:, :, 0:N], in_=cB[:, :, 0:N], func=AF.Copy), r=("cB",), w=("xb",))
        wav, wak = self.load_slab(self.d["lru_wa"][l], 8, 128)
        wxv, wxk = self.load_slab(self.d["lru_wx"][l], 8, 128)
        hst = s["hstate"][l]
        for n in range(8):
            o = (n % 2) * 5
            (t1, k1), (t2, k2), (t3, k3), (t4, k4), (t5, k5) = [self.T(o + i) for i in range(5)]
            ba, bb = self.bank(), self.bank()
            pa, pb = self.ps[ba], self.ps[bb]
            P.op("pe", lambda e, n=n, pa=pa: e.matmul(pa[:, 0:N], lhsT=wav[:, n, :], rhs=xb[:, n, 0:N], start=True, stop=True),
                 r=(wak, "xb"), w=(("ps", ba),))
            P.op("pe", lambda e, n=n, pb=pb: e.matmul(pb[:, 0:N], lhsT=wxv[:, n, :], rhs=xb[:, n, 0:N], start=True, stop=True),
                 r=(wxk, "xb"), w=(("ps", bb),))
            P.op("act", lambda e, n=n, pa=pa, t1=t1: e.activation(out=t1[:, 0:N], in_=pa[:, 0:N], func=AF.Sigmoid,
                                                                bias=self.vec("lru_ba", l, n, 1)), r=(("ps", ba), "vecs"), w=(k1,))
            P.op("act", lambda e, n=n, pb=pb, t2=t2: e.activation(out=t2[:, 0:N], in_=pb[:, 0:N], func=AF.Sigmoid,
                                                                bias=self.vec("lru_bx", l, n, 1)), r=(("ps", bb), "vecs"), w=(k2,))
            P.op("act", lambda e, n=n, t1=t1, t3=t3: e.activation(out=t3[:, 0:N], in_=t1[:, 0:N], func=AF.Exp,
                                                                scale=self.der("cl", l, n, 1)), r=(k1, "der"), w=(k3,))
            P.op("act", lambda e, n=n, t1=t1, t4=t4: e.activation(out=t4[:, 0:N], in_=t1[:, 0:N], func=AF.Exp,
                                                                scale=self.der("cl2", l, n, 1)), r=(k1, "der"), w=(k4,))
            P.op("dve", lambda e, t4=t4: e.tensor_scalar(out=t4[:, 0:N], in0=t4[:, 0:N], scalar1=-1.0, scalar2=1.0,
                                                         op0=ALU.mult, op1=ALU.add), r=(k4,), w=(k4,))
            P.op("act", lambda e, t4=t4: e.activation(out=t4[:, 0:N], in_=t4[:, 0:N], func=AF.Sqrt), r=(k4,), w=(k4,))
            P.op("pool", lambda e, n=n, t2=t2: e.tensor_tensor(out=t2[:, 0:N], in0=t2[:, 0:N], in1=cB[:, n, 0:N], op=ALU.mult),
                 r=(k2, "cB"), w=(k2,))
            P.op("dve", lambda e, t2=t2, t4=t4: e.tensor_tensor(out=t2[:, 0:N], in0=t2[:, 0:N], in1=t4[:, 0:N], op=ALU.mult),
                 r=(k2, k4), w=(k2,))
            P.op("dve", lambda e, n=n, t3=t3, t2=t2, t5=t5: e.tensor_tensor_scan(
                out=t5[:, 0:N], data0=t3[:, 0:N], data1=t2[:, 0:N], initial=hst[:, n:n + 1], op0=ALU.mult, op1=ALU.add),
                r=(k3, k2, ("hstate", l)), w=(k5,))
            P.op("pool", lambda e, n=n, t5=t5: e.tensor_copy(out=hst[:, n:n + 1], in_=t5[:, N - 1:N]),
                 r=(k5,), w=(("hstate", l),))
            P.op("act", lambda e, n=n, t5=t5: e.activation(out=br[:, n, 0:N], in_=t5[:, 0:N], func=AF.Copy),
                 r=(k5,), w=("br",))

    def mlstm(self, l):
        P, s, N = self.P, self.s, self.N
        nblk, Lc = self.nblk, min(128, self.N)
        cB, qT, kTb, ktm, vs, og = s["cB"], s["xb"], s["kTb"], s["ktm"], s["vs"], s["og"]
        gtok, gLbc, sml = s["gtok"], s["gLbc"], s["sml"]
        ident = self.cst("ident")
        sv, sk = self.load_slab(self.d["w_in"][l][:, C_IF:C_IF + 8], 8, 8)
        bi, bf_ = self.bank(), self.bank()
        pi, pf = self.ps[bi], self.ps[bf_]
        for kc in range(8):
            P.op("pe", lambda e, kc=kc: e.matmul(pi[0:4, 0:N], lhsT=sv[:, kc, 0:4], rhs=s["uT"][:, kc, 0:N],
                                                  start=(kc == 0), stop=(kc == 7)), r=(sk, "uT"), w=(("ps", bi),))
        for kc in range(8):
            P.op("pe", lambda e, kc=kc: e.matmul(pf[0:4, 0:N], lhsT=sv[:, kc, 4:8], rhs=s["uT"][:, kc, 0:N],
                                                  start=(kc == 0), stop=(kc == 7)), r=(sk, "uT"), w=(("ps", bf_),))
        G = [self.T(i) for i in range(6)]
        (g0, k0), (g1, k1), (g2, k2), (g3, k3), (g4, k4), (g5, k5) = G
        ibias = self.vec("if_bias", l, 0, 1)
        P.op("act", lambda e: e.activation(out=g0[0:4, 0:N], in_=pi[0:4, 0:N], func=AF.Identity, bias=ibias[0:4, :]),
             r=(("ps", bi), "vecs"), w=(k0,))
        P.op("act", lambda e: e.activation(out=g1[0:4, 0:N], in_=pf[0:4, 0:N], func=AF.Exp, scale=-1.0,
                                           bias=self.der("nbf", l, 0, 1)[0:4, :]), r=(("ps", bf_), "der"), w=(k1,))
        P.op("act", lambda e: e.activation(out=g1[0:4, 0:N], in_=g1[0:4, 0:N], func=AF.Ln, bias=s["eps"][1.0][0:4, :]),
             r=(k1, "consts"), w=(k1,))
        P.op("dve", lambda e: e.tensor_scalar(out=g1[0:4, 0:N], in0=g1[0:4, 0:N], scalar1=-1.0, scalar2=None, op0=ALU.mult),
             r=(k1,), w=(k1,))
        mst = s["mstate"][l]
        P.op("dve", lambda e: e.tensor_tensor_scan(out=g2[0:4, 0:N], data0=g1[0:4, 0:N], data1=g0[0:4, 0:N],
                                                   initial=mst[0:4, 0:1], op0=ALU.add, op1=ALU.max),
             r=(k1, k0, ("mstate", l)), w=(k2,))
        P.op("pool", lambda e: e.tensor_copy(out=mst[0:4, 0:1], in_=g2[0:4, N - 1:N]), r=(k2,), w=(("mstate", l),))
        for c in range(nblk):
            P.op("dve", lambda e, c=c: e.tensor_tensor_scan(
                out=g3[0:4, c * Lc:(c + 1) * Lc], data0=s["ones_f"][0:4, 0:Lc], data1=g1[0:4, c * Lc:(c + 1) * Lc],
                initial=0.0, op0=ALU.mult, op1=ALU.add), r=(k1, "ones"), w=(k3,))
        P.op("act", lambda e: e.activation(out=g4[0:4, 0:N], in_=g3[0:4, 0:N], func=AF.Exp), r=(k3,), w=(k4,))
        P.op("dve", lambda e: e.tensor_tensor(out=g5[0:4, 0:N], in0=g0[0:4, 0:N], in1=g3[0:4, 0:N], op=ALU.subtract),
             r=(k0, k3), w=(k5,))
        P.op("act", lambda e: e.activation(out=g5[0:4, 0:N], in_=g5[0:4, 0:N], func=AF.Exp), r=(k5,), w=(k5,))
        for b in range(nblk):
            nb = min(128, N - b * 128)
            bk = self.bank()
            ps = self.ps[bk]
            P.op("pe", lambda e, b=b, nb=nb, ps=ps: e.transpose(out=ps[0:nb, 0:4], in_=g4[0:4, b * 128:b * 128 + nb],
                                                                identity=ident[0:4, 0:4]), r=(k4, "cst"), w=(("ps", bk),))
            P.op("pe", lambda e, b=b, nb=nb, ps=ps: e.transpose(out=ps[0:nb, 4:8], in_=g5[0:4, b * 128:b * 128 + nb],
                                                                identity=ident[0:4, 0:4]), r=(k5, "cst"), w=(("ps", bk),))
            P.op("act", lambda e, b=b, nb=nb, ps=ps: e.activation(out=gtok[0:nb, b, 0:8], in_=ps[0:nb, 0:8], func=AF.Copy),
                 r=(("ps", bk),), w=("gtok",))
        bk = self.bank()
        ps = self.ps[bk]
        for h in range(4):
            if nblk > 1:
                rhs = g4[0:4, Lc - 1:N:Lc]
            else:
                rhs = g4[0:4, N - 1:N]
            P.op("pe", lambda e, h=h, rhs=rhs: e.matmul(ps[:, h * nblk:(h + 1) * nblk], lhsT=self.sel(h), rhs=rhs,
                                                         start=True, stop=True), r=(k4, "cst"), w=(("ps", bk),))
        P.op("act", lambda e: e.activation(out=gLbc[:, :, 0:nblk], in_=ps[:, 0:4 * nblk].rearrange("p (h c) -> p h c", h=4),
                                           func=AF.Copy), r=(("ps", bk),), w=("gLbc",))
        for half in range(2):
            def ev(b, nb, ps, bk, half=half):
                for hh in range(2):
                    h = half * 2 + hh
                    P.op("act", lambda e, h=h, hh=hh: e.activation(out=vs[0:nb, b, h, 0:256], in_=ps[0:nb, hh * 256:(hh + 1) * 256],
                                                                  func=AF.Copy, scale=gtok[0:nb, b, 4 + h:5 + h]),
                         r=(("ps", bk), "gtok"), w=("vs",))
            self.proj_tm(l, C_V + half * 512, 512, ev)
        for b in range(nblk):
            nb = min(128, N - b * 128)
            P.op("dve", lambda e, b=b, nb=nb: e.tensor_copy(out=vs[0:nb, b, :, 256], in_=gtok[0:nb, b, 4:8]),
                 r=("gtok",), w=("vs",))
        for half in range(2):
            def ev(b, nb, ps, bk, half=half):
                P.op("act", lambda e: e.activation(out=og[0:nb, b, half * 512:(half + 1) * 512], in_=ps[0:nb, 0:512],
                                                   func=AF.Sigmoid), r=(("ps", bk),), w=("og",))
            self.proj_tm(l, C_O + half * 512, 512, ev)
        self.conv_group(l, 2)
        P.op("act", lambda e: e.activation(out=cB[:, :, 0:N], in_=cB[:, :, 0:N], func=AF.Silu), r=("cB",), w=("cB",))
        P.op("dve", lambda e: e.tensor_scalar(out=kTb[:, :, 0:N], in0=cB[:, :, 0:N], scalar1=0.0625, scalar2=None, op0=ALU.mult),
             r=("cB",), w=("kTb",))
        for b in range(nblk):
            nb = min(128, N - b * 128)
            for g in range(2):
                bk = self.bank()
                ps = self.ps[bk]
                for j in range(4):
                    fc = g * 4 + j
                    P.op("pe", lambda e, ps=ps, fc=fc, j=j, b=b, nb=nb: e.transpose(
                        out=ps[0:nb, j * 128:(j + 1) * 128], in_=cB[:, fc, b * 128:b * 128 + nb], identity=ident),
                        r=("cB", "cst"), w=(("ps", bk),))
                P.op("act", lambda e, ps=ps, g=g, b=b, nb=nb: e.activation(
                    out=ktm[0:nb, b, g * 512:(g + 1) * 512], in_=ps[0:nb, :], func=AF.Copy, scale=0.0625),
                    r=(("ps", bk),), w=("ktm",))
        self.conv_group(l, 1)
        P.op("act", lambda e: e.activation(out=qT[:, :, 0:N], in_=cB[:, :, 0:N], func=AF.Silu), r=("cB",), w=("xb",))
        Ct, Ctb = s["Ct"][l], s["Ctb"][l]
        mU1 = self.cst("mU1")
        for c in range(nblk):
            cs = slice(c * Lc, (c + 1) * Lc)
            for h in range(4):
                pi_ = h % 2
                sm, hs, tmpC, sml = s["sm2"][pi_], s["hs2"][pi_], s["tmpC2"][pi_], s["sml2"][pi_]
                SM, HS, TC, SL = ("sm", pi_), ("hs", pi_), ("tmpC", pi_), ("sml", pi_)
                bs, bo = self.bank(), self.bank()
                pS, pO = self.ps[bs], self.ps[bo]
                for dc in range(2):
                    P.op("pe", lambda e, dc=dc, h=h, cs=cs, pS=pS: e.matmul(
                        pS[0:Lc, 0:Lc], lhsT=kTb[:, 2 * h + dc, cs], rhs=qT[:, 2 * h + dc, cs],
                        start=(dc == 0), stop=(dc == 1)), r=("kTb", "xb"), w=(("ps", bs),))
                P.op("dve", lambda e, pS=pS: e.tensor_tensor(out=sm[0:Lc, 0:Lc], in0=pS[0:Lc, 0:Lc], in1=mU1[0:Lc, 0:Lc], op=ALU.mult),
                     r=(("ps", bs), "cst"), w=(SM,))
                P.op("pe", lambda e, h=h, c=c, pO=pO: e.matmul(pO[0:Lc, 0:257], lhsT=sm[0:Lc, 0:Lc], rhs=vs[0:Lc, c, h, 0:257],
                                                                 start=True, stop=False), r=(SM, "vs"), w=(("ps", bo),))
                for dc in range(2):
                    P.op("pe", lambda e, dc=dc, h=h, cs=cs, pO=pO: e.matmul(
                        pO[0:Lc, 0:257], lhsT=qT[:, 2 * h + dc, cs], rhs=Ctb[:, h, dc, 0:257],
                        start=False, stop=(dc == 1)), r=("xb", ("Ctb", l)), w=(("ps", bo),))
                rowf = gtok[0:Lc, c, h:h + 1]
                d0, d1, d2 = sml[0:Lc, 0:1], sml[0:Lc, 1:2], sml[0:Lc, 2:3]
                P.op("dve", lambda e, pO=pO, rowf=rowf: e.tensor_scalar(out=d0, in0=pO[0:Lc, 256:257], scalar1=rowf, scalar2=None,
                                                                       op0=ALU.mult), r=(("ps", bo), "gtok"), w=(SL,))
                P.op("dve", lambda e: e.tensor_scalar(out=d1, in0=d0, scalar1=-1.0, scalar2=1.0, op0=ALU.mult, op1=ALU.max), r=(SL,), w=(SL,))
                P.op("dve", lambda e: e.tensor_tensor(out=d0, in0=d0, in1=d1, op=ALU.max), r=(SL,), w=(SL,))
                P.op("dve", lambda e: e.reciprocal(out=d1, in_=d0), r=(SL,), w=(SL,))
                P.op("dve", lambda e, rowf=rowf: e.tensor_tensor(out=d2, in0=d1, in1=rowf, op=ALU.mult), r=(SL, "gtok"), w=(SL,))
                P.op("act", lambda e, pO=pO: e.activation(out=hs[0:Lc, 0:256], in_=pO[0:Lc, 0:256], func=AF.Copy, scale=d2),
                     r=(("ps", bo), SL), w=(HS,))
                st6, mv, rs = sml[0:Lc, 4:10], sml[0:Lc, 10:12], sml[0:Lc, 12:13]
                P.op("dve", lambda e: e.bn_stats(out=st6, in_=hs[0:Lc, 0:256]), r=(HS,), w=(SL,))
                P.op("dve", lambda e: e.bn_aggr(out=mv, in_=st6), r=(SL,), w=(SL,))
                P.op("act", lambda e: e.activation(out=rs, in_=sml[0:Lc, 11:12], func=AF.Sqrt, bias=s["eps"][MH_EPS][0:Lc, :]),
                     r=(SL, "consts"), w=(SL,))
                P.op("dve", lambda e: e.reciprocal(out=rs, in_=rs), r=(SL,), w=(SL,))
                P.op("dve", lambda e: e.tensor_scalar(out=hs[0:Lc, 0:256], in0=hs[0:Lc, 0:256], scalar1=sml[0:Lc, 10:11], scalar2=rs,
                                                      op0=ALU.subtract, op1=ALU.mult), r=(HS, SL), w=(HS,))
                P.op("pool", lambda e, c=c, h=h: e.tensor_tensor(out=og[0:Lc, c, h * 256:(h + 1) * 256], in0=og[0:Lc, c, h * 256:(h + 1) * 256],
                                                                in1=hs[0:Lc, 0:256], op=ALU.mult), r=(HS, "og"), w=("og",))
                for dc in range(2):
                    bc = self.bank()
                    pC = self.ps[bc]
                    P.op("pe", lambda e, dc=dc, h=h, c=c, pC=pC: e.matmul(
                        pC[:, 0:257], lhsT=ktm[0:Lc, c, (2 * h + dc) * 128:(2 * h + dc + 1) * 128], rhs=vs[0:Lc, c, h, 0:257],
                        start=True, stop=True), r=("ktm", "vs"), w=(("ps", bc),))
                    P.op("dve", lambda e, dc=dc, h=h, pC=pC: e.tensor_tensor(out=tmpC[:, :], in0=pC[:, 0:257], in1=Ct[:, h, dc, :], op=ALU.add),
                         r=(("ps", bc), ("Ct", l)), w=(TC,))
                    gl = gLbc[:, h, c:c + 1]
                    P.op("dve", lambda e, dc=dc, h=h, gl=gl: e.tensor_scalar(out=Ct[:, h, dc, :], in0=tmpC[:, :], scalar1=gl, scalar2=None, op0=ALU.mult),
                         r=(TC, "gLbc"), w=(("Ct", l),))
                    P.op("act", lambda e, dc=dc, h=h, gl=gl: e.activation(out=Ctb[:, h, dc, 0:257], in_=tmpC[:, :], func=AF.Copy, scale=gl),
                         r=(TC, "gLbc"), w=(("Ctb", l),))
        br = s["br"]
        for b in range(nblk):
            nb = min(128, N - b * 128)
            for g in range(2):
                bk = self.bank()
                ps = self.ps[bk]
                for j in range(4):
                    fc = g * 4 + j
                    P.op("pe", lambda e, ps=ps, fc=fc, j=j, b=b, nb=nb: e.transpose(
                        out=ps[:, j * 128:j * 128 + nb], in_=og[0:nb, b, fc * 128:(fc + 1) * 128], identity=ident[0:nb, 0:nb]),
                        r=("og", "cst"), w=(("ps", bk),))
                for j in range(4):
                    fc = g * 4 + j
                    P.op("act", lambda e, ps=ps, fc=fc, j=j, b=b, nb=nb: e.activation(
                        out=br[:, fc, b * 128:b * 128 + nb], in_=ps[:, j * 128:j * 128 + nb], func=AF.Copy,
                        scale=self.vec("mlstm_norm", l, fc, 1)), r=(("ps", bk), "vecs"), w=("br",))

    def shiftmix(self, z, j, idx, l, out, okey, zkey, tmp, tkey):
        P, N = self.P, self.N
        mu = self.vec("rwkv_mu", l, idx, 1)
        P.op("pool", lambda e: e.tensor_tensor(out=tmp[:, 0:N], in0=z[:, j, 0:N], in1=z[:, j, 1:N + 1], op=ALU.subtract),
             r=(zkey,), w=(tkey,))
        P.op("dve", lambda e: e.scalar_tensor_tensor(out=out, in0=tmp[:, 0:N], scalar=mu, in1=z[:, j, 1:N + 1],
                                                     op0=ALU.mult, op1=ALU.add), r=(tkey, zkey, "vecs"), w=(okey,))

    def rwkv(self, l):
        P, s, N = self.P, self.s, self.N
        Lr = min(64, N)
        nch = N // Lr
        nsq = {64: 5, 16: 3}[Lr]
        CW = 0.6065306597126334
        zA, cB, zL, sh = s["zA"], s["cB"], s["zL"], s["shist"][l]
        ident = self.cst("ident")
        identb, blkb = s["cstb"][:, 0:128], s["cstb"][:, 128:256]
        lorab, sgb = s["lorab"], s["sgb"]
        w2a2, g2 = s["w2a2"][l], s["g2"][l]
        Z, Zb = s["Z"][l], s["Zb"][l]
        P.op("pool", lambda e: e.tensor_copy(out=zL[:, :, 0], in_=sh[:, 24:26]), r=(("shist", l),), w=("zL",))
        def evl(m, ps, bk):
            P.op("act", lambda e: e.activation(out=zL[:, m, 1:1 + N], in_=ps[:, 0:N], func=AF.Copy), r=(("ps", bk),), w=("zL",))
        self.proj_fm(l, C_RW + 3072, 2, evl)
        P.op("pool", lambda e: e.tensor_copy(out=sh[:, 24:26], in_=zL[:, :, N]), r=("zL",), w=(("shist", l),))
        (t0, k0), (t1, k1) = self.T(0), self.T(1)
        (t2, k2) = self.T(2)
        self.shiftmix(zL, 0, 24, l, t0[:, 0:N], k0, "zL", t2, k2)
        self.shiftmix(zL, 1, 25, l, t1[:, 0:N], k1, "zL", t2, k2)
        P.op("act", lambda e: e.activation(out=lorab[0:64, 0:N], in_=t0[0:64, 0:N], func=AF.Tanh), r=(k0,), w=("lorab",))
        P.op("act", lambda e: e.activation(out=lorab[64:128, 0:N], in_=t0[64:128, 0:N], func=AF.Copy), r=(k0,), w=("lorab",))
        P.op("act", lambda e: e.activation(out=sgb[:, 0:N], in_=t1[:, 0:N], func=AF.Sigmoid), r=(k1,), w=("sgb",))
        atX, btX, ktX, rtX = s["atX"], s["btX"], s["ktX"], s["rtX"]
        xvg, bon, gfm, egL, Gs = s["xvg"], s["bon"], s["gfm"], s["egL"], s["Gs"]
        pmask = s["cst"][:, CST_PM:CST_PM + 2]
        if N < 64:
            for nm in ("atX", "btX", "ktX", "rtX", "xvX"):
                P.op("pool", lambda e, nm=nm: e.memset(s[nm][:], 0.0), w=(nm,))
        br = s["br"]
        for hg in range(2):
            fc0 = hg * 4
            for (buf, bkey, j0, cbase, hbase) in ((zA, "zA", 0, 0, 0), (zA, "zA", 4, 1024, 8), (cB, "cB", 0, 2048, 16)):
                P.op("pool", lambda e, buf=buf, j0=j0, hbase=hbase: e.tensor_copy(out=buf[:, j0:j0 + 4, 0], in_=sh[:, hbase + fc0:hbase + fc0 + 4]),
                     r=(("shist", l),), w=(bkey,))
                def ev(m, ps, bk, buf=buf, j0=j0, bkey=bkey):
                    P.op("act", lambda e: e.activation(out=buf[:, j0 + m, 1:1 + N], in_=ps[:, 0:N], func=AF.Copy), r=(("ps", bk),), w=(bkey,))
                self.proj_fm(l, C_RW + cbase + fc0 * 128, 4, ev)
                P.op("pool", lambda e, buf=buf, j0=j0, hbase=hbase: e.tensor_copy(out=sh[:, hbase + fc0:hbase + fc0 + 4], in_=buf[:, j0:j0 + 4, N]),
                     r=(bkey,), w=(("shist", l),))
            for j in range(4):
                fc = fc0 + j
                TT_ = [self.T(i) for i in range(12)]
                (xr, kr), (xk, kk_), (tq, kq), (sg, ksg), (G, kG), (Gr, kGr), (Gx, kGx), (eg, keg), (egi, kegi), (asg, kas), (kkn, kkk), (tz, ktz) = TT_
                self.shiftmix(zA, j, fc, l, xr[:, 0:N], kr, "zA", tq, kq)
                self.shiftmix(zA, 4 + j, 8 + fc, l, xk[:, 0:N], kk_, "zA", tq, kq)
                self.shiftmix(cB, j, 16 + fc, l, xvg[:, j, 0:N], "xvg", "cB", tq, kq)
                bk = self.bank(); ps = self.ps[bk]
                P.op("pe", lambda e, ps=ps, fc=fc: e.matmul(ps[:, 0:N], lhsT=w2a2[0:64, fc * 128:(fc + 1) * 128], rhs=lorab[0:64, 0:N],
                                                            start=True, stop=True), r=("lorab", "w2a2"), w=(("ps", bk),))
                P.op("act", lambda e, ps=ps, fc=fc: e.activation(out=sg[:, 0:N], in_=ps[:, 0:N], func=AF.Sigmoid, bias=self.vec("rwkv_w0", l, fc, 1)),
                     r=(("ps", bk), "vecs"), w=(ksg,))
                P.op("dve", lambda e: e.tensor_tensor_scan(out=G[:, 0:N], data0=s["ones_f"][:, 0:N], data1=sg[:, 0:N], initial=0.0,
                                                           op0=ALU.mult, op1=ALU.add), r=(ksg, "ones"), w=(kG,))
                P.op("pool", lambda e: e.memset(Gs[:, 0:1], 0.0), w=("Gs",))
                if nch > 1:
                    P.op("pool", lambda e: e.tensor_copy(out=Gs[:, 1:nch], in_=G[:, Lr - 1:N - 1:Lr]), r=(kG,), w=("Gs",))
                P.op("dve", lambda e: e.tensor_tensor(out=Gr[:, 0:N].rearrange("p (c t) -> p c t", c=nch),
                                                      in0=G[:, 0:N].rearrange("p (c t) -> p c t", c=nch),
                                                      in1=Gs[:, 0:nch].unsqueeze(2).to_broadcast([128, nch, Lr]), op=ALU.subtract),
                     r=(kG, "Gs"), w=(kGr,))
                P.op("pool", lambda e: e.tensor_tensor(out=Gx[:, 0:N], in0=Gr[:, 0:N], in1=sg[:, 0:N], op=ALU.subtract), r=(kGr, ksg), w=(kGx,))
                P.op("act", lambda e: e.activation(out=eg[:, 0:N], in_=Gr[:, 0:N], func=AF.Exp, scale=-CW), r=(kGr,), w=(keg,))
                P.op("act", lambda e: e.activation(out=egi[:, 0:N], in_=Gr[:, 0:N], func=AF.Exp, scale=CW), r=(kGr,), w=(kegi,))
                P.op("act", lambda e: e.activation(out=Gx[:, 0:N], in_=Gx[:, 0:N], func=AF.Exp, scale=-CW), r=(kGx,), w=(kGx,))
                if nch > 1:
                    P.op("pool", lambda e, j=j: e.tensor_copy(out=egL[:, j, 0:nch], in_=eg[:, Lr - 1:N:Lr]), r=(keg,), w=("egL",))
                else:
                    P.op("pool", lambda e, j=j: e.tensor_copy(out=egL[:, j, 0:1], in_=eg[:, N - 1:N]), r=(keg,), w=("egL",))
                bk = self.bank(); ps = self.ps[bk]
                P.op("pe", lambda e, ps=ps, fc=fc: e.matmul(ps[:, 0:N], lhsT=w2a2[64:128, fc * 128:(fc + 1) * 128], rhs=lorab[64:128, 0:N],
                                                            start=True, stop=True), r=("lorab", "w2a2"), w=(("ps", bk),))
                P.op("act", lambda e, ps=ps, fc=fc: e.activation(out=asg[:, 0:N], in_=ps[:, 0:N], func=AF.Sigmoid, bias=self.vec("rwkv_a0", l, fc, 1)),
                     r=(("ps", bk), "vecs"), w=(kas,))
                tb0, tb1 = s["tb"]
                P.op("dve", lambda e, fc=fc: e.tensor_scalar(out=kkn[:, 0:N], in0=xk[:, 0:N], scalar1=self.vec("rwkv_k_k", l, fc, 1), scalar2=None, op0=ALU.mult),
                     r=(kk_, "vecs"), w=(kkk,))
                P.op("act", lambda e: e.activation(out=tb0[:, 0:N], in_=kkn[:, 0:N], func=AF.Square), r=(kkk,), w=(("tb", 0),))
                bk = self.bank(); ps = self.ps[bk]
                P.op("pe", lambda e, ps=ps: e.matmul(ps[:, 0:N], lhsT=blkb, rhs=tb0[:, 0:N], start=True, stop=True), r=(("tb", 0), "cstb"), w=(("ps", bk),))
                P.op("dve", lambda e, ps=ps: e.tensor_scalar(out=tz[:, 0:N], in0=ps[:, 0:N], scalar1=1e-24, scalar2=None, op0=ALU.max), r=(("ps", bk),), w=(ktz,))
                P.op("act", lambda e: e.activation(out=tz[:, 0:N], in_=tz[:, 0:N], func=AF.Sqrt), r=(ktz,), w=(ktz,))
                P.op("dve", lambda e: e.reciprocal(out=tz[:, 0:N], in_=tz[:, 0:N]), r=(ktz,), w=(ktz,))
                P.op("dve", lambda e: e.tensor_tensor(out=kkn[:, 0:N], in0=kkn[:, 0:N], in1=tz[:, 0:N], op=ALU.mult), r=(kkk, ktz), w=(kkk,))
                P.op("dve", lambda e, fc=fc: e.tensor_scalar(out=tz[:, 0:N], in0=asg[:, 0:N], scalar1=self.vec("rwkv_k_a", l, fc, 1),
                                                             scalar2=self.der("omka", l, fc, 1), op0=ALU.mult, op1=ALU.add), r=(kas, "vecs", "der"), w=(ktz,))
                P.op("dve", lambda e: e.tensor_tensor(out=xk[:, 0:N], in0=xk[:, 0:N], in1=tz[:, 0:N], op=ALU.mult), r=(kk_, ktz), w=(kk_,))
                def expand(dstX, dkey, src_key):
                    tbx = tb0
                    P.op("pool", lambda e: e.tensor_tensor(
                        out=dstX[:, j, 0:nch, :, 0:Lr],
                        in0=tbx[:, 0:N].rearrange("p (c t) -> p c t", c=nch).unsqueeze(2).to_broadcast([128, nch, 2, Lr]),
                        in1=pmask.unsqueeze(1).unsqueeze(3).to_broadcast([128, nch, 2, Lr]), op=ALU.mult),
                        r=(("tb", 0), "cst"), w=(dkey,))
                P.op("dve", lambda e: e.tensor_tensor(out=tb0[:, 0:N], in0=xr[:, 0:N], in1=eg[:, 0:N], op=ALU.mult), r=(kr, keg), w=(("tb", 0),))
                expand(rtX, "rtX", None)
                P.op("dve", lambda e: e.tensor_tensor(out=tb0[:, 0:N], in0=xk[:, 0:N], in1=egi[:, 0:N], op=ALU.mult), r=(kk_, kegi), w=(("tb", 0),))
                expand(ktX, "ktX", None)
                P.op("dve", lambda e: e.scalar_tensor_tensor(out=tb0[:, 0:N], in0=kkn[:, 0:N], scalar=-1.0, in1=Gx[:, 0:N], op0=ALU.mult, op1=ALU.mult),
                     r=(kkk, kGx), w=(("tb", 0),))
                expand(atX, "atX", None)
                P.op("pool", lambda e: e.tensor_tensor(out=asg[:, 0:N], in0=asg[:, 0:N], in1=egi[:, 0:N], op=ALU.mult), r=(kas, kegi), w=(kas,))
                P.op("dve", lambda e: e.tensor_tensor(out=tb0[:, 0:N], in0=kkn[:, 0:N], in1=asg[:, 0:N], op=ALU.mult), r=(kkk, kas), w=(("tb", 0),))
                expand(btX, "btX", None)
                P.op("dve", lambda e, fc=fc: e.scalar_tensor_tensor(out=tb1[:, 0:N], in0=xr[:, 0:N], scalar=self.vec("rwkv_r_k", l, fc, 1), in1=xk[:, 0:N],
                                                                    op0=ALU.mult, op1=ALU.mult), r=(kr, kk_, "vecs"), w=(("tb", 1),))
                bk = self.bank(); ps = self.ps[bk]
                P.op("pe", lambda e, ps=ps: e.matmul(ps[:, 0:N], lhsT=blkb, rhs=tb1[:, 0:N], start=True, stop=True), r=(("tb", 1), "cstb"), w=(("ps", bk),))
                P.op("dve", lambda e, ps=ps, j=j: e.tensor_tensor(out=bon[:, j, 0:N], in0=ps[:, 0:N], in1=xvg[:, j, 0:N], op=ALU.mult),
                     r=(("ps", bk), "xvg"), w=("bon",))
                bk = self.bank(); ps = self.ps[bk]
                P.op("pe", lambda e, ps=ps, fc=fc: e.matmul(ps[:, 0:N], lhsT=g2[:, fc * 128:(fc + 1) * 128], rhs=sgb[:, 0:N], start=True, stop=True),
                     r=("sgb", "g2"), w=(("ps", bk),))
                P.op("act", lambda e, ps=ps, j=j: e.activation(out=gfm[:, j, 0:N], in_=ps[:, 0:N], func=AF.Copy), r=(("ps", bk),), w=("gfm",))
            P0, Q0, P1, Q1, TTm, Wb, Yf, Ysq, TTb, Yt = (s[n] for n in ("P0", "Q0", "P1", "Q1", "TT", "Wb", "Yf", "Ysq", "TTb", "Yt"))
            AakT, ArbT, ArkT, Ub = s["AakT"], s["ArbT"], s["ArkT"], s["Ub"]
            atX, btX, ktX, rtX = s["atX"], s["btX"], s["ktX"], s["rtX"]
            vtmX, ktTX, btTX, xvX = s["vtmX"], s["ktTX"], s["btTX"], s["xvX"]
            bdU0, bdU1, bdL0 = self.cstn(CST_BD + 0), self.cstn(CST_BD + 128), self.cstn(CST_BD + 256)
            def v4(buf):
                return buf[:, 0:512].rearrange("p (j c) -> p j c", j=4)
            def bc4(m):
                return m.unsqueeze(1).to_broadcast([128, 4, 128])
            def halves(buf):
                v = v4(buf)
                return v[:, :, 0:64], v[:, :, 64:128]
            for ch in range(nch):
                cs = slice(ch * Lr, (ch + 1) * Lr)
                P.op("pool", lambda e: e.tensor_tensor(out=xvX[:, :, :, 0:Lr], in0=xvg[:, :, cs].unsqueeze(2).to_broadcast([128, 4, 2, Lr]),
                                                       in1=pmask.unsqueeze(1).unsqueeze(3).to_broadcast([128, 4, 2, Lr]), op=ALU.mult),
                     r=("xvg", "cst"), w=("xvX",))
                bk = self.bank(); ps = self.ps[bk]
                for j in range(4):
                    P.op("pe", lambda e, j=j: e.transpose(out=ps[:, j * 128:(j + 1) * 128], in_=xvX[:, j, :, :].rearrange("p a b -> p (a b)"), identity=ident),
                         r=("xvX", "cst"), w=(("ps", bk),))
                P.op("act", lambda e: e.activation(out=vtmX[:, ch, :], in_=ps[:, :], func=AF.Copy), r=(("ps", bk),), w=("vtmX",))
                for (src, skey, dst, dkey) in ((ktX, "ktX", ktTX, "ktTX"), (btX, "btX", btTX, "btTX")):
                    bk = self.bank(); psb = self.ps[bk][:, :].bitcast(BF16)
                    for j in range(4):
                        P.op("pe", lambda e, j=j: e.transpose(out=psb[:, j * 128:(j + 1) * 128], in_=src[:, j, ch, :, :].rearrange("p a b -> p (a b)"), identity=identb),
                             r=(skey, "cstb"), w=(("ps", bk),))
                    P.op("dve", lambda e: e.tensor_copy(out=dst[:, ch, :], in_=psb[:, 0:512]), r=(("ps", bk),), w=(dkey,))
                def amat(lh, lk, rh, rk, dst, dkey, mask):
                    bk = self.bank(); ps = self.ps[bk]
                    for j in range(4):
                        P.op("pe", lambda e, j=j: e.matmul(ps[:, j * 128:(j + 1) * 128], lhsT=lh[:, j, ch, :, :].rearrange("p a b -> p (a b)"),
                                                           rhs=rh[:, j, ch, :, :].rearrange("p a b -> p (a b)"), start=True, stop=True),
                             r=(lk, rk), w=(("ps", bk),))
                    P.op("dve", lambda e: e.tensor_tensor(out=v4(dst), in0=v4(ps), in1=bc4(mask), op=ALU.mult), r=(("ps", bk), "cst"), w=(dkey,))
                amat(btX, "btX", atX, "atX", P0, "P0", bdU0)
                amat(atX, "atX", btX, "btX", Q0, "Q0", bdL0)
                amat(ktX, "ktX", atX, "atX", AakT, "AakT", bdU0)
                amat(btX, "btX", rtX, "rtX", ArbT, "ArbT", bdU1)
                amat(ktX, "ktX", rtX, "rtX", ArkT, "ArkT", bdU1)
                P.op("pool", lambda e: e.tensor_tensor(out=v4(TTm), in0=v4(P0), in1=bc4(ident), op=ALU.add), r=("P0", "cst"), w=("TT",))
                P.op("act", lambda e: e.activation(out=TTb[:, 0:512], in_=TTm[:, 0:512], func=AF.Copy), r=("TT",), w=("TTb",))
                Pc, Pk, Qc, Qk = P0, "P0", Q0, "Q0"
                Pn, Pnk, Qn, Qnk = P1, "P1", Q1, "Q1"
                for it in range(nsq):
                    b1, b2 = self.bank(), self.bank()
                    p1, p2 = self.ps[b1], self.ps[b2]
                    for j in range(4):
                        c_ = slice(j * 128, (j + 1) * 128)
                        P.op("pe", lambda e: e.matmul(p1[:, c_], lhsT=Qc[:, c_], rhs=Pc[:, c_], start=True, stop=True), r=(Pk, Qk), w=(("ps", b1),))
                    for j in range(4):
                        c_ = slice(j * 128, (j + 1) * 128)
                        P.op("pe", lambda e: e.matmul(p2[:, c_], lhsT=Pc[:, c_], rhs=Qc[:, c_], start=True, stop=True), r=(Pk, Qk), w=(("ps", b2),))
                    P.op("act", lambda e: e.activation(out=Pn[:, 0:512], in_=p1[:, :], func=AF.Copy), r=(("ps", b1),), w=(Pnk,))
                    P.op("dve", lambda e: e.tensor_copy(out=Qn[:, 0:512], in_=p2[:, :]), r=(("ps", b2),), w=(Qnk,))
                    b3 = self.bank(); p3 = self.ps[b3]
                    for j in range(4):
                        c_ = slice(j * 128, (j + 1) * 128)
                        P.op("pe", lambda e: e.matmul(p3[:, c_], lhsT=Qn[:, c_], rhs=TTb[:, c_], start=True, stop=True), r=(Qnk, "TTb"), w=(("ps", b3),))
                    P.op("dve", lambda e: e.tensor_tensor(out=TTm[:, 0:512], in0=p3[:, :], in1=TTm[:, 0:512], op=ALU.add), r=(("ps", b3), "TT"), w=("TT",))
                    P.op("act", lambda e: e.activation(out=TTb[:, 0:512], in_=TTm[:, 0:512], func=AF.Copy), r=("TT",), w=("TTb",))
                    Pc, Pk, Qc, Qk, Pn, Pnk, Qn, Qnk = Pn, Pnk, Qn, Qnk, Pc, Pk, Qc, Qk
                def zx(j):
                    return Zb[:, :, fc0 + j, :]
                bw = self.bank(); pw = self.ps[bw]
                for j in range(4):
                    c_ = slice(j * 128, (j + 1) * 128)
                    zj = zx(j)
                    P.op("pe", lambda e, j=j: e.matmul(pw[:, c_], lhsT=atX[:, j, ch, :, :].rearrange("p a b -> p (a b)"), rhs=zj, start=True, stop=False),
                         r=("atX", ("Zb", l)), w=(("ps", bw),))
                    P.op("pe", lambda e, j=j: e.matmul(pw[:, c_], lhsT=AakT[:, c_], rhs=vtmX[:, ch, c_], start=False, stop=True),
                         r=("AakT", "vtmX"), w=(("ps", bw),))
                P.op("act", lambda e: e.activation(out=Wb[:, 0:512], in_=pw[:, :], func=AF.Copy), r=(("ps", bw),), w=("Wb",))
                bu = self.bank(); pu = self.ps[bu]
                for j in range(4):
                    c_ = slice(j * 128, (j + 1) * 128)
                    P.op("pe", lambda e: e.matmul(pu[:, c_], lhsT=TTb[:, c_], rhs=Wb[:, c_], start=True, stop=True), r=("TTb", "Wb"), w=(("ps", bu),))
                P.op("act", lambda e: e.activation(out=Ub[:, 0:512], in_=pu[:, :], func=AF.Copy), r=(("ps", bu),), w=("Ub",))
                by = self.bank(); py = self.ps[by]
                for j in range(4):
                    c_ = slice(j * 128, (j + 1) * 128)
                    zj = zx(j)
                    P.op("pe", lambda e, j=j: e.matmul(py[:, c_], lhsT=rtX[:, j, ch, :, :].rearrange("p a b -> p (a b)"), rhs=zj, start=True, stop=False),
                         r=("rtX", ("Zb", l)), w=(("ps", by),))
                    P.op("pe", lambda e: e.matmul(py[:, c_], lhsT=ArbT[:, c_], rhs=Ub[:, c_], start=False, stop=False), r=("ArbT", "Ub"), w=(("ps", by),))
                    P.op("pe", lambda e: e.matmul(py[:, c_], lhsT=ArkT[:, c_], rhs=vtmX[:, ch, c_], start=False, stop=True), r=("ArkT", "vtmX"), w=(("ps", by),))
                st = s["st"]
                s1, s2, mn, vr = st[:, 0:4], st[:, 4:8], st[:, 8:12], st[:, 12:16]
                Ys = Ysq[:, 0:256].rearrange("p (j c) -> p j c", j=4)
                Yq = Ysq[:, 256:512].rearrange("p (j c) -> p j c", j=4)
                P.op("act", lambda e: e.activation(out=Yf[:, 0:512], in_=py[:, :], func=AF.Copy), r=(("ps", by),), w=("Yf",))
                yl, yr_ = halves(Yf)
                P.op("dve", lambda e: e.tensor_tensor(out=Ys, in0=yl, in1=yr_, op=ALU.add), r=("Yf",), w=("Ys",))
                P.op("act", lambda e: e.activation(out=Yq, in_=Ys, func=AF.Square), r=("Ys",), w=("Yq",))
                P.op("dve", lambda e: e.tensor_reduce(out=s1, in_=Ys, axis=AX.X, op=ALU.add), r=("Ys",), w=("st",))
                P.op("dve", lambda e: e.tensor_reduce(out=s2, in_=Yq, axis=AX.X, op=ALU.add), r=("Yq",), w=("st",))
                P.op("dve", lambda e: e.tensor_scalar(out=mn, in0=s1, scalar1=1.0 / 64, scalar2=None, op0=ALU.mult), r=("st",), w=("st",))
                P.op("dve", lambda e: e.tensor_tensor(out=s1, in0=mn, in1=mn, op=ALU.mult), r=("st",), w=("st",))
                P.op("dve", lambda e: e.scalar_tensor_tensor(out=vr, in0=s2, scalar=1.0 / 64, in1=s1, op0=ALU.mult, op1=ALU.subtract), r=("st",), w=("st",))
                P.op("act", lambda e: e.activation(out=vr, in_=vr, func=AF.Sqrt, bias=s["eps"][GN_EPS]), r=("st", "consts"), w=("st",))
                P.op("dve", lambda e: e.reciprocal(out=vr, in_=vr), r=("st",), w=("st",))
                P.op("dve", lambda e: e.tensor_tensor(out=Ys, in0=Ys, in1=mn.unsqueeze(2).to_broadcast([128, 4, 64]), op=ALU.subtract), r=("Ys", "st"), w=("Ys",))
                P.op("dve", lambda e: e.tensor_tensor(out=Ys, in0=Ys, in1=vr.unsqueeze(2).to_broadcast([128, 4, 64]), op=ALU.mult), r=("Ys", "st"), w=("Ys",))
                Yx = Yf[:, 0:512].rearrange("p (j a b) -> p j a b", j=4, a=2)
                P.op("pool", lambda e: e.tensor_tensor(out=Yx, in0=Ys.unsqueeze(2).to_broadcast([128, 4, 2, 64]),
                                                       in1=pmask.unsqueeze(1).unsqueeze(3).to_broadcast([128, 4, 2, 64]), op=ALU.mult),
                     r=("Ys", "cst"), w=("Yf",))
                bt_ = self.bank(); pt = self.ps[bt_]
                for j in range(4):
                    c_ = slice(j * 128, (j + 1) * 128)
                    P.op("pe", lambda e: e.transpose(out=pt[:, c_], in_=Yf[:, c_], identity=ident), r=("Yf", "cst"), w=(("ps", bt_),))
                P.op("act", lambda e: e.activation(out=Yt[:, 0:512], in_=pt[:, :], func=AF.Copy), r=(("ps", bt_),), w=("Yt",))
                ot = s["ot"]
                wl, wr = halves(Yt)
                P.op("dve", lambda e: e.tensor_tensor(out=ot[:, :, :], in0=wl, in1=wr, op=ALU.add), r=("Yt",), w=("ot",))
                for j in range(4):
                    fc = fc0 + j
                    P.op("dve", lambda e, j=j, fc=fc: e.tensor_scalar(out=ot[:, j, :], in0=ot[:, j, :], scalar1=self.vec("rwkv_ln_w", l, fc, 1),
                                                                    scalar2=self.vec("rwkv_ln_b", l, fc, 1), op0=ALU.mult, op1=ALU.add),
                         r=("ot", "vecs"), w=("ot",))
                P.op("pool", lambda e: e.tensor_tensor(out=ot[:, :, 0:Lr], in0=ot[:, :, 0:Lr], in1=bon[:, :, cs], op=ALU.add), r=("ot", "bon"), w=("ot",))
                P.op("dve", lambda e: e.tensor_tensor(out=br[:, fc0:fc0 + 4, cs], in0=ot[:, :, 0:Lr], in1=gfm[:, :, cs], op=ALU.mult), r=("ot", "gfm"), w=("br",))
                bz = self.bank(); pz = self.ps[bz]
                for j in range(4):
                    c_ = slice(j * 128, (j + 1) * 128)
                    P.op("pe", lambda e: e.matmul(pz[:, c_], lhsT=btTX[:, ch, c_], rhs=Ub[:, c_], start=True, stop=False), r=("btTX", "Ub"), w=(("ps", bz),))
                    P.op("pe", lambda e: e.matmul(pz[:, c_], lhsT=ktTX[:, ch, c_], rhs=vtmX[:, ch, c_], start=False, stop=True), r=("ktTX", "vtmX"), w=(("ps", bz),))
                tmpZ = s["tmpZ"]
                P.op("act", lambda e: e.activation(out=Yf[:, 0:512], in_=pz[:, :], func=AF.Copy), r=(("ps", bz),), w=("Yf",))
                zl, zr = halves(Yf)
                P.op("dve", lambda e: e.tensor_tensor(out=tmpZ[:, :, :], in0=zl, in1=zr, op=ALU.add), r=("Yf",), w=("tmpZ",))
                P.op("dve", lambda e: e.tensor_tensor(out=tmpZ[:, :, :], in0=tmpZ[:, :, :], in1=Z[:, fc0:fc0 + 4, :], op=ALU.add), r=("tmpZ", ("Z", l)), w=("tmpZ",))
                P.op("dve", lambda e: e.tensor_tensor(out=Z[:, fc0:fc0 + 4, :], in0=tmpZ[:, :, :], in1=egL[:, :, ch:ch + 1].to_broadcast([128, 4, 64]), op=ALU.mult),
                     r=("tmpZ", "egL"), w=(("Z", l),))
                for par in range(2):
                    P.op("act", lambda e, par=par: e.activation(out=Zb[:, par, fc0:fc0 + 4, :], in_=Z[:, fc0:fc0 + 4, :], func=AF.Copy,
                                                                scale=s["cst"][:, 512 + 64 * par:513 + 64 * par]), r=(("Z", l), "cst"), w=(("Zb", l),))

    def mixer(self, l, N):
        P = self.P
        self.N = N
        self.nblk = max(1, N // 128)
        self.rmsnorm("mix_norm", l, N)
        parts = getattr(self, "parts", ("lru", "mlstm", "rwkv"))
        P.op("pool", lambda e: e.memset(self.s["macc"][:], 0.0), w=("macc",))
        if "lru" in parts:
            self.lru(l)
            self.merge_branch(l, 0)
        if "mlstm" in parts:
            self.mlstm(l)
            self.merge_branch(l, 1)
        P.barrier()
        if "rwkv" in parts:
            self.rwkv(l)
            self.merge_branch(l, 2)
        self.merge_out(l)
        P.barrier()

    def convert_weights(self):
        P = self.P
        self.wkeys = {}
        pat = {2: None, 3: "a r c -> (a r) c", 4: "a b r c -> (a b r) c"}
        for nm in ("ffn1_w_gate", "ffn1_w_up", "ffn1_w_down", "w_in", "lru_wa", "lru_wx", "w_branch", "w_out",
                   "ffn2_w_gate", "ffn2_w_up", "ffn2_w_down"):
            src, dst = self.d32[nm], self.d[nm]
            p = pat[len(src.shape)]
            s2, d2 = (src.rearrange(p), dst.rearrange(p)) if p else (src, dst)
            rows = s2.shape[0]
            keys = []
            for r0 in range(0, rows, 128):
                k = ("wbf", nm, r0)
                P.dma("pool", lambda e, s2=s2, d2=d2, r0=r0: e.dma_start(out=d2[r0:r0 + 128, :], in_=s2[r0:r0 + 128, :]), w=(k,))
                keys.append(k)
            self.wkeys[dst.tensor.name] = keys

    def setup_mixer(self):
        P, s, d = self.P, self.s, self.d
        P.dma("sp", lambda e: e.dma_start(out=s["cst"][:], in_=d["cst"][:, :]), w=("cst",))
        P.op("dve", lambda e: e.memset(s["ones_f"][:], 1.0), w=("ones",))
        P.op("act", lambda e: e.activation(out=s["cstb"][:, 0:128], in_=s["cst"][:, 0:128], func=AF.Copy), r=("cst",), w=("cstb",))
        P.op("act", lambda e: e.activation(out=s["cstb"][:, 128:256], in_=s["cst"][:, 512:640], func=AF.Copy), r=("cst",), w=("cstb",))
        for l in range(DEPTH):
            P.dma("pool", lambda e, l=l: e.dma_start(out=s["w2a2"][l][:], in_=d["w2a2"][l]), w=("w2a2",))
            P.dma("pool", lambda e, l=l: e.dma_start(out=s["g2"][l][:], in_=d["g2"][l]), w=("g2",))
            lam = self.vec("lru_lambda", l, 0, 8)
            cl, cl2 = self.der("cl", l, 0, 8), self.der("cl2", l, 0, 8)
            P.op("act", lambda e, lam=lam, cl=cl: e.activation(out=cl, in_=lam, func=AF.Exp, scale=-1.0), r=("vecs",), w=("der",))
            P.op("act", lambda e, cl=cl: e.activation(out=cl, in_=cl, func=AF.Ln, bias=s["eps"][1.0]), r=("der", "consts"), w=("der",))
            P.op("dve", lambda e, cl=cl, cl2=cl2: e.tensor_scalar(out=cl2, in0=cl, scalar1=-16.0, scalar2=None, op0=ALU.mult), r=("der",), w=("der",))
            P.op("dve", lambda e, cl=cl: e.tensor_scalar(out=cl, in0=cl, scalar1=-8.0, scalar2=None, op0=ALU.mult), r=("der",), w=("der",))
            ka = self.vec("rwkv_k_a", l, 0, 8)
            P.op("dve", lambda e, ka=ka, l=l: e.tensor_scalar(out=self.der("omka", l, 0, 8), in0=ka, scalar1=-1.0, scalar2=1.0, op0=ALU.mult, op1=ALU.add),
                 r=("vecs",), w=("der",))
            fb = self.vec("if_bias", l, 1, 1)
            P.op("dve", lambda e, fb=fb, l=l: e.tensor_scalar(out=self.der("nbf", l, 0, 1), in0=fb, scalar1=-1.0, scalar2=None, op0=ALU.mult),
                 r=("vecs",), w=("der",))

    def init_states(self, kind, i):
        P, s, d = self.P, self.s, self.d
        P.barrier()
        for l in range(DEPTH):
            if kind == "p":
                for nm in ("chist", "hstate", "mstate", "Ct", "Ctb", "shist", "Z", "Zb"):
                    t = s[nm][l]
                    P.op("pool", lambda e, t=t: e.memset(t[:], 0.0), w=((nm, l),))
            else:
                P.dma("sp", lambda e, l=l: e.dma_start(out=s["chist"][l][:].rearrange("p a b -> p (a b)"), in_=d["s_conv"][l, i]), w=(("chist", l),))
                P.dma("sp", lambda e, l=l: e.dma_start(out=s["hstate"][l][:], in_=d["s_lru"][l, i]), w=(("hstate", l),))
                P.dma("sp", lambda e, l=l: e.dma_start(out=s["mstate"][l][0:4, :], in_=d["s_m"][l, i]), w=(("mstate", l),))
                P.dma("sp", lambda e, l=l: e.dma_start(out=s["shist"][l][:], in_=d["s_shift"][l, i]), w=(("shist", l),))
                P.dma("sp", lambda e, l=l: e.dma_start(out=s["Z"][l][:].rearrange("p a b -> p (a b)"), in_=d["s_S"][l, i]), w=(("Z", l),))
                P.dma("sp", lambda e, l=l: e.dma_start(out=s["Ct"][l][:].rearrange("p a b c -> p (a b c)"), in_=d["s_C"][l, i]), w=(("Ct", l),))
                em = s["sml"][:, 16:20]
                P.dma("sp", lambda e, l=l, em=em: e.dma_start(out=em, in_=d["s_mbc"][l, i]), w=("sml",))
                P.op("act", lambda e, em=em: e.activation(out=em, in_=em, func=AF.Exp), r=("sml",), w=("sml",))
                for h in range(4):
                    v = s["Ct"][l][:, h].rearrange("p a b -> p (a b)")
                    P.op("dve", lambda e, v=v, h=h, em=em: e.tensor_scalar(out=v, in0=v, scalar1=em[:, h:h + 1], scalar2=None, op0=ALU.mult),
                         r=("sml", ("Ct", l)), w=(("Ct", l),))
                P.op("act", lambda e, l=l: e.activation(out=s["Ctb"][l][:, :, :, 0:257], in_=s["Ct"][l][:, :, :, :], func=AF.Copy), r=(("Ct", l),), w=(("Ctb", l),))
                for par in range(2):
                    P.op("act", lambda e, l=l, par=par: e.activation(out=s["Zb"][l][:, par], in_=s["Z"][l][:], func=AF.Copy,
                                                                     scale=s["cst"][:, 512 + 64 * par:513 + 64 * par]), r=(("Z", l), "cst"), w=(("Zb", l),))
        P.barrier()

    def out_states(self, q):
        P, s, o = self.P, self.s, self.o
        P.barrier()
        for l in range(DEPTH):
            P.dma("sp", lambda e, l=l: e.dma_start(out=o["o_conv"][l, q], in_=s["chist"][l][:].rearrange("p a b -> p (a b)")), r=(("chist", l),))
            P.dma("sp", lambda e, l=l: e.dma_start(out=o["o_lru"][l, q], in_=s["hstate"][l][:]), r=(("hstate", l),))
            P.dma("sp", lambda e, l=l: e.dma_start(out=o["o_m"][l, q], in_=s["mstate"][l][0:4, :]), r=(("mstate", l),))
            P.dma("sp", lambda e, l=l: e.dma_start(out=o["o_shift"][l, q], in_=s["shist"][l][:]), r=(("shist", l),))
            P.dma("sp", lambda e, l=l: e.dma_start(out=o["o_S"][l, q], in_=s["Z"][l][:].rearrange("p a b -> p (a b)")), r=(("Z", l),))
            bk = self.bank(); ps = self.ps[bk]
            for h in range(4):
                P.op("pe", lambda e, h=h, l=l: e.matmul(ps[:, h:h + 1], lhsT=self.sel(h), rhs=s["mstate"][l][0:4, 0:1], start=True, stop=True),
                     r=(("mstate", l), "cst"), w=(("ps", bk),))
            em = s["sml"][:, 20:24]
            P.op("act", lambda e, em=em: e.activation(out=em, in_=ps[:, 0:4], func=AF.Exp, scale=-1.0), r=(("ps", bk),), w=("sml",))
            stC = s["stC"]
            for h in range(4):
                v = s["Ct"][l][:, h].rearrange("p a b -> p (a b)")
                P.op("dve", lambda e, v=v, h=h, em=em: e.tensor_scalar(out=stC[:, h * 514:(h + 1) * 514], in0=v, scalar1=em[:, h:h + 1], scalar2=None, op0=ALU.mult),
                     r=("sml", ("Ct", l)), w=("stC",))
            P.dma("sp", lambda e, l=l: e.dma_start(out=o["o_C"][l, q], in_=stC[:, 0:2056]), r=("stC",))
            P.barrier()


    def build(self):
        self.declare()
        with self.es:
            self.alloc()
            s = self.s
            s["eps"] = {}
            for v in sorted(set((RMS_EPS, MH_EPS, GN_EPS, 1.0))):
                t = self.sb("eps_%g" % v, [128, 1])
                s["eps"][v] = t[:, 0:1]
                self.P.op("dve", lambda e, t=t, v=v: e.memset(t[:], v), w=("consts",))
            self.setup()
            self.convert_weights()
            self.setup_mixer()
            xi = 0
            tiles = [("p", i) for i in range(self.NPT)] + [("s", i) for i in range(self.NSMP)]
            for kind, i in tiles:
                if kind == "p":
                    N = self.NP
                    src = self.d["xp"][i * N:(i + 1) * N, :]
                    dst = self.o["yp"][i * N:(i + 1) * N, :]
                else:
                    N = self.NS
                    src = self.d["xs"][i * N:(i + 1) * N, :]
                    dst = self.o["ys"][i * N:(i + 1) * N, :]
                self.N = N
                if self.do_mix and (kind == "s" or i == 0):
                    self.init_states(kind, i)
                xi = self.load_x(src, N, xi)
                for l in range(DEPTH):
                    self.ffn("ffn1", l, N)
                    if self.do_mix:
                        self.mixer(l, N)
                    self.ffn("ffn2", l, N)
                xi = self.store_y(dst, N, xi)
                if self.do_mix and (kind == "s" or i == self.NPT - 1):
                    self.out_states(0 if kind == "p" else 1 + i)
            self.P.emit()
        return self.nc


def pack_vecs(inp):
    v = np.zeros((128, VEC_N), np.float32)
    def put(name, l, arr):
        off, n = VEC_LAY[(name, l)]
        v[:, off:off + n] = arr
    def fm(a):
        return np.ascontiguousarray(a.reshape(-1, 128).T)
    for l in range(DEPTH):
        for nm in ("ffn1_norm", "mix_norm", "ffn2_norm", "lru_ba", "lru_bx", "lru_lambda",
                   "rwkv_w0", "rwkv_a0", "rwkv_k_k", "rwkv_k_a", "rwkv_ln_w", "rwkv_ln_b", "mlstm_norm"):
            put(nm, l, fm(inp[nm][l]))
        put("rwkv_r_k", l, fm(inp["rwkv_r_k"][l].reshape(-1)))
        cw = inp["conv_w"][l]
        put("conv_w", l, np.concatenate([fm(cw[j]) for j in range(4)], axis=1))
        put("conv_b", l, fm(inp["conv_b"][l]))
        put("rwkv_mu", l, fm(inp["rwkv_mu"][l]))
        ib = np.zeros((128, 2), np.float32)
        ib[0:4, 0] = inp["mlstm_if_bias"][l][0:4]
        ib[0:4, 1] = inp["mlstm_if_bias"][l][4:8]
        put("if_bias", l, ib)
    put("final_norm", 0, fm(inp["final_norm"]))
    return v


def make_consts():
    c = np.zeros((128, CST_N), np.float32)
    c[:, 0:128] = np.eye(128)
    i = np.arange(128)
    c[:, 128:256] = (i[:, None] <= i[None, :])
    c[:, 256:384] = (i[:, None] < i[None, :])
    c[:, 384:512] = (i[None, :] < i[:, None])
    c[0:64, 512:576] = 1.0
    c[64:128, 576:640] = 1.0
    for h in range(4):
        c[h, 640 + h * 128:640 + (h + 1) * 128] = 1.0
    for k, src in enumerate((256, 128, 384)):
        blk = c[0:64, src:src + 64]
        o = CST_BD + 128 * k
        c[0:64, o:o + 64] = blk
        c[64:128, o + 64:o + 128] = blk
    c[0:64, CST_PM] = 1.0
    c[64:128, CST_PM + 1] = 1.0
    return c


WNAMES = ("ffn1_w_gate", "ffn1_w_up", "ffn2_w_gate", "ffn2_w_up", "ffn1_w_down", "ffn2_w_down", "w_in", "w_branch", "w_out")


def make_in_map(inp, xp, sample_ids):
    ns = len(sample_ids)
    m = {"xp": np.ascontiguousarray(xp),
         "xs": np.ascontiguousarray(inp["x_sample"][sample_ids].reshape(ns * 16, D)),
         "vecs": pack_vecs(inp), "cst": make_consts(), "ident": np.eye(128, dtype=np.float32)}
    for nm in WNAMES:
        m[nm] = inp[nm]
    m["lru_wa"] = np.ascontiguousarray(inp["lru_wa"].reshape(DEPTH, 1024, 128))
    m["lru_wx"] = np.ascontiguousarray(inp["lru_wx"].reshape(DEPTH, 1024, 128))
    m["w2a2"] = np.ascontiguousarray(np.concatenate([inp["rwkv_w2"], inp["rwkv_a2"]], axis=1))
    m["g2"] = np.ascontiguousarray(inp["rwkv_g2"])
    sc = np.zeros((DEPTH, ns, 128, 72), np.float32)
    sl = np.zeros((DEPTH, ns, 128, 8), np.float32)
    sC = np.zeros((DEPTH, ns, 128, 2056), np.float32)
    smm = np.zeros((DEPTH, ns, 4, 1), np.float32)
    smb = np.zeros((DEPTH, ns, 128, 4), np.float32)
    ssh = np.zeros((DEPTH, ns, 128, 26), np.float32)
    sS = np.zeros((DEPTH, ns, 128, 512), np.float32)
    for l in range(DEPTH):
        for k, g in enumerate(sample_ids):
            sc[l, k] = inp["state_conv"][l, g].reshape(3, 24, 128).transpose(2, 1, 0).reshape(128, 72)
            sl[l, k] = inp["state_lru_h"][l, g].reshape(8, 128).T
            C0 = inp["state_mlstm_C"][l, g].reshape(4, 256, 2, 128).transpose(3, 0, 2, 1)
            n0 = inp["state_mlstm_n"][l, g].reshape(4, 2, 128).transpose(2, 0, 1)[..., None]
            sC[l, k] = np.concatenate([C0, n0], axis=-1).reshape(128, 2056)
            smm[l, k, :, 0] = inp["state_mlstm_m"][l, g]
            smb[l, k] = np.broadcast_to(inp["state_mlstm_m"][l, g][None, :], (128, 4))
            ssh[l, k] = inp["state_rwkv_shift"][l, g, 0].reshape(26, 128).T
            sS[l, k] = inp["state_rwkv_S"][l, g].reshape(8, 2, 64, 64).transpose(1, 3, 0, 2).reshape(128, 512)
    m.update(s_conv=sc, s_lru=sl, s_C=sC, s_m=smm, s_mbc=smb, s_shift=ssh, s_S=sS)
    return m


def unpack_states(r, q):
    conv = np.stack([r["o_conv"][l, q].reshape(128, 24, 3).transpose(2, 1, 0).reshape(3, 3072) for l in range(DEPTH)])
    lru = np.stack([r["o_lru"][l, q].T.reshape(1024) for l in range(DEPTH)])
    Cn = [r["o_C"][l, q].reshape(128, 4, 2, 257) for l in range(DEPTH)]
    C = np.stack([c[..., :256].transpose(1, 3, 2, 0).reshape(4, 256, 256) for c in Cn])
    n = np.stack([c[..., 256].transpose(1, 2, 0).reshape(4, 256) for c in Cn])
    mm = np.stack([r["o_m"][l, q].reshape(4) for l in range(DEPTH)])
    sh = np.stack([r["o_shift"][l, q].T.reshape(1, 3328) for l in range(DEPTH)])
    S = np.stack([r["o_S"][l, q].reshape(2, 64, 8, 64).transpose(2, 0, 3, 1).reshape(16, 64, 64) for l in range(DEPTH)])
    return [np.ascontiguousarray(a, dtype=np.float32) for a in (conv, lru, C, n, mm, sh, S)]


_NC_CACHE = {}


def kernel(**inp):
    inp = {k: np.asarray(v) for k, v in inp.items()}
    if "full" not in _NC_CACHE:
        _NC_CACHE["full"] = Builder().build()
    nc = _NC_CACHE["full"]
    in_maps = [make_in_map(inp, inp["x_prompt"][c % 4], [2 * c, 2 * c + 1]) for c in range(8)]
    res = run_bass_kernel_spmd(nc, in_maps, core_ids=list(range(8)))
    R = res.results
    yp = np.stack([R[c]["yp"] for c in range(4)], axis=0)
    ys = np.concatenate([R[c]["ys"].reshape(2, 16, D) for c in range(8)], axis=0)
    pst = [unpack_states(R[c], 0) for c in range(4)]
    p_states = [np.stack([pst[c][k] for c in range(4)], axis=1) for k in range(7)]
    sst = []
    for c in range(8):
        for q in (1, 2):
            sst.append(unpack_states(R[c], q))
    s_states = [np.stack([sst[g][k] for g in range(16)], axis=1) for k in range(7)]
    return tuple([yp, ys] + p_states + s_states)
```

```python
import types
import numpy as np
from contextlib import ExitStack
import concourse.bass as bass
import concourse.mybir as mybir
from concourse.bass_utils import run_bass_kernel_spmd

F32 = mybir.dt.float32
BF16 = mybir.dt.bfloat16
AF = mybir.ActivationFunctionType
ALU = mybir.AluOpType
AX = mybir.AxisListType

D = 1024
DFF = 2816
NFF = 22
DEPTH = 2
SEQ = 8192
IN_W = 11528
C_V = 3072
C_O = 4096
C_IF = 5120
C_RW = 5128
C_G = 8456
R_IN = 3328
RMS_EPS = 1e-6
MH_EPS = 1e-6
GN_EPS = 64e-5


def _freeze(fn):
    if getattr(fn, "__closure__", None) is None:
        return fn
    cells = []
    for c in fn.__closure__:
        try:
            cells.append(types.CellType(c.cell_contents))
        except ValueError:
            cells.append(c)
    return types.FunctionType(fn.__code__, fn.__globals__, fn.__name__, fn.__defaults__, tuple(cells))


class Prog:
    ENGS = ("pe", "act", "dve", "pool", "sp")

    def __init__(self, nc, ndma=8):
        self.nc = nc
        self.stream = {e: [] for e in self.ENGS}
        self.cnt = {e: 0 for e in self.ENGS}
        self.lastw = {}
        self.readers = {}
        self.seen = {e: {} for e in self.ENGS}
        self.ndma = ndma
        self.dma_uses = {}
        self.dma_rr = {e: 0 for e in self.ENGS}
        self.semkeys = set(["pe", "act", "dve", "pool"])
        self.pending = {}

    def _deps(self, eng, r, w):
        deps = {}
        def add(tok):
            k, v = tok
            if k == "pe" and eng == "pe":
                return
            if deps.get(k, 0) < v:
                deps[k] = v
        for x in r:
            if x in self.lastw:
                add(self.lastw[x])
        for x in w:
            if x in self.lastw:
                add(self.lastw[x])
            for t in self.readers.get(x, ()):
                add(t)
        out = []
        seen = self.seen[eng]
        for k, v in deps.items():
            if seen.get(k, 0) >= v:
                continue
            seen[k] = v
            out.append((k, v))
        return out

    def _mark(self, tok, r, w):
        for x in r:
            self.readers.setdefault(x, []).append(tok)
        for x in w:
            self.lastw[x] = tok
            self.readers[x] = []

    def barrier(self):
        toks = [(e, self.cnt[e]) for e in ("pe", "act", "dve", "pool") if self.cnt[e]]
        toks += [(k, 16 * u) for k, u in self.dma_uses.items()]
        for e in self.ENGS:
            self.pending[e] = list(toks)

    def _flush(self, eng, waits):
        pend = self.pending.get(eng)
        if pend:
            seen = self.seen[eng]
            for k, v in pend:
                if k == eng and eng == "pe":
                    continue
                if seen.get(k, 0) < v:
                    seen[k] = v
                    waits.append((k, v))
            self.pending[eng] = []
        return waits

    def op(self, eng, fn, r=(), w=()):
        fn = _freeze(fn)
        waits = self._flush(eng, self._deps(eng, r, w))
        self.cnt[eng] += 1
        tok = (eng, self.cnt[eng])
        self.stream[eng].append((waits, fn, (eng, 1)))
        self._mark(tok, r, w)
        return tok

    def dma(self, q, fn, r=(), w=()):
        fn = _freeze(fn)
        slot = self.dma_rr[q] % self.ndma
        self.dma_rr[q] += 1
        key = ("dma", q, slot)
        self.semkeys.add(key)
        uses = self.dma_uses.get(key, 0)
        waits = self._flush(q, self._deps(q, r, w))
        if uses > 0 and self.seen[q].get(key, 0) < 16 * uses:
            self.seen[q][key] = 16 * uses
            waits.append((key, 16 * uses))
        self.dma_uses[key] = uses + 1
        tok = (key, 16 * (uses + 1))
        self.stream[q].append((waits, fn, (key, 16)))
        self._mark(tok, r, w)
        return tok

    def emit(self):
        nc = self.nc
        with ExitStack() as es:
            sems = {}
            for i, k in enumerate(sorted(self.semkeys, key=str)):
                sems[k] = es.enter_context(nc.semaphore("s%d" % i))
            fin = []
            for k, u in self.dma_uses.items():
                fin.append((k, 16 * u))
            for e in ("pe", "act", "dve", "pool"):
                if self.cnt[e]:
                    fin.append((e, self.cnt[e]))
            block = es.enter_context(nc.Block())

            def run(e, name):
                for waits, fn, inc in self.stream[name]:
                    for k, v in waits:
                        e.wait_ge(sems[k], v)
                    ins = fn(e)
                    ins.then_inc(sems[inc[0]], inc[1])

            @block.tensor
            def _(e):
                run(e, "pe")

            @block.scalar
            def _(e):
                run(e, "act")

            @block.vector
            def _(e):
                run(e, "dve")

            @block.gpsimd
            def _(e):
                run(e, "pool")

            @block.sync
            def _(e):
                run(e, "sp")
                for k, v in fin:
                    e.wait_ge(sems[k], v)


VEC_COLS = {}


def _vec_layout():
    off = 0
    lay = {}
    def add(name, n):
        nonlocal off
        lay[name] = (off, n)
        off += n
    for l in range(DEPTH):
        for nm in ("ffn1_norm", "mix_norm", "ffn2_norm"):
            add((nm, l), 8)
        add(("conv_w", l), 4 * 24)
        add(("conv_b", l), 24)
        for nm in ("lru_ba", "lru_bx", "lru_lambda"):
            add((nm, l), 8)
        add(("rwkv_mu", l), 26)
        for nm in ("rwkv_w0", "rwkv_a0", "rwkv_k_k", "rwkv_k_a"):
            add((nm, l), 8)
        add(("if_bias", l), 2)
        for nm in ("rwkv_r_k", "rwkv_ln_w", "rwkv_ln_b", "mlstm_norm"):
            add((nm, l), 8)
    add(("final_norm", 0), 8)
    return lay, off


VEC_LAY, VEC_N = _vec_layout()
CST_BD = 640 + 512
CST_PM = CST_BD + 384
CST_N = CST_PM + 2


class Builder:
    def __init__(self, n_prompt_tiles=32, n_sample=2, NP=256, NS=16, do_mix=True):
        self.NPT = n_prompt_tiles
        self.NSMP = n_sample
        self.NP = NP
        self.NS = NS
        self.TP = n_prompt_tiles * NP
        self.do_mix = do_mix
        self.nc = bass.Bass("TRN2", target_bir_lowering=False)
        self.P = Prog(self.nc)
        self.es = ExitStack()
        self._bank = 0
        self._slab = 0

    def din(self, name, shape, dt=F32):
        return self.nc.dram_tensor(name, list(shape), dt, kind="ExternalInput").ap()

    def dout(self, name, shape, dt=F32):
        return self.nc.dram_tensor(name, list(shape), dt, kind="ExternalOutput").ap()

    def sb(self, name, shape, dt=F32):
        return self.es.enter_context(self.nc.sbuf_tensor("sb_" + name, list(shape), dt))

    def bank(self):
        b = self._bank
        self._bank = (self._bank + 1) % 8
        return b

    def declare(self):
        TP, NSMP, NS = self.TP, self.NSMP, self.NS
        d = {}
        d["xp"] = self.din("xp", [TP, D])
        d["xs"] = self.din("xs", [NSMP * NS, D])
        d["vecs"] = self.din("vecs", [128, VEC_N])
        d["ident"] = self.din("ident", [128, 128])
        for nm in ("ffn1_w_gate", "ffn1_w_up", "ffn2_w_gate", "ffn2_w_up"):
            d[nm] = self.din(nm, [DEPTH, D, DFF])
        for nm in ("ffn1_w_down", "ffn2_w_down"):
            d[nm] = self.din(nm, [DEPTH, DFF, D])
        d["w_in"] = self.din("w_in", [DEPTH, D, IN_W])
        d["w_branch"] = self.din("w_branch", [DEPTH, 3, D, D])
        d["w_out"] = self.din("w_out", [DEPTH, D, D])
        d["cst"] = self.din("cst", [128, CST_N])
        d["lru_wa"] = self.din("lru_wa", [DEPTH, 1024, 128])
        d["lru_wx"] = self.din("lru_wx", [DEPTH, 1024, 128])
        self.d32 = {}
        self.wrows = {}
        for nm in ("ffn1_w_gate", "ffn1_w_up", "ffn1_w_down", "w_in", "lru_wa", "lru_wx", "w_branch", "w_out",
                   "ffn2_w_gate", "ffn2_w_up", "ffn2_w_down"):
            a32 = d[nm]
            self.d32[nm] = a32
            d[nm] = self.nc.dram_tensor(nm + "_bf", list(a32.shape), BF16, kind="Internal").ap()
        d["w2a2"] = self.din("w2a2", [DEPTH, 128, 1024])
        d["g2"] = self.din("g2", [DEPTH, 128, 1024])
        d["s_conv"] = self.din("s_conv", [DEPTH, NSMP, 128, 72])
        d["s_lru"] = self.din("s_lru", [DEPTH, NSMP, 128, 8])
        d["s_C"] = self.din("s_C", [DEPTH, NSMP, 128, 4 * 2 * 257])
        d["s_m"] = self.din("s_m", [DEPTH, NSMP, 4, 1])
        d["s_mbc"] = self.din("s_mbc", [DEPTH, NSMP, 128, 4])
        d["s_shift"] = self.din("s_shift", [DEPTH, NSMP, 128, 26])
        d["s_S"] = self.din("s_S", [DEPTH, NSMP, 128, 512])
        self.d = d
        o = {}
        o["yp"] = self.dout("yp", [TP, D])
        o["ys"] = self.dout("ys", [NSMP * NS, D])
        NQ = NSMP + 1
        o["o_conv"] = self.dout("o_conv", [DEPTH, NQ, 128, 72])
        o["o_lru"] = self.dout("o_lru", [DEPTH, NQ, 128, 8])
        o["o_C"] = self.dout("o_C", [DEPTH, NQ, 128, 4 * 2 * 257])
        o["o_m"] = self.dout("o_m", [DEPTH, NQ, 4, 1])
        o["o_shift"] = self.dout("o_shift", [DEPTH, NQ, 128, 26])
        o["o_S"] = self.dout("o_S", [DEPTH, NQ, 128, 512])
        self.o = o

    def alloc(self):
        NP = self.NP
        s = {}
        s["vecs"] = self.sb("vecs", [128, VEC_N])
        s["ident"] = self.sb("ident", [128, 128])
        s["ones_bf"] = self.sb("ones_bf", [128, 128], BF16)
        s["xT"] = self.sb("xT", [128, 8, NP])
        s["uT"] = self.sb("uT", [128, 8, NP], BF16)
        s["sq"] = self.sb("sq", [128, 8, NP], BF16)
        s["rstd"] = self.sb("rstd", [128, NP])
        s["hT"] = self.sb("hT", [128, NFF, NP], BF16)
        _x = self.sb("xio0", [128, D])
        s["xio"] = [_x, _x]
        self.NSLAB = 3
        s["slab"] = [self.sb("slab%d" % i, [128, 4096], BF16) for i in range(self.NSLAB)]
        W = NP + 4
        s["cst"] = self.sb("cst", [128, CST_N])
        s["cstb"] = self.sb("cstb", [128, 384], BF16)
        s["der"] = self.sb("der", [128, DEPTH * 32])
        s["ones_f"] = self.sb("ones_f", [128, NP])
        s["zA"] = self.sb("zA", [128, 8, W])
        s["cB"] = self.sb("cB", [128, 8, W])
        self.NT = 12
        s["T"] = [self.sb("T%d" % i, [128, NP]) for i in range(self.NT)]
        s["br"] = self.sb("br", [128, 8, NP], BF16)
        s["macc"] = self.sb("macc", [128, 8, NP])
        nbl = max(1, NP // 128)
        self.nbl = nbl
        AM = 4 * (NP * 8 // 2) // 4 * 0 + (NP * 8 // 2) * 3 + (nbl * 4 * 258 // 2) + nbl * 1024 + 64
        arena = self.sb("arenaM", [128, max(AM, 6144 + 64)])
        self.arena = arena
        o = 0
        def carve(n_f32, dt, pat=None, **kw):
            nonlocal o
            v = arena[:, o:o + n_f32]
            o += n_f32
            if dt is BF16:
                v = v.bitcast(BF16)
            if pat:
                v = v.rearrange(pat, **kw)
            return v
        s["xb"] = carve(NP * 4, BF16, "p (a b) -> p a b", a=8)
        s["kTb"] = carve(NP * 4, BF16, "p (a b) -> p a b", a=8)
        s["ktm"] = carve(nbl * 512, BF16, "p (a b) -> p a b", a=nbl)
        s["vs"] = carve(nbl * 4 * 129, BF16, "p (a h b) -> p a h b", a=nbl, h=4)
        s["og"] = carve(nbl * 1024, F32, "p (a b) -> p a b", a=nbl)
        o = 0
        for nm in ("P0", "Q0", "P1", "Q1", "Wb", "TTb"):
            s[nm] = carve(256, BF16)
        for nm in ("TT", "Yf", "Ysq", "Yt"):
            s[nm] = carve(512, F32)
        o = 4096
        nchr = max(1, NP // 64)
        for nm in ("vtmX", "ktTX"):
            s[nm] = carve(nchr * 256, BF16, "p (a b) -> p a b", a=nchr)
        s["gtok"] = self.sb("gtok", [128, nbl, 8])
        s["gLbc"] = self.sb("gLbc", [128, 4, nbl])
        s["sm2"] = [self.sb("sm%d" % i, [128, 128], BF16) for i in range(2)]
        s["hs2"] = [self.sb("hs%d" % i, [128, 256]) for i in range(2)]
        s["sml2"] = [self.sb("sml%d" % i, [128, 32]) for i in range(2)]
        s["tmpC2"] = [self.sb("tmpC%d" % i, [128, 257]) for i in range(2)]
        s["sml"] = s["sml2"][0]
        s["chist"] = [self.sb("chist%d" % l, [128, 24, 3]) for l in range(DEPTH)]
        s["hstate"] = [self.sb("hstate%d" % l, [128, 8]) for l in range(DEPTH)]
        s["mstate"] = [self.sb("mstate%d" % l, [128, 1]) for l in range(DEPTH)]
        s["Ct"] = [self.sb("Ct%d" % l, [128, 4, 2, 257]) for l in range(DEPTH)]
        s["Ctb"] = [self.sb("Ctb%d" % l, [128, 4, 2, 258], BF16) for l in range(DEPTH)]
        s["shist"] = [self.sb("shist%d" % l, [128, 26]) for l in range(DEPTH)]
        s["Z"] = [self.sb("Z%d" % l, [128, 8, 64]) for l in range(DEPTH)]
        s["Zb"] = [self.sb("Zb%d" % l, [128, 2, 8, 64], BF16) for l in range(DEPTH)]
        s["w2a2"] = [self.sb("w2a2_%d" % l, [128, 1024], BF16) for l in range(DEPTH)]
        s["g2"] = [self.sb("g2_%d" % l, [128, 1024], BF16) for l in range(DEPTH)]
        s["zL"] = s["cB"][:, 6:8, :]
        s["lorab"] = self.sb("lorab", [128, NP], BF16)
        s["sgb"] = self.sb("sgb", [128, NP], BF16)
        nchr_ = max(1, NP // 64)
        for nm in ("atX", "btX", "ktX", "rtX"):
            s[nm] = self.sb(nm, [128, 4, nchr_, 2, 64], BF16)
        s["xvX"] = self.sb("xvX", [128, 4, 2, 64])
        s["btTX"] = self.sb("btTX", [128, nchr_, 512], BF16)
        s["tb"] = [self.sb("tb%d" % i, [128, NP], BF16) for i in range(2)]
        s["xvg"] = s["sq"][:, :, :].rearrange("p a b -> p (a b)").bitcast(F32).rearrange("p (a b) -> p a b", a=4)
        hTf = s["hT"][:, :, :].rearrange("p a b -> p (a b)").bitcast(F32)
        s["bon"] = hTf[:, 0:4 * NP].rearrange("p (a b) -> p a b", a=4)
        s["gfm"] = hTf[:, 4 * NP:8 * NP].rearrange("p (a b) -> p a b", a=4)
        s["stC"] = hTf[:, 0:2056]
        s["egL"] = self.sb("egL", [128, 4, nchr])
        s["Gs"] = self.sb("Gs", [128, nchr])
        for nm in ("AakT", "ArbT", "ArkT", "Ub"):
            s[nm] = self.sb(nm, [128, 512], BF16)
        s["ot"] = self.sb("ot", [128, 4, 64])
        s["tmpZ"] = self.sb("tmpZ", [128, 4, 64])
        s["st"] = self.sb("st", [128, 16])
        self.s = s
        self.ps = [self.es.enter_context(self.nc.psum_tensor("ps%d" % i, [128, 512], F32)) for i in range(8)]

    def vec(self, name, l, c0=0, n=1):
        off, cnt = VEC_LAY[(name, l)]
        return self.s["vecs"][:, off + c0: off + c0 + n]

    def load_slab(self, src_ap, kch, cols, wkeys=()):
        i = self._slab % self.NSLAB
        self._slab += 1
        t = self.s["slab"][i]
        view = t[:, 0:kch * cols].rearrange("p (k c) -> p k c", k=kch)
        key = ("slab", i)
        src = src_ap.rearrange("(k p) c -> p k c", p=128)
        wkeys = self.wkeys.get(src_ap.tensor.name, ())
        self.P.dma("sp", lambda e, view=view, src=src: e.dma_start(out=view, in_=src), r=tuple(wkeys), w=(key,))
        return view, key

    def setup(self):
        P, s, d = self.P, self.s, self.d
        P.dma("sp", lambda e: e.dma_start(out=s["vecs"][:], in_=d["vecs"][:, :]), w=("vecs",))
        P.dma("sp", lambda e: e.dma_start(out=s["ident"][:], in_=d["ident"][:, :]), w=("ident",))
        P.op("dve", lambda e: e.memset(s["ones_bf"][:], 1.0), w=("ones",))

    def load_x(self, src, N, xi):
        P, s = self.P, self.s
        nblk = (N + 127) // 128
        for b in range(nblk):
            nb = min(128, N - b * 128)
            xio = s["xio"][xi % 2]
            xk = ("xio", 0)
            xi += 1
            P.dma("sp", lambda e, xio=xio, b=b, nb=nb: e.dma_start(out=xio[0:nb, :], in_=src[b * 128:b * 128 + nb, :]), w=(xk,))
            for g in range(2):
                bk = self.bank()
                ps = self.ps[bk]
                for j in range(4):
                    kc = g * 4 + j
                    P.op("pe", lambda e, ps=ps, xio=xio, kc=kc, j=j, nb=nb: e.transpose(
                        out=ps[:, j * 128:j * 128 + nb], in_=xio[0:nb, kc * 128:(kc + 1) * 128],
                        identity=s["ident"][0:nb, 0:nb]), r=(xk, "ident"), w=(("ps", bk),))
                src_v = ps[:, :].rearrange("p (j t) -> p j t", j=4)[:, :, 0:nb]
                dst_v = s["xT"][:, g * 4:(g + 1) * 4, b * 128:b * 128 + nb]
                P.op("dve", lambda e, src_v=src_v, dst_v=dst_v: e.tensor_copy(out=dst_v, in_=src_v),
                     r=(("ps", bk),), w=("xT",))
        return xi

    def rmsnorm(self, gname, l, N, out_key="uT"):
        P, s = self.P, self.s
        xT, sq, rstd, uT = s["xT"], s["sq"], s["rstd"], s["uT"]
        P.op("act", lambda e: e.activation(out=sq[:, :, 0:N], in_=xT[:, :, 0:N], func=AF.Square),
             r=("xT",), w=("sq",))
        bk = self.bank()
        ps = self.ps[bk]
        for kc in range(8):
            P.op("pe", lambda e, kc=kc: e.matmul(ps[:, 0:N], lhsT=s["ones_bf"][:, :], rhs=sq[:, kc, 0:N],
                                                  start=(kc == 0), stop=(kc == 7)),
                 r=("sq", "ones"), w=(("ps", bk),))
        P.op("act", lambda e: e.activation(out=rstd[:, 0:N], in_=ps[:, 0:N], func=AF.Sqrt,
                                           scale=1.0 / D, bias=self.eps_col(RMS_EPS)),
             r=(("ps", bk), "consts"), w=("rstd",))
        P.op("dve", lambda e: e.reciprocal(out=rstd[:, 0:N], in_=rstd[:, 0:N]), r=("rstd",), w=("rstd",))
        for kc in range(8):
            g = self.vec(gname, l, kc, 1)
            P.op("dve", lambda e, kc=kc, g=g: e.scalar_tensor_tensor(
                out=uT[:, kc, 0:N], in0=xT[:, kc, 0:N], scalar=g, in1=rstd[:, 0:N],
                op0=ALU.mult, op1=ALU.mult), r=("xT", "rstd", "vecs"), w=(out_key,))

    def eps_col(self, val):
        return self.s["eps"][val]

    def ffn(self, pfx, l, N):
        P, s, d = self.P, self.s, self.d
        self.rmsnorm(pfx + "_norm", l, N)
        uT, hT, xT = s["uT"], s["hT"], s["xT"]
        wg, wu, wd = d[pfx + "_w_gate"], d[pfx + "_w_up"], d[pfx + "_w_down"]
        MG = 4
        m = 0
        while m < NFF:
            nm = min(MG, NFF - m)
            cols = nm * 128
            sg_v, sg_k = self.load_slab(wg[l, :, m * 128:m * 128 + cols], 8, cols)
            su_v, su_k = self.load_slab(wu[l, :, m * 128:m * 128 + cols], 8, cols)
            for j in range(nm):
                bg, bu = self.bank(), self.bank()
                pg, pu = self.ps[bg], self.ps[bu]
                for kc in range(8):
                    P.op("pe", lambda e, kc=kc, j=j, pg=pg, sg_v=sg_v: e.matmul(
                        pg[:, 0:N], lhsT=sg_v[:, kc, j * 128:(j + 1) * 128], rhs=uT[:, kc, 0:N],
                        start=(kc == 0), stop=(kc == 7)), r=(sg_k, "uT"), w=(("ps", bg),))
                for kc in range(8):
                    P.op("pe", lambda e, kc=kc, j=j, pu=pu, su_v=su_v: e.matmul(
                        pu[:, 0:N], lhsT=su_v[:, kc, j * 128:(j + 1) * 128], rhs=uT[:, kc, 0:N],
                        start=(kc == 0), stop=(kc == 7)), r=(su_k, "uT"), w=(("ps", bu),))
                sgi = (m + j) % 2
                sgt = s["T"][10 + sgi]
                P.op("act", lambda e, pg=pg, sgt=sgt: e.activation(out=sgt[:, 0:N], in_=pg[:, 0:N], func=AF.Silu),
                     r=(("ps", bg),), w=(("sg", sgi),))
                P.op("dve", lambda e, pu=pu, sgt=sgt, mj=m + j: e.tensor_tensor(
                    out=hT[:, mj, 0:N], in0=pu[:, 0:N], in1=sgt[:, 0:N], op=ALU.mult),
                    r=(("ps", bu), ("sg", sgi)), w=("hT",))
            m += nm
        H = NFF // 2
        for mp in range(4):
            c0 = mp * 256
            sa_v, sa_k = self.load_slab(wd[l, 0:H * 128, c0:c0 + 256], H, 256)
            sb_v, sb_k = self.load_slab(wd[l, H * 128:NFF * 128, c0:c0 + 256], H, 256)
            for jj in range(2):
                mo = mp * 2 + jj
                bk = self.bank()
                ps = self.ps[bk]
                for kc in range(NFF):
                    sv, sk, kk = (sa_v, sa_k, kc) if kc < H else (sb_v, sb_k, kc - H)
                    P.op("pe", lambda e, kc=kc, kk=kk, ps=ps, sv=sv, jj=jj: e.matmul(
                        ps[:, 0:N], lhsT=sv[:, kk, jj * 128:(jj + 1) * 128], rhs=hT[:, kc, 0:N],
                        start=(kc == 0), stop=(kc == NFF - 1)), r=(sk, "hT"), w=(("ps", bk),))
                P.op("dve", lambda e, mo=mo, ps=ps: e.scalar_tensor_tensor(
                    out=xT[:, mo, 0:N], in0=ps[:, 0:N], scalar=0.5, in1=xT[:, mo, 0:N],
                    op0=ALU.mult, op1=ALU.add), r=(("ps", bk), "xT"), w=("xT",))

    def store_y(self, dst, N, xi):
        P, s = self.P, self.s
        xT, sq, rstd = s["xT"], s["sq"], s["rstd"]
        P.op("act", lambda e: e.activation(out=sq[:, :, 0:N], in_=xT[:, :, 0:N], func=AF.Square),
             r=("xT",), w=("sq",))
        bk = self.bank()
        ps = self.ps[bk]
        for kc in range(8):
            P.op("pe", lambda e, kc=kc: e.matmul(ps[:, 0:N], lhsT=s["ones_bf"][:, :], rhs=sq[:, kc, 0:N],
                                                  start=(kc == 0), stop=(kc == 7)),
                 r=("sq", "ones"), w=(("ps", bk),))
        P.op("act", lambda e: e.activation(out=rstd[:, 0:N], in_=ps[:, 0:N], func=AF.Sqrt,
                                           scale=1.0 / D, bias=self.eps_col(RMS_EPS)),
             r=(("ps", bk), "consts"), w=("rstd",))
        P.op("dve", lambda e: e.reciprocal(out=rstd[:, 0:N], in_=rstd[:, 0:N]), r=("rstd",), w=("rstd",))
        for kc in range(8):
            g = self.vec("final_norm", 0, kc, 1)
            P.op("dve", lambda e, kc=kc, g=g: e.scalar_tensor_tensor(
                out=xT[:, kc, 0:N], in0=xT[:, kc, 0:N], scalar=g, in1=rstd[:, 0:N],
                op0=ALU.mult, op1=ALU.mult), r=("xT", "rstd", "vecs"), w=("xT",))
        nblk = (N + 127) // 128
        for b in range(nblk):
            nb = min(128, N - b * 128)
            xio = s["xio"][xi % 2]
            xk = ("xio", 0)
            xi += 1
            for g in range(2):
                bk = self.bank()
                ps = self.ps[bk]
                for j in range(4):
                    kc = g * 4 + j
                    P.op("pe", lambda e, ps=ps, kc=kc, j=j, nb=nb, b=b: e.transpose(
                        out=ps[0:nb, j * 128:(j + 1) * 128], in_=xT[:, kc, b * 128:b * 128 + nb],
                        identity=s["ident"][:, :]), r=("xT", "ident"), w=(("ps", bk),))
                P.op("act", lambda e, ps=ps, xio=xio, g=g, nb=nb: e.activation(
                    out=xio[0:nb, g * 512:(g + 1) * 512], in_=ps[0:nb, :], func=AF.Copy),
                    r=(("ps", bk),), w=(xk,))
            P.dma("sp", lambda e, xio=xio, b=b, nb=nb: e.dma_start(out=dst[b * 128:b * 128 + nb, :], in_=xio[0:nb, :]),
                  r=(xk,), w=())
        return xi

    def T(self, i):
        return self.s["T"][i], ("T", i)

    def der(self, name, l, c0=0, n=1):
        off = l * 32 + {"cl": 0, "cl2": 8, "omka": 16, "nbf": 24}[name]
        return self.s["der"][:, off + c0: off + c0 + n]

    def cst(self, name):
        c = self.s["cst"]
        o = {"ident": 0, "mU1": 128, "mU0": 256, "mL0": 384, "blk": 512}[name]
        return c[:, o:o + 128]

    def cstn(self, o):
        return self.s["cst"][:, o:o + 128]

    def sel(self, h):
        return self.s["cst"][0:4, 640 + h * 128: 640 + (h + 1) * 128]

    def proj_fm(self, l, c0, nch, evac, wsrc=None, rhs=None, rkey="uT", kch=8):
        P = self.P
        N = self.N
        wsrc = wsrc if wsrc is not None else self.d["w_in"][l]
        rhs = rhs if rhs is not None else self.s["uT"]
        m = 0
        while m < nch:
            nm = min(4, nch - m)
            cols = nm * 128
            sv, sk = self.load_slab(wsrc[:, c0 + m * 128: c0 + m * 128 + cols], kch, cols)
            for j in range(nm):
                bk = self.bank()
                ps = self.ps[bk]
                for kc in range(kch):
                    P.op("pe", lambda e, kc=kc, j=j, ps=ps, sv=sv: e.matmul(
                        ps[:, 0:N], lhsT=sv[:, kc, j * 128:(j + 1) * 128], rhs=rhs[:, kc, 0:N],
                        start=(kc == 0), stop=(kc == kch - 1)), r=(sk, rkey), w=(("ps", bk),))
                evac(m + j, ps, bk)
            m += nm

    def proj_tm(self, l, c0, cols, evac):
        P = self.P
        N = self.N
        uT = self.s["uT"]
        sv, sk = self.load_slab(self.d["w_in"][l][:, c0:c0 + cols], 8, cols)
        for b in range(self.nblk):
            nb = min(128, N - b * 128)
            bk = self.bank()
            ps = self.ps[bk]
            for kc in range(8):
                P.op("pe", lambda e, kc=kc, ps=ps, sv=sv, b=b, nb=nb: e.matmul(
                    ps[0:nb, 0:cols], lhsT=uT[:, kc, b * 128:b * 128 + nb], rhs=sv[:, kc, 0:cols],
                    start=(kc == 0), stop=(kc == 7)), r=(sk, "uT"), w=(("ps", bk),))
            evac(b, nb, ps, bk)

    def conv_group(self, l, g):
        P, s, N = self.P, self.s, self.N
        zA, cB, ch = s["zA"], s["cB"], s["chist"][l]
        P.op("pool", lambda e: e.tensor_copy(out=zA[:, :, 0:3], in_=ch[:, g * 8:(g + 1) * 8, :]),
             r=(("chist", l),), w=("zA",))
        def ev(m, ps, bk):
            P.op("act", lambda e: e.activation(out=zA[:, m, 3:3 + N], in_=ps[:, 0:N], func=AF.Copy),
                 r=(("ps", bk),), w=("zA",))
        self.proj_fm(l, g * 1024, 8, ev)
        P.op("pool", lambda e: e.tensor_copy(out=ch[:, g * 8:(g + 1) * 8, :], in_=zA[:, :, N:N + 3]),
             r=("zA",), w=(("chist", l),))
        for m in range(8):
            c = g * 8 + m
            w = [self.vec("conv_w", l, j * 24 + c, 1) for j in range(4)]
            b = self.vec("conv_b", l, c, 1)
            P.op("dve", lambda e, m=m, w=w, b=b: e.tensor_scalar(
                out=cB[:, m, 0:N], in0=zA[:, m, 0:N], scalar1=w[0], scalar2=b, op0=ALU.mult, op1=ALU.add),
                r=("zA", "vecs"), w=("cB",))
            for j in range(1, 4):
                P.op("dve", lambda e, m=m, w=w, j=j: e.scalar_tensor_tensor(
                    out=cB[:, m, 0:N], in0=zA[:, m, j:j + N], scalar=w[j], in1=cB[:, m, 0:N],
                    op0=ALU.mult, op1=ALU.add), r=("zA", "vecs", "cB"), w=("cB",))

    def merge_branch(self, l, b):
        P, s, N = self.P, self.s, self.N
        br, macc = s["br"], s["macc"]
        wb = self.d["w_branch"][l, b]
        m = 0
        while m < 8:
            gv, gk = self.load_slab(self.d["w_in"][l][:, C_G + b * 1024 + m * 128: C_G + b * 1024 + m * 128 + 512], 8, 512)
            pv, pk = self.load_slab(wb[:, m * 128:m * 128 + 512], 8, 512)
            for j in range(4):
                bg, bp = self.bank(), self.bank()
                pg, pp = self.ps[bg], self.ps[bp]
                for kc in range(8):
                    P.op("pe", lambda e, kc=kc, j=j, pg=pg, gv=gv: e.matmul(
                        pg[:, 0:N], lhsT=gv[:, kc, j * 128:(j + 1) * 128], rhs=s["uT"][:, kc, 0:N],
                        start=(kc == 0), stop=(kc == 7)), r=(gk, "uT"), w=(("ps", bg),))
                for kc in range(8):
                    P.op("pe", lambda e, kc=kc, j=j, pp=pp, pv=pv: e.matmul(
                        pp[:, 0:N], lhsT=pv[:, kc, j * 128:(j + 1) * 128], rhs=br[:, kc, 0:N],
                        start=(kc == 0), stop=(kc == 7)), r=(pk, "br"), w=(("ps", bp),))
                ti = (m + j) % 2
                t, tk = self.T(ti)
                P.op("act", lambda e, t=t, pg=pg: e.activation(out=t[:, 0:N], in_=pg[:, 0:N], func=AF.Sigmoid),
                     r=(("ps", bg),), w=(tk,))
                mj = m + j
                if False:
                    P.op("dve", lambda e, t=t, pp=pp, mj=mj: e.tensor_tensor(
                        out=macc[:, mj, 0:N], in0=pp[:, 0:N], in1=t[:, 0:N], op=ALU.mult),
                        r=(("ps", bp), tk), w=("macc",))
                else:
                    P.op("dve", lambda e, t=t, pp=pp: e.tensor_tensor(
                        out=t[:, 0:N], in0=pp[:, 0:N], in1=t[:, 0:N], op=ALU.mult),
                        r=(("ps", bp), tk), w=(tk,))
                    P.op("pool", lambda e, t=t, mj=mj: e.tensor_tensor(
                        out=macc[:, mj, 0:N], in0=macc[:, mj, 0:N], in1=t[:, 0:N], op=ALU.add),
                        r=(tk, "macc"), w=("macc",))
            m += 4

    def merge_out(self, l):
        P, s, N = self.P, self.s, self.N
        br, macc, xT = s["br"], s["macc"], s["xT"]
        P.op("act", lambda e: e.activation(out=br[:, :, 0:N], in_=macc[:, :, 0:N], func=AF.Copy),
             r=("macc",), w=("br",))
        def ev(m, ps, bk):
            P.op("dve", lambda e: e.tensor_tensor(out=xT[:, m, 0:N], in0=ps[:, 0:N], in1=xT[:, m, 0:N], op=ALU.add),
                 r=(("ps", bk), "xT"), w=("xT",))
        self.proj_fm(l, 0, 8, ev, wsrc=self.d["w_out"][l], rhs=br, rkey="br")

    def lru(self, l):
        P, s, N = self.P, self.s, self.N
        cB, xb, br = s["cB"], s["xb"], s["br"]
        self.conv_group(l, 0)
        P.op("act", lambda e: e.activation(out=xb[:, :, 0:N], in_=cB[:, :, 0:N], func=AF.Copy), r=("cB",), w=("xb",))
        wav, wak = self.load_slab(self.d["lru_wa"][l], 8, 128)
        wxv, wxk = self.load_slab(self.d["lru_wx"][l], 8, 128)
        hst = s["hstate"][l]
        def blk(n, stage):
            o = (n % 2) * 5
            (t1, k1), (t2, k2), (t3, k3), (t4, k4), (t5, k5) = [self.T(o + i) for i in range(5)]
            if stage == 0:
                ba, bb = self.bank(), self.bank()
                self._lru_banks[n] = (ba, bb)
            ba, bb = self._lru_banks[n]
            pa, pb = self.ps[ba], self.ps[bb]
            if stage == 0:
                P.op("pe", lambda e, n=n, pa=pa: e.matmul(pa[:, 0:N], lhsT=wav[:, n, :], rhs=xb[:, n, 0:N], start=True, stop=True),
                     r=(wak, "xb"), w=(("ps", ba),))
            if stage == 0:
                P.op("pe", lambda e, n=n, pb=pb: e.matmul(pb[:, 0:N], lhsT=wxv[:, n, :], rhs=xb[:, n, 0:N], start=True, stop=True),
                     r=(wxk, "xb"), w=(("ps", bb),))
            if stage == 0:
                P.op("act", lambda e, n=n, pa=pa, t1=t1: e.activation(out=t1[:, 0:N], in_=pa[:, 0:N], func=AF.Sigmoid,
                                                                    bias=self.vec("lru_ba", l, n, 1)), r=(("ps", ba), "vecs"), w=(k1,))
            if stage == 0:
                P.op("act", lambda e, n=n, pb=pb, t2=t2: e.activation(out=t2[:, 0:N], in_=pb[:, 0:N], func=AF.Sigmoid,
                                                                    bias=self.vec("lru_bx", l, n, 1)), r=(("ps", bb), "vecs"), w=(k2,))
            if stage == 1:
                P.op("act", lambda e, n=n, t1=t1, t3=t3: e.activation(out=t3[:, 0:N], in_=t1[:, 0:N], func=AF.Exp,
                                                                    scale=self.der("cl", l, n, 1)), r=(k1, "der"), w=(k3,))
            if stage == 1:
                P.op("act", lambda e, n=n, t1=t1, t4=t4: e.activation(out=t4[:, 0:N], in_=t1[:, 0:N], func=AF.Exp,
                                                                    scale=self.der("cl2", l, n, 1)), r=(k1, "der"), w=(k4,))
            if stage == 2:
                P.op("dve", lambda e, t4=t4: e.tensor_scalar(out=t4[:, 0:N], in0=t4[:, 0:N], scalar1=-1.0, scalar2=1.0,
                                                             op0=ALU.mult, op1=ALU.add), r=(k4,), w=(k4,))
            if stage == 2:
                P.op("act", lambda e, t4=t4: e.activation(out=t4[:, 0:N], in_=t4[:, 0:N], func=AF.Sqrt), r=(k4,), w=(k4,))
            if stage == 3:
                P.op("pool", lambda e, n=n, t2=t2: e.tensor_tensor(out=t2[:, 0:N], in0=t2[:, 0:N], in1=cB[:, n, 0:N], op=ALU.mult),
                     r=(k2, "cB"), w=(k2,))
            if stage == 3:
                P.op("dve", lambda e, t2=t2, t4=t4: e.tensor_tensor(out=t2[:, 0:N], in0=t2[:, 0:N], in1=t4[:, 0:N], op=ALU.mult),
                     r=(k2, k4), w=(k2,))
            if stage == 3:
                P.op("dve", lambda e, n=n, t3=t3, t2=t2, t5=t5: e.tensor_tensor_scan(
                    out=t5[:, 0:N], data0=t3[:, 0:N], data1=t2[:, 0:N], initial=hst[:, n:n + 1], op0=ALU.mult, op1=ALU.add),
                    r=(k3, k2, ("hstate", l)), w=(k5,))
            if stage == 3:
                P.op("pool", lambda e, n=n, t5=t5: e.tensor_copy(out=hst[:, n:n + 1], in_=t5[:, N - 1:N]),
                     r=(k5,), w=(("hstate", l),))
            if stage == 3:
                P.op("act", lambda e, n=n, t5=t5: e.activation(out=br[:, n, 0:N], in_=t5[:, 0:N], func=AF.Copy),
                     r=(k5,), w=("br",))
        self._lru_banks = {}
        for n0 in range(0, 8, 2):
            for stage in range(4):
                for n in (n0, n0 + 1):
                    blk(n, stage)

    def mlstm(self, l):
        P, s, N = self.P, self.s, self.N
        nblk, Lc = self.nblk, min(128, self.N)
        cB, qT, kTb, ktm, vs, og = s["cB"], s["xb"], s["kTb"], s["ktm"], s["vs"], s["og"]
        gtok, gLbc, sml = s["gtok"], s["gLbc"], s["sml"]
        ident = self.cst("ident")
        sv, sk = self.load_slab(self.d["w_in"][l][:, C_IF:C_IF + 8], 8, 8)
        bi, bf_ = self.bank(), self.bank()
        pi, pf = self.ps[bi], self.ps[bf_]
        for kc in range(8):
            P.op("pe", lambda e, kc=kc: e.matmul(pi[0:4, 0:N], lhsT=sv[:, kc, 0:4], rhs=s["uT"][:, kc, 0:N],
                                                  start=(kc == 0), stop=(kc == 7)), r=(sk, "uT"), w=(("ps", bi),))
        for kc in range(8):
            P.op("pe", lambda e, kc=kc: e.matmul(pf[0:4, 0:N], lhsT=sv[:, kc, 4:8], rhs=s["uT"][:, kc, 0:N],
                                                  start=(kc == 0), stop=(kc == 7)), r=(sk, "uT"), w=(("ps", bf_),))
        G = [self.T(i) for i in range(6)]
        (g0, k0), (g1, k1), (g2, k2), (g3, k3), (g4, k4), (g5, k5) = G
        ibias = self.vec("if_bias", l, 0, 1)
        P.op("act", lambda e: e.activation(out=g0[0:4, 0:N], in_=pi[0:4, 0:N], func=AF.Identity, bias=ibias[0:4, :]),
             r=(("ps", bi), "vecs"), w=(k0,))
        P.op("act", lambda e: e.activation(out=g1[0:4, 0:N], in_=pf[0:4, 0:N], func=AF.Exp, scale=-1.0,
                                           bias=self.der("nbf", l, 0, 1)[0:4, :]), r=(("ps", bf_), "der"), w=(k1,))
        P.op("act", lambda e: e.activation(out=g1[0:4, 0:N], in_=g1[0:4, 0:N], func=AF.Ln, bias=s["eps"][1.0][0:4, :]),
             r=(k1, "consts"), w=(k1,))
        P.op("dve", lambda e: e.tensor_scalar(out=g1[0:4, 0:N], in0=g1[0:4, 0:N], scalar1=-1.0, scalar2=None, op0=ALU.mult),
             r=(k1,), w=(k1,))
        mst = s["mstate"][l]
        P.op("dve", lambda e: e.tensor_tensor_scan(out=g2[0:4, 0:N], data0=g1[0:4, 0:N], data1=g0[0:4, 0:N],
                                                   initial=mst[0:4, 0:1], op0=ALU.add, op1=ALU.max),
             r=(k1, k0, ("mstate", l)), w=(k2,))
        P.op("pool", lambda e: e.tensor_copy(out=mst[0:4, 0:1], in_=g2[0:4, N - 1:N]), r=(k2,), w=(("mstate", l),))
        for c in range(nblk):
            P.op("dve", lambda e, c=c: e.tensor_tensor_scan(
                out=g3[0:4, c * Lc:(c + 1) * Lc], data0=s["ones_f"][0:4, 0:Lc], data1=g1[0:4, c * Lc:(c + 1) * Lc],
                initial=0.0, op0=ALU.mult, op1=ALU.add), r=(k1, "ones"), w=(k3,))
        P.op("act", lambda e: e.activation(out=g4[0:4, 0:N], in_=g3[0:4, 0:N], func=AF.Exp), r=(k3,), w=(k4,))
        P.op("dve", lambda e: e.tensor_tensor(out=g5[0:4, 0:N], in0=g0[0:4, 0:N], in1=g3[0:4, 0:N], op=ALU.subtract),
             r=(k0, k3), w=(k5,))
        P.op("act", lambda e: e.activation(out=g5[0:4, 0:N], in_=g5[0:4, 0:N], func=AF.Exp), r=(k5,), w=(k5,))
        for b in range(nblk):
            nb = min(128, N - b * 128)
            bk = self.bank()
            ps = self.ps[bk]
            P.op("pe", lambda e, b=b, nb=nb, ps=ps: e.transpose(out=ps[0:nb, 0:4], in_=g4[0:4, b * 128:b * 128 + nb],
                                                                identity=ident[0:4, 0:4]), r=(k4, "cst"), w=(("ps", bk),))
            P.op("pe", lambda e, b=b, nb=nb, ps=ps: e.transpose(out=ps[0:nb, 4:8], in_=g5[0:4, b * 128:b * 128 + nb],
                                                                identity=ident[0:4, 0:4]), r=(k5, "cst"), w=(("ps", bk),))
            P.op("act", lambda e, b=b, nb=nb, ps=ps: e.activation(out=gtok[0:nb, b, 0:8], in_=ps[0:nb, 0:8], func=AF.Copy),
                 r=(("ps", bk),), w=("gtok",))
        bk = self.bank()
        ps = self.ps[bk]
        for h in range(4):
            if nblk > 1:
                rhs = g4[0:4, Lc - 1:N:Lc]
            else:
                rhs = g4[0:4, N - 1:N]
            P.op("pe", lambda e, h=h, rhs=rhs: e.matmul(ps[:, h * nblk:(h + 1) * nblk], lhsT=self.sel(h), rhs=rhs,
                                                         start=True, stop=True), r=(k4, "cst"), w=(("ps", bk),))
        P.op("act", lambda e: e.activation(out=gLbc[:, :, 0:nblk], in_=ps[:, 0:4 * nblk].rearrange("p (h c) -> p h c", h=4),
                                           func=AF.Copy), r=(("ps", bk),), w=("gLbc",))
        for half in range(2):
            def ev(b, nb, ps, bk, half=half):
                for hh in range(2):
                    h = half * 2 + hh
                    P.op("act", lambda e, h=h, hh=hh: e.activation(out=vs[0:nb, b, h, 0:256], in_=ps[0:nb, hh * 256:(hh + 1) * 256],
                                                                  func=AF.Copy, scale=gtok[0:nb, b, 4 + h:5 + h]),
                         r=(("ps", bk), "gtok"), w=("vs",))
            self.proj_tm(l, C_V + half * 512, 512, ev)
        for b in range(nblk):
            nb = min(128, N - b * 128)
            P.op("dve", lambda e, b=b, nb=nb: e.tensor_copy(out=vs[0:nb, b, :, 256], in_=gtok[0:nb, b, 4:8]),
                 r=("gtok",), w=("vs",))
        for half in range(2):
            def ev(b, nb, ps, bk, half=half):
                P.op("act", lambda e: e.activation(out=og[0:nb, b, half * 512:(half + 1) * 512], in_=ps[0:nb, 0:512],
                                                   func=AF.Sigmoid), r=(("ps", bk),), w=("og",))
            self.proj_tm(l, C_O + half * 512, 512, ev)
        self.conv_group(l, 2)
        P.op("act", lambda e: e.activation(out=cB[:, :, 0:N], in_=cB[:, :, 0:N], func=AF.Silu), r=("cB",), w=("cB",))
        P.op("dve", lambda e: e.tensor_scalar(out=kTb[:, :, 0:N], in0=cB[:, :, 0:N], scalar1=0.0625, scalar2=None, op0=ALU.mult),
             r=("cB",), w=("kTb",))
        for b in range(nblk):
            nb = min(128, N - b * 128)
            for g in range(2):
                bk = self.bank()
                ps = self.ps[bk]
                for j in range(4):
                    fc = g * 4 + j
                    P.op("pe", lambda e, ps=ps, fc=fc, j=j, b=b, nb=nb: e.transpose(
                        out=ps[0:nb, j * 128:(j + 1) * 128], in_=cB[:, fc, b * 128:b * 128 + nb], identity=ident),
                        r=("cB", "cst"), w=(("ps", bk),))
                P.op("act", lambda e, ps=ps, g=g, b=b, nb=nb: e.activation(
                    out=ktm[0:nb, b, g * 512:(g + 1) * 512], in_=ps[0:nb, :], func=AF.Copy, scale=0.0625),
                    r=(("ps", bk),), w=("ktm",))
        self.conv_group(l, 1)
        P.op("act", lambda e: e.activation(out=qT[:, :, 0:N], in_=cB[:, :, 0:N], func=AF.Silu), r=("cB",), w=("xb",))
        Ct, Ctb = s["Ct"][l], s["Ctb"][l]
        mU1 = self.cst("mU1")
        for c in range(nblk):
            cs = slice(c * Lc, (c + 1) * Lc)
            for h in range(4):
                pi_ = h % 2
                sm, hs, tmpC, sml = s["sm2"][pi_], s["hs2"][pi_], s["tmpC2"][pi_], s["sml2"][pi_]
                SM, HS, TC, SL = ("sm", pi_), ("hs", pi_), ("tmpC", pi_), ("sml", pi_)
                bs, bo = self.bank(), self.bank()
                pS, pO = self.ps[bs], self.ps[bo]
                for dc in range(2):
                    P.op("pe", lambda e, dc=dc, h=h, cs=cs, pS=pS: e.matmul(
                        pS[0:Lc, 0:Lc], lhsT=kTb[:, 2 * h + dc, cs], rhs=qT[:, 2 * h + dc, cs],
                        start=(dc == 0), stop=(dc == 1)), r=("kTb", "xb"), w=(("ps", bs),))
                P.op("dve", lambda e, pS=pS: e.tensor_tensor(out=sm[0:Lc, 0:Lc], in0=pS[0:Lc, 0:Lc], in1=mU1[0:Lc, 0:Lc], op=ALU.mult),
                     r=(("ps", bs), "cst"), w=(SM,))
                P.op("pe", lambda e, h=h, c=c, pO=pO: e.matmul(pO[0:Lc, 0:257], lhsT=sm[0:Lc, 0:Lc], rhs=vs[0:Lc, c, h, 0:257],
                                                                 start=True, stop=False), r=(SM, "vs"), w=(("ps", bo),))
                for dc in range(2):
                    P.op("pe", lambda e, dc=dc, h=h, cs=cs, pO=pO: e.matmul(
                        pO[0:Lc, 0:257], lhsT=qT[:, 2 * h + dc, cs], rhs=Ctb[:, h, dc, 0:257],
                        start=False, stop=(dc == 1)), r=("xb", ("Ctb", l)), w=(("ps", bo),))
                rowf = gtok[0:Lc, c, h:h + 1]
                d0, d1, d2 = sml[0:Lc, 0:1], sml[0:Lc, 1:2], sml[0:Lc, 2:3]
                P.op("dve", lambda e, pO=pO, rowf=rowf: e.tensor_scalar(out=d0, in0=pO[0:Lc, 256:257], scalar1=rowf, scalar2=None,
                                                                       op0=ALU.mult), r=(("ps", bo), "gtok"), w=(SL,))
                P.op("dve", lambda e: e.tensor_scalar(out=d1, in0=d0, scalar1=-1.0, scalar2=1.0, op0=ALU.mult, op1=ALU.max), r=(SL,), w=(SL,))
                P.op("dve", lambda e: e.tensor_tensor(out=d0, in0=d0, in1=d1, op=ALU.max), r=(SL,), w=(SL,))
                P.op("dve", lambda e: e.reciprocal(out=d1, in_=d0), r=(SL,), w=(SL,))
                P.op("dve", lambda e, rowf=rowf: e.tensor_tensor(out=d2, in0=d1, in1=rowf, op=ALU.mult), r=(SL, "gtok"), w=(SL,))
                P.op("act", lambda e, pO=pO: e.activation(out=hs[0:Lc, 0:256], in_=pO[0:Lc, 0:256], func=AF.Copy, scale=d2),
                     r=(("ps", bo), SL), w=(HS,))
                st6, mv, rs = sml[0:Lc, 4:10], sml[0:Lc, 10:12], sml[0:Lc, 12:13]
                P.op("dve", lambda e: e.bn_stats(out=st6, in_=hs[0:Lc, 0:256]), r=(HS,), w=(SL,))
                P.op("dve", lambda e: e.bn_aggr(out=mv, in_=st6), r=(SL,), w=(SL,))
                P.op("act", lambda e: e.activation(out=rs, in_=sml[0:Lc, 11:12], func=AF.Sqrt, bias=s["eps"][MH_EPS][0:Lc, :]),
                     r=(SL, "consts"), w=(SL,))
                P.op("dve", lambda e: e.reciprocal(out=rs, in_=rs), r=(SL,), w=(SL,))
                P.op("dve", lambda e: e.tensor_scalar(out=hs[0:Lc, 0:256], in0=hs[0:Lc, 0:256], scalar1=sml[0:Lc, 10:11], scalar2=rs,
                                                      op0=ALU.subtract, op1=ALU.mult), r=(HS, SL), w=(HS,))
                P.op("pool", lambda e, c=c, h=h: e.tensor_tensor(out=og[0:Lc, c, h * 256:(h + 1) * 256], in0=og[0:Lc, c, h * 256:(h + 1) * 256],
                                                                in1=hs[0:Lc, 0:256], op=ALU.mult), r=(HS, "og"), w=("og",))
                for dc in range(2):
                    bc = self.bank()
                    pC = self.ps[bc]
                    P.op("pe", lambda e, dc=dc, h=h, c=c, pC=pC: e.matmul(
                        pC[:, 0:257], lhsT=ktm[0:Lc, c, (2 * h + dc) * 128:(2 * h + dc + 1) * 128], rhs=vs[0:Lc, c, h, 0:257],
                        start=True, stop=True), r=("ktm", "vs"), w=(("ps", bc),))
                    P.op("dve", lambda e, dc=dc, h=h, pC=pC: e.tensor_tensor(out=tmpC[:, :], in0=pC[:, 0:257], in1=Ct[:, h, dc, :], op=ALU.add),
                         r=(("ps", bc), ("Ct", l)), w=(TC,))
                    gl = gLbc[:, h, c:c + 1]
                    P.op("dve", lambda e, dc=dc, h=h, gl=gl: e.tensor_scalar(out=Ct[:, h, dc, :], in0=tmpC[:, :], scalar1=gl, scalar2=None, op0=ALU.mult),
                         r=(TC, "gLbc"), w=(("Ct", l),))
                    P.op("act", lambda e, dc=dc, h=h, gl=gl: e.activation(out=Ctb[:, h, dc, 0:257], in_=tmpC[:, :], func=AF.Copy, scale=gl),
                         r=(TC, "gLbc"), w=(("Ctb", l),))
        br = s["br"]
        for b in range(nblk):
            nb = min(128, N - b * 128)
            for g in range(2):
                bk = self.bank()
                ps = self.ps[bk]
                for j in range(4):
                    fc = g * 4 + j
                    P.op("pe", lambda e, ps=ps, fc=fc, j=j, b=b, nb=nb: e.transpose(
                        out=ps[:, j * 128:j * 128 + nb], in_=og[0:nb, b, fc * 128:(fc + 1) * 128], identity=ident[0:nb, 0:nb]),
                        r=("og", "cst"), w=(("ps", bk),))
                for j in range(4):
                    fc = g * 4 + j
                    P.op("act", lambda e, ps=ps, fc=fc, j=j, b=b, nb=nb: e.activation(
                        out=br[:, fc, b * 128:b * 128 + nb], in_=ps[:, j * 128:j * 128 + nb], func=AF.Copy,
                        scale=self.vec("mlstm_norm", l, fc, 1)), r=(("ps", bk), "vecs"), w=("br",))

    def shiftmix(self, z, j, idx, l, out, okey, zkey, tmp, tkey):
        P, N = self.P, self.N
        mu = self.vec("rwkv_mu", l, idx, 1)
        P.op("pool", lambda e: e.tensor_tensor(out=tmp[:, 0:N], in0=z[:, j, 0:N], in1=z[:, j, 1:N + 1], op=ALU.subtract),
             r=(zkey,), w=(tkey,))
        P.op("dve", lambda e: e.scalar_tensor_tensor(out=out, in0=tmp[:, 0:N], scalar=mu, in1=z[:, j, 1:N + 1],
                                                     op0=ALU.mult, op1=ALU.add), r=(tkey, zkey, "vecs"), w=(okey,))

    def rwkv(self, l):
        P, s, N = self.P, self.s, self.N
        Lr = min(64, N)
        nch = N // Lr
        nsq = {64: 5, 16: 3}[Lr]
        CW = 0.6065306597126334
        zA, cB, zL, sh = s["zA"], s["cB"], s["zL"], s["shist"][l]
        ident = self.cst("ident")
        identb, blkb = s["cstb"][:, 0:128], s["cstb"][:, 128:256]
        lorab, sgb = s["lorab"], s["sgb"]
        w2a2, g2 = s["w2a2"][l], s["g2"][l]
        Z, Zb = s["Z"][l], s["Zb"][l]
        P.op("pool", lambda e: e.tensor_copy(out=zL[:, :, 0], in_=sh[:, 24:26]), r=(("shist", l),), w=("zL",))
        def evl(m, ps, bk):
            P.op("act", lambda e: e.activation(out=zL[:, m, 1:1 + N], in_=ps[:, 0:N], func=AF.Copy), r=(("ps", bk),), w=("zL",))
        self.proj_fm(l, C_RW + 3072, 2, evl)
        P.op("pool", lambda e: e.tensor_copy(out=sh[:, 24:26], in_=zL[:, :, N]), r=("zL",), w=(("shist", l),))
        (t0, k0), (t1, k1) = self.T(0), self.T(1)
        (t2, k2) = self.T(2)
        self.shiftmix(zL, 0, 24, l, t0[:, 0:N], k0, "zL", t2, k2)
        self.shiftmix(zL, 1, 25, l, t1[:, 0:N], k1, "zL", t2, k2)
        P.op("act", lambda e: e.activation(out=lorab[0:64, 0:N], in_=t0[0:64, 0:N], func=AF.Tanh), r=(k0,), w=("lorab",))
        P.op("act", lambda e: e.activation(out=lorab[64:128, 0:N], in_=t0[64:128, 0:N], func=AF.Copy), r=(k0,), w=("lorab",))
        P.op("act", lambda e: e.activation(out=sgb[:, 0:N], in_=t1[:, 0:N], func=AF.Sigmoid), r=(k1,), w=("sgb",))
        atX, btX, ktX, rtX = s["atX"], s["btX"], s["ktX"], s["rtX"]
        xvg, bon, gfm, egL, Gs = s["xvg"], s["bon"], s["gfm"], s["egL"], s["Gs"]
        pmask = s["cst"][:, CST_PM:CST_PM + 2]
        if N < 64:
            for nm in ("atX", "btX", "ktX", "rtX", "xvX"):
                P.op("pool", lambda e, nm=nm: e.memset(s[nm][:], 0.0), w=(nm,))
        br = s["br"]
        for hg in range(2):
            fc0 = hg * 4
            for (buf, bkey, j0, cbase, hbase) in ((zA, "zA", 0, 0, 0), (zA, "zA", 4, 1024, 8), (cB, "cB", 0, 2048, 16)):
                P.op("pool", lambda e, buf=buf, j0=j0, hbase=hbase: e.tensor_copy(out=buf[:, j0:j0 + 4, 0], in_=sh[:, hbase + fc0:hbase + fc0 + 4]),
                     r=(("shist", l),), w=(bkey,))
                def ev(m, ps, bk, buf=buf, j0=j0, bkey=bkey):
                    P.op("act", lambda e: e.activation(out=buf[:, j0 + m, 1:1 + N], in_=ps[:, 0:N], func=AF.Copy), r=(("ps", bk),), w=(bkey,))
                self.proj_fm(l, C_RW + cbase + fc0 * 128, 4, ev)
                P.op("pool", lambda e, buf=buf, j0=j0, hbase=hbase: e.tensor_copy(out=sh[:, hbase + fc0:hbase + fc0 + 4], in_=buf[:, j0:j0 + 4, N]),
                     r=(bkey,), w=(("shist", l),))
            for j in range(4):
                fc = fc0 + j
                TT_ = [self.T(i) for i in range(12)]
                (xr, kr), (xk, kk_), (tq, kq), (sg, ksg), (G, kG), (Gr, kGr), (Gx, kGx), (eg, keg), (egi, kegi), (asg, kas), (kkn, kkk), (tz, ktz) = TT_
                self.shiftmix(zA, j, fc, l, xr[:, 0:N], kr, "zA", tq, kq)
                self.shiftmix(zA, 4 + j, 8 + fc, l, xk[:, 0:N], kk_, "zA", tq, kq)
                self.shiftmix(cB, j, 16 + fc, l, xvg[:, j, 0:N], "xvg", "cB", tq, kq)
                bk = self.bank(); ps = self.ps[bk]
                P.op("pe", lambda e, ps=ps, fc=fc: e.matmul(ps[:, 0:N], lhsT=w2a2[0:64, fc * 128:(fc + 1) * 128], rhs=lorab[0:64, 0:N],
                                                            start=True, stop=True), r=("lorab", "w2a2"), w=(("ps", bk),))
                P.op("act", lambda e, ps=ps, fc=fc: e.activation(out=sg[:, 0:N], in_=ps[:, 0:N], func=AF.Sigmoid, bias=self.vec("rwkv_w0", l, fc, 1)),
                     r=(("ps", bk), "vecs"), w=(ksg,))
                bk = self.bank(); ps = self.ps[bk]
                P.op("pe", lambda e, ps=ps, fc=fc: e.matmul(ps[:, 0:N], lhsT=w2a2[64:128, fc * 128:(fc + 1) * 128], rhs=lorab[64:128, 0:N],
                                                            start=True, stop=True), r=("lorab", "w2a2"), w=(("ps", bk),))
                P.op("act", lambda e, ps=ps, fc=fc: e.activation(out=asg[:, 0:N], in_=ps[:, 0:N], func=AF.Sigmoid, bias=self.vec("rwkv_a0", l, fc, 1)),
                     r=(("ps", bk), "vecs"), w=(kas,))
                P.op("dve", lambda e: e.tensor_tensor_scan(out=G[:, 0:N], data0=s["ones_f"][:, 0:N], data1=sg[:, 0:N], initial=0.0,
                                                           op0=ALU.mult, op1=ALU.add), r=(ksg, "ones"), w=(kG,))
                P.op("pool", lambda e: e.memset(Gs[:, 0:1], 0.0), w=("Gs",))
                if nch > 1:
                    P.op("pool", lambda e: e.tensor_copy(out=Gs[:, 1:nch], in_=G[:, Lr - 1:N - 1:Lr]), r=(kG,), w=("Gs",))
                P.op("dve", lambda e: e.tensor_tensor(out=Gr[:, 0:N].rearrange("p (c t) -> p c t", c=nch),
                                                      in0=G[:, 0:N].rearrange("p (c t) -> p c t", c=nch),
                                                      in1=Gs[:, 0:nch].unsqueeze(2).to_broadcast([128, nch, Lr]), op=ALU.subtract),
                     r=(kG, "Gs"), w=(kGr,))
                P.op("pool", lambda e: e.tensor_tensor(out=Gx[:, 0:N], in0=Gr[:, 0:N], in1=sg[:, 0:N], op=ALU.subtract), r=(kGr, ksg), w=(kGx,))
                P.op("act", lambda e: e.activation(out=eg[:, 0:N], in_=Gr[:, 0:N], func=AF.Exp, scale=-CW), r=(kGr,), w=(keg,))
                P.op("act", lambda e: e.activation(out=egi[:, 0:N], in_=Gr[:, 0:N], func=AF.Exp, scale=CW), r=(kGr,), w=(kegi,))
                P.op("act", lambda e: e.activation(out=Gx[:, 0:N], in_=Gx[:, 0:N], func=AF.Exp, scale=-CW), r=(kGx,), w=(kGx,))
                if nch > 1:
                    P.op("pool", lambda e, j=j: e.tensor_copy(out=egL[:, j, 0:nch], in_=eg[:, Lr - 1:N:Lr]), r=(keg,), w=("egL",))
                else:
                    P.op("pool", lambda e, j=j: e.tensor_copy(out=egL[:, j, 0:1], in_=eg[:, N - 1:N]), r=(keg,), w=("egL",))
                tb0, tb1 = s["tb"]
                P.op("dve", lambda e, fc=fc: e.tensor_scalar(out=kkn[:, 0:N], in0=xk[:, 0:N], scalar1=self.vec("rwkv_k_k", l, fc, 1), scalar2=None, op0=ALU.mult),
                     r=(kk_, "vecs"), w=(kkk,))
                P.op("act", lambda e: e.activation(out=tb0[:, 0:N], in_=kkn[:, 0:N], func=AF.Square), r=(kkk,), w=(("tb", 0),))
                bk = self.bank(); ps = self.ps[bk]
                P.op("pe", lambda e, ps=ps: e.matmul(ps[:, 0:N], lhsT=blkb, rhs=tb0[:, 0:N], start=True, stop=True), r=(("tb", 0), "cstb"), w=(("ps", bk),))
                P.op("dve", lambda e, ps=ps: e.tensor_scalar(out=tz[:, 0:N], in0=ps[:, 0:N], scalar1=1e-24, scalar2=None, op0=ALU.max), r=(("ps", bk),), w=(ktz,))
                P.op("act", lambda e: e.activation(out=tz[:, 0:N], in_=tz[:, 0:N], func=AF.Sqrt), r=(ktz,), w=(ktz,))
                P.op("dve", lambda e: e.reciprocal(out=tz[:, 0:N], in_=tz[:, 0:N]), r=(ktz,), w=(ktz,))
                P.op("dve", lambda e: e.tensor_tensor(out=kkn[:, 0:N], in0=kkn[:, 0:N], in1=tz[:, 0:N], op=ALU.mult), r=(kkk, ktz), w=(kkk,))
                P.op("dve", lambda e, fc=fc: e.tensor_scalar(out=tz[:, 0:N], in0=asg[:, 0:N], scalar1=self.vec("rwkv_k_a", l, fc, 1),
                                                             scalar2=self.der("omka", l, fc, 1), op0=ALU.mult, op1=ALU.add), r=(kas, "vecs", "der"), w=(ktz,))
                P.op("dve", lambda e: e.tensor_tensor(out=xk[:, 0:N], in0=xk[:, 0:N], in1=tz[:, 0:N], op=ALU.mult), r=(kk_, ktz), w=(kk_,))
                def expand(dstX, dkey, src_key):
                    tbx = tb0
                    P.op("pool", lambda e: e.tensor_tensor(
                        out=dstX[:, j, 0:nch, :, 0:Lr],
                        in0=tbx[:, 0:N].rearrange("p (c t) -> p c t", c=nch).unsqueeze(2).to_broadcast([128, nch, 2, Lr]),
                        in1=pmask.unsqueeze(1).unsqueeze(3).to_broadcast([128, nch, 2, Lr]), op=ALU.mult),
                        r=(("tb", 0), "cst"), w=(dkey,))
                P.op("dve", lambda e: e.tensor_tensor(out=tb0[:, 0:N], in0=xr[:, 0:N], in1=eg[:, 0:N], op=ALU.mult), r=(kr, keg), w=(("tb", 0),))
                expand(rtX, "rtX", None)
                P.op("dve", lambda e: e.tensor_tensor(out=tb0[:, 0:N], in0=xk[:, 0:N], in1=egi[:, 0:N], op=ALU.mult), r=(kk_, kegi), w=(("tb", 0),))
                expand(ktX, "ktX", None)
                P.op("dve", lambda e: e.scalar_tensor_tensor(out=tb0[:, 0:N], in0=kkn[:, 0:N], scalar=-1.0, in1=Gx[:, 0:N], op0=ALU.mult, op1=ALU.mult),
                     r=(kkk, kGx), w=(("tb", 0),))
                expand(atX, "atX", None)
                P.op("pool", lambda e: e.tensor_tensor(out=asg[:, 0:N], in0=asg[:, 0:N], in1=egi[:, 0:N], op=ALU.mult), r=(kas, kegi), w=(kas,))
                P.op("dve", lambda e: e.tensor_tensor(out=tb0[:, 0:N], in0=kkn[:, 0:N], in1=asg[:, 0:N], op=ALU.mult), r=(kkk, kas), w=(("tb", 0),))
                expand(btX, "btX", None)
                P.op("dve", lambda e, fc=fc: e.scalar_tensor_tensor(out=tb1[:, 0:N], in0=xr[:, 0:N], scalar=self.vec("rwkv_r_k", l, fc, 1), in1=xk[:, 0:N],
                                                                    op0=ALU.mult, op1=ALU.mult), r=(kr, kk_, "vecs"), w=(("tb", 1),))
                bk = self.bank(); ps = self.ps[bk]
                P.op("pe", lambda e, ps=ps: e.matmul(ps[:, 0:N], lhsT=blkb, rhs=tb1[:, 0:N], start=True, stop=True), r=(("tb", 1), "cstb"), w=(("ps", bk),))
                P.op("dve", lambda e, ps=ps, j=j: e.tensor_tensor(out=bon[:, j, 0:N], in0=ps[:, 0:N], in1=xvg[:, j, 0:N], op=ALU.mult),
                     r=(("ps", bk), "xvg"), w=("bon",))
                bk = self.bank(); ps = self.ps[bk]
                P.op("pe", lambda e, ps=ps, fc=fc: e.matmul(ps[:, 0:N], lhsT=g2[:, fc * 128:(fc + 1) * 128], rhs=sgb[:, 0:N], start=True, stop=True),
                     r=("sgb", "g2"), w=(("ps", bk),))
                P.op("act", lambda e, ps=ps, j=j: e.activation(out=gfm[:, j, 0:N], in_=ps[:, 0:N], func=AF.Copy), r=(("ps", bk),), w=("gfm",))
            P0, Q0, P1, Q1, TTm, Wb, Yf, Ysq, TTb, Yt = (s[n] for n in ("P0", "Q0", "P1", "Q1", "TT", "Wb", "Yf", "Ysq", "TTb", "Yt"))
            AakT, ArbT, ArkT, Ub = s["AakT"], s["ArbT"], s["ArkT"], s["Ub"]
            atX, btX, ktX, rtX = s["atX"], s["btX"], s["ktX"], s["rtX"]
            vtmX, ktTX, btTX, xvX = s["vtmX"], s["ktTX"], s["btTX"], s["xvX"]
            bdU0, bdU1, bdL0 = self.cstn(CST_BD + 0), self.cstn(CST_BD + 128), self.cstn(CST_BD + 256)
            def v4(buf):
                return buf[:, 0:512].rearrange("p (j c) -> p j c", j=4)
            def bc4(m):
                return m.unsqueeze(1).to_broadcast([128, 4, 128])
            def halves(buf):
                v = v4(buf)
                return v[:, :, 0:64], v[:, :, 64:128]
            for ch in range(nch):
                cs = slice(ch * Lr, (ch + 1) * Lr)
                P.op("pool", lambda e: e.tensor_tensor(out=xvX[:, :, :, 0:Lr], in0=xvg[:, :, cs].unsqueeze(2).to_broadcast([128, 4, 2, Lr]),
                                                       in1=pmask.unsqueeze(1).unsqueeze(3).to_broadcast([128, 4, 2, Lr]), op=ALU.mult),
                     r=("xvg", "cst"), w=("xvX",))
                bk = self.bank(); ps = self.ps[bk]
                for j in range(4):
                    P.op("pe", lambda e, j=j: e.transpose(out=ps[:, j * 128:(j + 1) * 128], in_=xvX[:, j, :, :].rearrange("p a b -> p (a b)"), identity=ident),
                         r=("xvX", "cst"), w=(("ps", bk),))
                P.op("act", lambda e: e.activation(out=vtmX[:, ch, :], in_=ps[:, :], func=AF.Copy), r=(("ps", bk),), w=("vtmX",))
                for (src, skey, dst, dkey) in ((ktX, "ktX", ktTX, "ktTX"), (btX, "btX", btTX, "btTX")):
                    bk = self.bank(); psb = self.ps[bk][:, :].bitcast(BF16)
                    for j in range(4):
                        P.op("pe", lambda e, j=j: e.transpose(out=psb[:, j * 128:(j + 1) * 128], in_=src[:, j, ch, :, :].rearrange("p a b -> p (a b)"), identity=identb),
                             r=(skey, "cstb"), w=(("ps", bk),))
                    P.op("dve", lambda e: e.tensor_copy(out=dst[:, ch, :], in_=psb[:, 0:512]), r=(("ps", bk),), w=(dkey,))
                def amat(lh, lk, rh, rk, dst, dkey, mask):
                    bk = self.bank(); ps = self.ps[bk]
                    for j in range(4):
                        P.op("pe", lambda e, j=j: e.matmul(ps[:, j * 128:(j + 1) * 128], lhsT=lh[:, j, ch, :, :].rearrange("p a b -> p (a b)"),
                                                           rhs=rh[:, j, ch, :, :].rearrange("p a b -> p (a b)"), start=True, stop=True),
                             r=(lk, rk), w=(("ps", bk),))
                    P.op("dve", lambda e: e.tensor_tensor(out=v4(dst), in0=v4(ps), in1=bc4(mask), op=ALU.mult), r=(("ps", bk), "cst"), w=(dkey,))
                amat(btX, "btX", atX, "atX", P0, "P0", bdU0)
                amat(atX, "atX", btX, "btX", Q0, "Q0", bdL0)
                amat(ktX, "ktX", atX, "atX", AakT, "AakT", bdU0)
                amat(btX, "btX", rtX, "rtX", ArbT, "ArbT", bdU1)
                amat(ktX, "ktX", rtX, "rtX", ArkT, "ArkT", bdU1)
                P.op("pool", lambda e: e.tensor_tensor(out=v4(TTm), in0=v4(P0), in1=bc4(ident), op=ALU.add), r=("P0", "cst"), w=("TT",))
                P.op("act", lambda e: e.activation(out=TTb[:, 0:512], in_=TTm[:, 0:512], func=AF.Copy), r=("TT",), w=("TTb",))
                Pc, Pk, Qc, Qk = P0, "P0", Q0, "Q0"
                Pn, Pnk, Qn, Qnk = P1, "P1", Q1, "Q1"
                for it in range(nsq):
                    b1, b2 = self.bank(), self.bank()
                    p1, p2 = self.ps[b1], self.ps[b2]
                    for j in range(4):
                        c_ = slice(j * 128, (j + 1) * 128)
                        P.op("pe", lambda e: e.matmul(p1[:, c_], lhsT=Qc[:, c_], rhs=Pc[:, c_], start=True, stop=True), r=(Pk, Qk), w=(("ps", b1),))
                    for j in range(4):
                        c_ = slice(j * 128, (j + 1) * 128)
                        P.op("pe", lambda e: e.matmul(p2[:, c_], lhsT=Pc[:, c_], rhs=Qc[:, c_], start=True, stop=True), r=(Pk, Qk), w=(("ps", b2),))
                    P.op("act", lambda e: e.activation(out=Pn[:, 0:512], in_=p1[:, :], func=AF.Copy), r=(("ps", b1),), w=(Pnk,))
                    P.op("dve", lambda e: e.tensor_copy(out=Qn[:, 0:512], in_=p2[:, :]), r=(("ps", b2),), w=(Qnk,))
                    b3 = self.bank(); p3 = self.ps[b3]
                    for j in range(4):
                        c_ = slice(j * 128, (j + 1) * 128)
                        P.op("pe", lambda e: e.matmul(p3[:, c_], lhsT=Qn[:, c_], rhs=TTb[:, c_], start=True, stop=True), r=(Qnk, "TTb"), w=(("ps", b3),))
                    P.op("dve", lambda e: e.tensor_tensor(out=TTm[:, 0:512], in0=p3[:, :], in1=TTm[:, 0:512], op=ALU.add), r=(("ps", b3), "TT"), w=("TT",))
                    P.op("act", lambda e: e.activation(out=TTb[:, 0:512], in_=TTm[:, 0:512], func=AF.Copy), r=("TT",), w=("TTb",))
                    Pc, Pk, Qc, Qk, Pn, Pnk, Qn, Qnk = Pn, Pnk, Qn, Qnk, Pc, Pk, Qc, Qk
                def zx(j):
                    return Zb[:, :, fc0 + j, :]
                bw = self.bank(); pw = self.ps[bw]
                for j in range(4):
                    c_ = slice(j * 128, (j + 1) * 128)
                    zj = zx(j)
                    P.op("pe", lambda e, j=j: e.matmul(pw[:, c_], lhsT=atX[:, j, ch, :, :].rearrange("p a b -> p (a b)"), rhs=zj, start=True, stop=False),
                         r=("atX", ("Zb", l)), w=(("ps", bw),))
                    P.op("pe", lambda e, j=j: e.matmul(pw[:, c_], lhsT=AakT[:, c_], rhs=vtmX[:, ch, c_], start=False, stop=True),
                         r=("AakT", "vtmX"), w=(("ps", bw),))
                P.op("act", lambda e: e.activation(out=Wb[:, 0:512], in_=pw[:, :], func=AF.Copy), r=(("ps", bw),), w=("Wb",))
                bu = self.bank(); pu = self.ps[bu]
                for j in range(4):
                    c_ = slice(j * 128, (j + 1) * 128)
                    P.op("pe", lambda e: e.matmul(pu[:, c_], lhsT=TTb[:, c_], rhs=Wb[:, c_], start=True, stop=True), r=("TTb", "Wb"), w=(("ps", bu),))
                P.op("act", lambda e: e.activation(out=Ub[:, 0:512], in_=pu[:, :], func=AF.Copy), r=(("ps", bu),), w=("Ub",))
                by = self.bank(); py = self.ps[by]
                for j in range(4):
                    c_ = slice(j * 128, (j + 1) * 128)
                    zj = zx(j)
                    P.op("pe", lambda e, j=j: e.matmul(py[:, c_], lhsT=rtX[:, j, ch, :, :].rearrange("p a b -> p (a b)"), rhs=zj, start=True, stop=False),
                         r=("rtX", ("Zb", l)), w=(("ps", by),))
                    P.op("pe", lambda e: e.matmul(py[:, c_], lhsT=ArbT[:, c_], rhs=Ub[:, c_], start=False, stop=False), r=("ArbT", "Ub"), w=(("ps", by),))
                    P.op("pe", lambda e: e.matmul(py[:, c_], lhsT=ArkT[:, c_], rhs=vtmX[:, ch, c_], start=False, stop=True), r=("ArkT", "vtmX"), w=(("ps", by),))
                st = s["st"]
                s1, s2, mn, vr = st[:, 0:4], st[:, 4:8], st[:, 8:12], st[:, 12:16]
                Ys = Ysq[:, 0:256].rearrange("p (j c) -> p j c", j=4)
                Yq = Ysq[:, 256:512].rearrange("p (j c) -> p j c", j=4)
                P.op("act", lambda e: e.activation(out=Yf[:, 0:512], in_=py[:, :], func=AF.Copy), r=(("ps", by),), w=("Yf",))
                yl, yr_ = halves(Yf)
                P.op("dve", lambda e: e.tensor_tensor(out=Ys, in0=yl, in1=yr_, op=ALU.add), r=("Yf",), w=("Ys",))
                P.op("act", lambda e: e.activation(out=Yq, in_=Ys, func=AF.Square), r=("Ys",), w=("Yq",))
                P.op("dve", lambda e: e.tensor_reduce(out=s1, in_=Ys, axis=AX.X, op=ALU.add), r=("Ys",), w=("st",))
                P.op("dve", lambda e: e.tensor_reduce(out=s2, in_=Yq, axis=AX.X, op=ALU.add), r=("Yq",), w=("st",))
                P.op("dve", lambda e: e.tensor_scalar(out=mn, in0=s1, scalar1=1.0 / 64, scalar2=None, op0=ALU.mult), r=("st",), w=("st",))
                P.op("dve", lambda e: e.tensor_tensor(out=s1, in0=mn, in1=mn, op=ALU.mult), r=("st",), w=("st",))
                P.op("dve", lambda e: e.scalar_tensor_tensor(out=vr, in0=s2, scalar=1.0 / 64, in1=s1, op0=ALU.mult, op1=ALU.subtract), r=("st",), w=("st",))
                P.op("act", lambda e: e.activation(out=vr, in_=vr, func=AF.Sqrt, bias=s["eps"][GN_EPS]), r=("st", "consts"), w=("st",))
                P.op("dve", lambda e: e.reciprocal(out=vr, in_=vr), r=("st",), w=("st",))
                P.op("dve", lambda e: e.tensor_tensor(out=Ys, in0=Ys, in1=mn.unsqueeze(2).to_broadcast([128, 4, 64]), op=ALU.subtract), r=("Ys", "st"), w=("Ys",))
                P.op("dve", lambda e: e.tensor_tensor(out=Ys, in0=Ys, in1=vr.unsqueeze(2).to_broadcast([128, 4, 64]), op=ALU.mult), r=("Ys", "st"), w=("Ys",))
                Yx = Yf[:, 0:512].rearrange("p (j a b) -> p j a b", j=4, a=2)
                P.op("pool", lambda e: e.tensor_tensor(out=Yx, in0=Ys.unsqueeze(2).to_broadcast([128, 4, 2, 64]),
                                                       in1=pmask.unsqueeze(1).unsqueeze(3).to_broadcast([128, 4, 2, 64]), op=ALU.mult),
                     r=("Ys", "cst"), w=("Yf",))
                bt_ = self.bank(); pt = self.ps[bt_]
                for j in range(4):
                    c_ = slice(j * 128, (j + 1) * 128)
                    P.op("pe", lambda e: e.transpose(out=pt[:, c_], in_=Yf[:, c_], identity=ident), r=("Yf", "cst"), w=(("ps", bt_),))
                P.op("act", lambda e: e.activation(out=Yt[:, 0:512], in_=pt[:, :], func=AF.Copy), r=(("ps", bt_),), w=("Yt",))
                ot = s["ot"]
                wl, wr = halves(Yt)
                P.op("dve", lambda e: e.tensor_tensor(out=ot[:, :, :], in0=wl, in1=wr, op=ALU.add), r=("Yt",), w=("ot",))
                for j in range(4):
                    fc = fc0 + j
                    P.op("dve", lambda e, j=j, fc=fc: e.tensor_scalar(out=ot[:, j, :], in0=ot[:, j, :], scalar1=self.vec("rwkv_ln_w", l, fc, 1),
                                                                    scalar2=self.vec("rwkv_ln_b", l, fc, 1), op0=ALU.mult, op1=ALU.add),
                         r=("ot", "vecs"), w=("ot",))
                P.op("pool", lambda e: e.tensor_tensor(out=ot[:, :, 0:Lr], in0=ot[:, :, 0:Lr], in1=bon[:, :, cs], op=ALU.add), r=("ot", "bon"), w=("ot",))
                P.op("dve", lambda e: e.tensor_tensor(out=br[:, fc0:fc0 + 4, cs], in0=ot[:, :, 0:Lr], in1=gfm[:, :, cs], op=ALU.mult), r=("ot", "gfm"), w=("br",))
                bz = self.bank(); pz = self.ps[bz]
                for j in range(4):
                    c_ = slice(j * 128, (j + 1) * 128)
                    P.op("pe", lambda e: e.matmul(pz[:, c_], lhsT=btTX[:, ch, c_], rhs=Ub[:, c_], start=True, stop=False), r=("btTX", "Ub"), w=(("ps", bz),))
                    P.op("pe", lambda e: e.matmul(pz[:, c_], lhsT=ktTX[:, ch, c_], rhs=vtmX[:, ch, c_], start=False, stop=True), r=("ktTX", "vtmX"), w=(("ps", bz),))
                tmpZ = s["tmpZ"]
                P.op("act", lambda e: e.activation(out=Yf[:, 0:512], in_=pz[:, :], func=AF.Copy), r=(("ps", bz),), w=("Yf",))
                zl, zr = halves(Yf)
                P.op("dve", lambda e: e.tensor_tensor(out=tmpZ[:, :, :], in0=zl, in1=zr, op=ALU.add), r=("Yf",), w=("tmpZ",))
                P.op("dve", lambda e: e.tensor_tensor(out=tmpZ[:, :, :], in0=tmpZ[:, :, :], in1=Z[:, fc0:fc0 + 4, :], op=ALU.add), r=("tmpZ", ("Z", l)), w=("tmpZ",))
                P.op("dve", lambda e: e.tensor_tensor(out=Z[:, fc0:fc0 + 4, :], in0=tmpZ[:, :, :], in1=egL[:, :, ch:ch + 1].to_broadcast([128, 4, 64]), op=ALU.mult),
                     r=("tmpZ", "egL"), w=(("Z", l),))
                for par in range(2):
                    P.op("act", lambda e, par=par: e.activation(out=Zb[:, par, fc0:fc0 + 4, :], in_=Z[:, fc0:fc0 + 4, :], func=AF.Copy,
                                                                scale=s["cst"][:, 512 + 64 * par:513 + 64 * par]), r=(("Z", l), "cst"), w=(("Zb", l),))

    def mixer(self, l, N):
        P = self.P
        self.N = N
        self.nblk = max(1, N // 128)
        self.rmsnorm("mix_norm", l, N)
        parts = getattr(self, "parts", ("lru", "mlstm", "rwkv"))
        P.op("pool", lambda e: e.memset(self.s["macc"][:], 0.0), w=("macc",))
        if "lru" in parts:
            self.lru(l)
            self.merge_branch(l, 0)
        if "mlstm" in parts:
            self.mlstm(l)
            self.merge_branch(l, 1)
        P.barrier()
        if "rwkv" in parts:
            self.rwkv(l)
            self.merge_branch(l, 2)
        self.merge_out(l)
        P.barrier()

    def convert_weights(self):
        P = self.P
        self.wkeys = {}
        pat = {2: None, 3: "a r c -> (a r) c", 4: "a b r c -> (a b r) c"}
        for nm in ("ffn1_w_gate", "ffn1_w_up", "ffn1_w_down", "w_in", "lru_wa", "lru_wx", "w_branch", "w_out",
                   "ffn2_w_gate", "ffn2_w_up", "ffn2_w_down"):
            src, dst = self.d32[nm], self.d[nm]
            p = pat[len(src.shape)]
            s2, d2 = (src.rearrange(p), dst.rearrange(p)) if p else (src, dst)
            rows = s2.shape[0]
            keys = []
            for r0 in range(0, rows, 128):
                k = ("wbf", nm, r0)
                P.dma("pool", lambda e, s2=s2, d2=d2, r0=r0: e.dma_start(out=d2[r0:r0 + 128, :], in_=s2[r0:r0 + 128, :]), w=(k,))
                keys.append(k)
            self.wkeys[dst.tensor.name] = keys

    def setup_mixer(self):
        P, s, d = self.P, self.s, self.d
        P.dma("sp", lambda e: e.dma_start(out=s["cst"][:], in_=d["cst"][:, :]), w=("cst",))
        P.op("dve", lambda e: e.memset(s["ones_f"][:], 1.0), w=("ones",))
        P.op("act", lambda e: e.activation(out=s["cstb"][:, 0:128], in_=s["cst"][:, 0:128], func=AF.Copy), r=("cst",), w=("cstb",))
        P.op("act", lambda e: e.activation(out=s["cstb"][:, 128:256], in_=s["cst"][:, 512:640], func=AF.Copy), r=("cst",), w=("cstb",))
        for l in range(DEPTH):
            P.dma("pool", lambda e, l=l: e.dma_start(out=s["w2a2"][l][:], in_=d["w2a2"][l]), w=("w2a2",))
            P.dma("pool", lambda e, l=l: e.dma_start(out=s["g2"][l][:], in_=d["g2"][l]), w=("g2",))
            lam = self.vec("lru_lambda", l, 0, 8)
            cl, cl2 = self.der("cl", l, 0, 8), self.der("cl2", l, 0, 8)
            P.op("act", lambda e, lam=lam, cl=cl: e.activation(out=cl, in_=lam, func=AF.Exp, scale=-1.0), r=("vecs",), w=("der",))
            P.op("act", lambda e, cl=cl: e.activation(out=cl, in_=cl, func=AF.Ln, bias=s["eps"][1.0]), r=("der", "consts"), w=("der",))
            P.op("dve", lambda e, cl=cl, cl2=cl2: e.tensor_scalar(out=cl2, in0=cl, scalar1=-16.0, scalar2=None, op0=ALU.mult), r=("der",), w=("der",))
            P.op("dve", lambda e, cl=cl: e.tensor_scalar(out=cl, in0=cl, scalar1=-8.0, scalar2=None, op0=ALU.mult), r=("der",), w=("der",))
            ka = self.vec("rwkv_k_a", l, 0, 8)
            P.op("dve", lambda e, ka=ka, l=l: e.tensor_scalar(out=self.der("omka", l, 0, 8), in0=ka, scalar1=-1.0, scalar2=1.0, op0=ALU.mult, op1=ALU.add),
                 r=("vecs",), w=("der",))
            fb = self.vec("if_bias", l, 1, 1)
            P.op("dve", lambda e, fb=fb, l=l: e.tensor_scalar(out=self.der("nbf", l, 0, 1), in0=fb, scalar1=-1.0, scalar2=None, op0=ALU.mult),
                 r=("vecs",), w=("der",))

    def init_states(self, kind, i):
        P, s, d = self.P, self.s, self.d
        P.barrier()
        for l in range(DEPTH):
            if kind == "p":
                for nm in ("chist", "hstate", "mstate", "Ct", "Ctb", "shist", "Z", "Zb"):
                    t = s[nm][l]
                    P.op("pool", lambda e, t=t: e.memset(t[:], 0.0), w=((nm, l),))
            else:
                P.dma("sp", lambda e, l=l: e.dma_start(out=s["chist"][l][:].rearrange("p a b -> p (a b)"), in_=d["s_conv"][l, i]), w=(("chist", l),))
                P.dma("sp", lambda e, l=l: e.dma_start(out=s["hstate"][l][:], in_=d["s_lru"][l, i]), w=(("hstate", l),))
                P.dma("sp", lambda e, l=l: e.dma_start(out=s["mstate"][l][0:4, :], in_=d["s_m"][l, i]), w=(("mstate", l),))
                P.dma("sp", lambda e, l=l: e.dma_start(out=s["shist"][l][:], in_=d["s_shift"][l, i]), w=(("shist", l),))
                P.dma("sp", lambda e, l=l: e.dma_start(out=s["Z"][l][:].rearrange("p a b -> p (a b)"), in_=d["s_S"][l, i]), w=(("Z", l),))
                P.dma("sp", lambda e, l=l: e.dma_start(out=s["Ct"][l][:].rearrange("p a b c -> p (a b c)"), in_=d["s_C"][l, i]), w=(("Ct", l),))
                em = s["sml"][:, 16:20]
                P.dma("sp", lambda e, l=l, em=em: e.dma_start(out=em, in_=d["s_mbc"][l, i]), w=("sml",))
                P.op("act", lambda e, em=em: e.activation(out=em, in_=em, func=AF.Exp), r=("sml",), w=("sml",))
                for h in range(4):
                    v = s["Ct"][l][:, h].rearrange("p a b -> p (a b)")
                    P.op("dve", lambda e, v=v, h=h, em=em: e.tensor_scalar(out=v, in0=v, scalar1=em[:, h:h + 1], scalar2=None, op0=ALU.mult),
                         r=("sml", ("Ct", l)), w=(("Ct", l),))
                P.op("act", lambda e, l=l: e.activation(out=s["Ctb"][l][:, :, :, 0:257], in_=s["Ct"][l][:, :, :, :], func=AF.Copy), r=(("Ct", l),), w=(("Ctb", l),))
                for par in range(2):
                    P.op("act", lambda e, l=l, par=par: e.activation(out=s["Zb"][l][:, par], in_=s["Z"][l][:], func=AF.Copy,
                                                                     scale=s["cst"][:, 512 + 64 * par:513 + 64 * par]), r=(("Z", l), "cst"), w=(("Zb", l),))
        P.barrier()

    def out_states(self, q):
        P, s, o = self.P, self.s, self.o
        P.barrier()
        for l in range(DEPTH):
            P.dma("sp", lambda e, l=l: e.dma_start(out=o["o_conv"][l, q], in_=s["chist"][l][:].rearrange("p a b -> p (a b)")), r=(("chist", l),))
            P.dma("sp", lambda e, l=l: e.dma_start(out=o["o_lru"][l, q], in_=s["hstate"][l][:]), r=(("hstate", l),))
            P.dma("sp", lambda e, l=l: e.dma_start(out=o["o_m"][l, q], in_=s["mstate"][l][0:4, :]), r=(("mstate", l),))
            P.dma("sp", lambda e, l=l: e.dma_start(out=o["o_shift"][l, q], in_=s["shist"][l][:]), r=(("shist", l),))
            P.dma("sp", lambda e, l=l: e.dma_start(out=o["o_S"][l, q], in_=s["Z"][l][:].rearrange("p a b -> p (a b)")), r=(("Z", l),))
            bk = self.bank(); ps = self.ps[bk]
            for h in range(4):
                P.op("pe", lambda e, h=h, l=l: e.matmul(ps[:, h:h + 1], lhsT=self.sel(h), rhs=s["mstate"][l][0:4, 0:1], start=True, stop=True),
                     r=(("mstate", l), "cst"), w=(("ps", bk),))
            em = s["sml"][:, 20:24]
            P.op("act", lambda e, em=em: e.activation(out=em, in_=ps[:, 0:4], func=AF.Exp, scale=-1.0), r=(("ps", bk),), w=("sml",))
            stC = s["stC"]
            for h in range(4):
                v = s["Ct"][l][:, h].rearrange("p a b -> p (a b)")
                P.op("dve", lambda e, v=v, h=h, em=em: e.tensor_scalar(out=stC[:, h * 514:(h + 1) * 514], in0=v, scalar1=em[:, h:h + 1], scalar2=None, op0=ALU.mult),
                     r=("sml", ("Ct", l)), w=("stC",))
            P.dma("sp", lambda e, l=l: e.dma_start(out=o["o_C"][l, q], in_=stC[:, 0:2056]), r=("stC",))
            P.barrier()


    def build(self):
        self.declare()
        with self.es:
            self.alloc()
            s = self.s
            s["eps"] = {}
            for v in sorted(set((RMS_EPS, MH_EPS, GN_EPS, 1.0))):
                t = self.sb("eps_%g" % v, [128, 1])
                s["eps"][v] = t[:, 0:1]
                self.P.op("dve", lambda e, t=t, v=v: e.memset(t[:], v), w=("consts",))
            self.setup()
            self.convert_weights()
            self.setup_mixer()
            xi = 0
            tiles = [("p", i) for i in range(self.NPT)] + [("s", i) for i in range(self.NSMP)]
            for kind, i in tiles:
                if kind == "p":
                    N = self.NP
                    src = self.d["xp"][i * N:(i + 1) * N, :]
                    dst = self.o["yp"][i * N:(i + 1) * N, :]
                else:
                    N = self.NS
                    src = self.d["xs"][i * N:(i + 1) * N, :]
                    dst = self.o["ys"][i * N:(i + 1) * N, :]
                self.N = N
                if self.do_mix and (kind == "s" or i == 0):
                    self.init_states(kind, i)
                xi = self.load_x(src, N, xi)
                for l in range(DEPTH):
                    self.ffn("ffn1", l, N)
                    if self.do_mix:
                        self.mixer(l, N)
                    self.ffn("ffn2", l, N)
                xi = self.store_y(dst, N, xi)
                if self.do_mix and (kind == "s" or i == self.NPT - 1):
                    self.out_states(0 if kind == "p" else 1 + i)
            self.P.emit()
        return self.nc


def pack_vecs(inp):
    v = np.zeros((128, VEC_N), np.float32)
    def put(name, l, arr):
        off, n = VEC_LAY[(name, l)]
        v[:, off:off + n] = arr
    def fm(a):
        return np.ascontiguousarray(a.reshape(-1, 128).T)
    for l in range(DEPTH):
        for nm in ("ffn1_norm", "mix_norm", "ffn2_norm", "lru_ba", "lru_bx", "lru_lambda",
                   "rwkv_w0", "rwkv_a0", "rwkv_k_k", "rwkv_k_a", "rwkv_ln_w", "rwkv_ln_b", "mlstm_norm"):
            put(nm, l, fm(inp[nm][l]))
        put("rwkv_r_k", l, fm(inp["rwkv_r_k"][l].reshape(-1)))
        cw = inp["conv_w"][l]
        put("conv_w", l, np.concatenate([fm(cw[j]) for j in range(4)], axis=1))
        put("conv_b", l, fm(inp["conv_b"][l]))
        put("rwkv_mu", l, fm(inp["rwkv_mu"][l]))
        ib = np.zeros((128, 2), np.float32)
        ib[0:4, 0] = inp["mlstm_if_bias"][l][0:4]
        ib[0:4, 1] = inp["mlstm_if_bias"][l][4:8]
        put("if_bias", l, ib)
    put("final_norm", 0, fm(inp["final_norm"]))
    return v


def make_consts():
    c = np.zeros((128, CST_N), np.float32)
    c[:, 0:128] = np.eye(128)
    i = np.arange(128)
    c[:, 128:256] = (i[:, None] <= i[None, :])
    c[:, 256:384] = (i[:, None] < i[None, :])
    c[:, 384:512] = (i[None, :] < i[:, None])
    c[0:64, 512:576] = 1.0
    c[64:128, 576:640] = 1.0
    for h in range(4):
        c[h, 640 + h * 128:640 + (h + 1) * 128] = 1.0
    for k, src in enumerate((256, 128, 384)):
        blk = c[0:64, src:src + 64]
        o = CST_BD + 128 * k
        c[0:64, o:o + 64] = blk
        c[64:128, o + 64:o + 128] = blk
    c[0:64, CST_PM] = 1.0
    c[64:128, CST_PM + 1] = 1.0
    return c


WNAMES = ("ffn1_w_gate", "ffn1_w_up", "ffn2_w_gate", "ffn2_w_up", "ffn1_w_down", "ffn2_w_down", "w_in", "w_branch", "w_out")


def make_in_map(inp, xp, sample_ids):
    ns = len(sample_ids)
    m = {"xp": np.ascontiguousarray(xp),
         "xs": np.ascontiguousarray(inp["x_sample"][sample_ids].reshape(ns * 16, D)),
         "vecs": pack_vecs(inp), "cst": make_consts(), "ident": np.eye(128, dtype=np.float32)}
    for nm in WNAMES:
        m[nm] = inp[nm]
    m["lru_wa"] = np.ascontiguousarray(inp["lru_wa"].reshape(DEPTH, 1024, 128))
    m["lru_wx"] = np.ascontiguousarray(inp["lru_wx"].reshape(DEPTH, 1024, 128))
    m["w2a2"] = np.ascontiguousarray(np.concatenate([inp["rwkv_w2"], inp["rwkv_a2"]], axis=1))
    m["g2"] = np.ascontiguousarray(inp["rwkv_g2"])
    sc = np.zeros((DEPTH, ns, 128, 72), np.float32)
    sl = np.zeros((DEPTH, ns, 128, 8), np.float32)
    sC = np.zeros((DEPTH, ns, 128, 2056), np.float32)
    smm = np.zeros((DEPTH, ns, 4, 1), np.float32)
    smb = np.zeros((DEPTH, ns, 128, 4), np.float32)
    ssh = np.zeros((DEPTH, ns, 128, 26), np.float32)
    sS = np.zeros((DEPTH, ns, 128, 512), np.float32)
    for l in range(DEPTH):
        for k, g in enumerate(sample_ids):
            sc[l, k] = inp["state_conv"][l, g].reshape(3, 24, 128).transpose(2, 1, 0).reshape(128, 72)
            sl[l, k] = inp["state_lru_h"][l, g].reshape(8, 128).T
            C0 = inp["state_mlstm_C"][l, g].reshape(4, 256, 2, 128).transpose(3, 0, 2, 1)
            n0 = inp["state_mlstm_n"][l, g].reshape(4, 2, 128).transpose(2, 0, 1)[..., None]
            sC[l, k] = np.concatenate([C0, n0], axis=-1).reshape(128, 2056)
            smm[l, k, :, 0] = inp["state_mlstm_m"][l, g]
            smb[l, k] = np.broadcast_to(inp["state_mlstm_m"][l, g][None, :], (128, 4))
            ssh[l, k] = inp["state_rwkv_shift"][l, g, 0].reshape(26, 128).T
            sS[l, k] = inp["state_rwkv_S"][l, g].reshape(8, 2, 64, 64).transpose(1, 3, 0, 2).reshape(128, 512)
    m.update(s_conv=sc, s_lru=sl, s_C=sC, s_m=smm, s_mbc=smb, s_shift=ssh, s_S=sS)
    return m


def unpack_states(r, q):
    conv = np.stack([r["o_conv"][l, q].reshape(128, 24, 3).transpose(2, 1, 0).reshape(3, 3072) for l in range(DEPTH)])
    lru = np.stack([r["o_lru"][l, q].T.reshape(1024) for l in range(DEPTH)])
    Cn = [r["o_C"][l, q].reshape(128, 4, 2, 257) for l in range(DEPTH)]
    C = np.stack([c[..., :256].transpose(1, 3, 2, 0).reshape(4, 256, 256) for c in Cn])
    n = np.stack([c[..., 256].transpose(1, 2, 0).reshape(4, 256) for c in Cn])
    mm = np.stack([r["o_m"][l, q].reshape(4) for l in range(DEPTH)])
    sh = np.stack([r["o_shift"][l, q].T.reshape(1, 3328) for l in range(DEPTH)])
    S = np.stack([r["o_S"][l, q].reshape(2, 64, 8, 64).transpose(2, 0, 3, 1).reshape(16, 64, 64) for l in range(DEPTH)])
    return [np.ascontiguousarray(a, dtype=np.float32) for a in (conv, lru, C, n, mm, sh, S)]


_NC_CACHE = {}


def kernel(**inp):
    inp = {k: np.asarray(v) for k, v in inp.items()}
    if "full" not in _NC_CACHE:
        _NC_CACHE["full"] = Builder().build()
    nc = _NC_CACHE["full"]
    in_maps = [make_in_map(inp, inp["x_prompt"][c % 4], [2 * c, 2 * c + 1]) for c in range(8)]
    res = run_bass_kernel_spmd(nc, in_maps, core_ids=list(range(8)))
    R = res.results
    yp = np.stack([R[c]["yp"] for c in range(4)], axis=0)
    ys = np.concatenate([R[c]["ys"].reshape(2, 16, D) for c in range(8)], axis=0)
    pst = [unpack_states(R[c], 0) for c in range(4)]
    p_states = [np.stack([pst[c][k] for c in range(4)], axis=1) for k in range(7)]
    sst = []
    for c in range(8):
        for q in (1, 2):
            sst.append(unpack_states(R[c], q))
    s_states = [np.stack([sst[g][k] for g in range(16)], axis=1) for k in range(7)]
    return tuple([yp, ys] + p_states + s_states)
```
